# Optimizing a Trainium2 kernel written in Bass

```python
import jax, jax.numpy as jnp
from jax import lax
import numpy as np

D_MODEL = 2048
BATCH = 2
SEQ = 8192
DEPTH = 1

GLA_HEADS = 4
GLA_KEY_DIM = D_MODEL // 2
GLA_VALUE_DIM = D_MODEL
GLA_HEAD_K = GLA_KEY_DIM // GLA_HEADS
GLA_HEAD_V = GLA_VALUE_DIM // GLA_HEADS
GK_RANK = 16
GATE_LOGIT_NORMALIZER = 16.0
CHUNK = 64
CONV_WIDTH = D_MODEL
CONV_K = 3
FFN_HIDDEN = ((8 * D_MODEL + 3 * 256 - 1) // (3 * 256)) * 256
EPS = 1e-6

IN_SPLITS = [
    GLA_KEY_DIM,
    GLA_KEY_DIM,
    GLA_VALUE_DIM,
    GLA_VALUE_DIM,
    GK_RANK,
    CONV_WIDTH,
    CONV_WIDTH,
    CONV_WIDTH,
    D_MODEL,
    D_MODEL,
]
IN_COLS = int(sum(IN_SPLITS))
SPLIT_IDX = [int(i) for i in np.cumsum(IN_SPLITS)[:-1]]

kernel_name = "hybrid_gla_shortconv_gated_merge"


def rmsnorm(x, w):
    xf = x.astype(jnp.float32)
    y = xf * lax.rsqrt(jnp.mean(xf * xf, axis=-1, keepdims=True) + EPS)
    return (y * w.astype(jnp.float32)).astype(x.dtype)


def gla_chunked(q, k, v, log_g):
    b_, s, h, dk = q.shape
    dv = v.shape[-1]
    n = s // CHUNK
    f32 = jnp.float32

    def chunks(t):
        return t.astype(f32).reshape(b_, n, CHUNK, h, t.shape[-1]).transpose(1, 0, 3, 2, 4)

    qc = chunks(q) * (dk ** -0.5)
    kc = chunks(k)
    vc = chunks(v)
    gc = chunks(log_g)
    bcum = jnp.cumsum(gc, axis=3)
    b_last = bcum[..., -1:, :]
    q_dec = qc * jnp.exp(bcum)
    k_dec = kc * jnp.exp(-bcum)
    k_to_end = kc * jnp.exp(b_last - bcum)

    causal = jnp.tril(jnp.ones((CHUNK, CHUNK), dtype=bool))
    scores = jnp.einsum('nbhik,nbhjk->nbhij', q_dec, k_dec)
    scores = jnp.where(causal, scores, 0.0)
    o_intra = jnp.einsum('nbhij,nbhjv->nbhiv', scores, vc)

    def step(state, inp):
        q_d, k_e, v_n, decay = inp
        o_inter = jnp.einsum('bhik,bhkv->bhiv', q_d, state)
        state = state * decay[:, :, 0, :, None] + jnp.einsum('bhjk,bhjv->bhkv', k_e, v_n)
        return state, o_inter

    state0 = jnp.zeros((b_, h, dk, dv), f32)
    _, o_inter = lax.scan(step, state0, (q_dec, k_to_end, vc, jnp.exp(b_last)))
    o = o_intra + o_inter
    return o.transpose(1, 0, 3, 2, 4).reshape(b_, s, h, dv)


def causal_depthwise_conv(u, w):
    s = u.shape[1]
    up = jnp.pad(u, ((0, 0), (CONV_K - 1, 0), (0, 0)))
    return sum(up[:, j:j + s, :] * w[j] for j in range(CONV_K))


def hybrid_mixer(h, w_in, w_gk_up, b_gk_up, gla_norm_w, conv_w, w_out):
    b_, s, _ = h.shape
    proj = jnp.einsum('bsd,de->bse', h, w_in)
    (q, k, v, g_out, gk_low, gate_b, gate_c, xc, merge_a, merge_b) = jnp.split(proj, SPLIT_IDX, axis=-1)

    gk = jnp.einsum('bsr,rk->bsk', gk_low, w_gk_up) + b_gk_up
    log_g = jax.nn.log_sigmoid(gk.astype(jnp.float32)) / GATE_LOGIT_NORMALIZER
    o = gla_chunked(q.reshape(b_, s, GLA_HEADS, GLA_HEAD_K),
                    k.reshape(b_, s, GLA_HEADS, GLA_HEAD_K),
                    v.reshape(b_, s, GLA_HEADS, GLA_HEAD_V),
                    log_g.reshape(b_, s, GLA_HEADS, GLA_HEAD_K))
    o = rmsnorm(o, gla_norm_w).astype(h.dtype)
    y_a = (o * jax.nn.silu(g_out.reshape(b_, s, GLA_HEADS, GLA_HEAD_V))).reshape(b_, s, GLA_VALUE_DIM)

    y_b = gate_b * causal_depthwise_conv(gate_c * xc, conv_w)

    merged = jax.nn.sigmoid(merge_a) * y_a + jax.nn.sigmoid(merge_b) * y_b
    return jnp.einsum('bse,ed->bsd', merged, w_out)


def swiglu(h, w_gate_up, w_down):
    gu = jnp.einsum('bsd,df->bsf', h, w_gate_up)
    gate, up = jnp.split(gu, [FFN_HIDDEN], axis=-1)
    return jnp.einsum('bsf,fd->bsd', jax.nn.silu(gate) * up, w_down)


def setup_inputs(seed: int = 0) -> dict:
    key = jax.random.key(seed)
    ks = jax.random.split(key, 12)
    f32 = jnp.float32
    nrm = lambda k, shape, scale: jax.random.normal(k, shape, f32) * scale
    return {
        "x": nrm(ks[0], (BATCH, SEQ, D_MODEL), 1.0),
        "mix_norm_w": 1.0 + nrm(ks[1], (DEPTH, D_MODEL), 0.02),
        "w_in": nrm(ks[2], (DEPTH, D_MODEL, IN_COLS), D_MODEL ** -0.5),
        "w_gk_up": nrm(ks[3], (DEPTH, GK_RANK, GLA_KEY_DIM), GK_RANK ** -0.5),
        "b_gk_up": nrm(ks[4], (DEPTH, GLA_KEY_DIM), 0.1),
        "gla_norm_w": 1.0 + nrm(ks[5], (DEPTH, GLA_HEAD_V), 0.02),
        "conv_w": nrm(ks[6], (DEPTH, CONV_K, CONV_WIDTH), CONV_K ** -0.5),
        "w_out": nrm(ks[7], (DEPTH, D_MODEL, D_MODEL), D_MODEL ** -0.5),
        "ffn_norm_w": 1.0 + nrm(ks[8], (DEPTH, D_MODEL), 0.02),
        "w_gate_up": nrm(ks[9], (DEPTH, D_MODEL, 2 * FFN_HIDDEN), D_MODEL ** -0.5),
        "w_down": nrm(ks[10], (DEPTH, FFN_HIDDEN, D_MODEL), FFN_HIDDEN ** -0.5),
        "final_norm_w": 1.0 + nrm(ks[11], (D_MODEL,), 0.02),
    }


def reference(x, mix_norm_w, w_in, w_gk_up, b_gk_up, gla_norm_w, conv_w, w_out,
              ffn_norm_w, w_gate_up, w_down, final_norm_w):
    h = x
    for l in range(DEPTH):
        h = h + hybrid_mixer(rmsnorm(h, mix_norm_w[l]), w_in[l], w_gk_up[l], b_gk_up[l],
                             gla_norm_w[l], conv_w[l], w_out[l])
        h = h + swiglu(rmsnorm(h, ffn_norm_w[l]), w_gate_up[l], w_down[l])
    return rmsnorm(h, final_norm_w)
```

```python
import numpy as np
from contextlib import ExitStack
import concourse.bass as bass
import concourse.mybir as mybir
from concourse.bass_utils import run_bass_kernel_spmd
from concourse.alu_op_type import AluOpType as ALU

F32 = mybir.dt.float32
BF16 = mybir.dt.bfloat16
AF = mybir.ActivationFunctionType

D = 2048
NKC = 16
TT = 512
NTC = 4
FF = 5632
NFC = 44
USE_CC = True
NPRE = 4 if USE_CC else 12
XROW0 = 128
MROW0 = XROW0 if USE_CC else XROW0 + NPRE * TT
NMAIN = 4
EPS = 1e-6
C_Q, C_K, C_V, C_GO, C_GKL, C_GB, C_GC, C_XC, C_MA, C_MB = 0, 1024, 2048, 4096, 6144, 6160, 8208, 10256, 12304, 14352
ENGS = ("pe", "act", "dve", "pool", "sp")
ARENAS = ("XT", "FA")


class Op:
    __slots__ = ("id", "eng", "fn", "dma", "deps", "sem", "tick", "signal", "inc")

    def __init__(self, id, eng, fn, dma):
        self.id = id
        self.eng = eng
        self.fn = fn
        self.dma = dma
        self.deps = set()
        self.sem = None
        self.tick = None
        self.signal = False
        self.inc = 16 if dma else 1


class Prog:
    def __init__(self, nc):
        self.nc = nc
        self.ops = []
        self.by_eng = {e: [] for e in ENGS}
        self.last_writer = {}
        self.readers = {}
        self.arena_keys = {a: [] for a in ARENAS}
        self.dma_keys = {}

    def _expand(self, k):
        if isinstance(k, tuple) and k and k[0] in ARENAS:
            lst = self.arena_keys[k[0]]
            if k not in self.last_writer and k not in self.readers:
                lst.append(k)
                self.readers[k] = []
            return [o for o in lst if o[1] < k[2] and k[1] < o[2]]
        return [k]

    def op(self, eng, fn, reads=(), writes=(), dma_key=None, extra_deps=(), inc=None):
        o = Op(len(self.ops), eng, fn, dma_key is not None)
        deps = o.deps
        for r in reads:
            for k in self._expand(r):
                w = self.last_writer.get(k)
                if w is not None:
                    deps.add(w)
        for w_ in writes:
            for k in self._expand(w_):
                w = self.last_writer.get(k)
                if w is not None:
                    deps.add(w)
                rl = self.readers.get(k)
                if rl:
                    deps.update(rl)
        deps.update(extra_deps)
        deps.discard(o.id)
        for r in reads:
            self.readers.setdefault(r, []).append(o.id)
        for w_ in writes:
            for k in self._expand(w_):
                self.last_writer[k] = o.id
                self.readers[k] = []
        if dma_key is not None:
            c = self.dma_keys.get(dma_key, 0) + 1
            self.dma_keys[dma_key] = c
            if inc is not None:
                o.inc = inc
            o.sem = ("dma", dma_key)
            o.tick = o.inc * c
            o.signal = True
        self.ops.append(o)
        self.by_eng[eng].append(o)
        return o.id

    def emit(self, stack):
        nc = self.nc
        ops = self.ops
        for o in ops:
            nd = set()
            for d in o.deps:
                p = ops[d]
                if (not p.dma) and p.eng == o.eng and o.eng == "pe":
                    continue
                nd.add(d)
            o.deps = nd
            for d in nd:
                ops[d].signal = True
        cnt = {e: 0 for e in ENGS}
        for e in ENGS:
            for o in self.by_eng[e]:
                if o.dma:
                    continue
                if o.signal:
                    cnt[e] += 1
                    o.sem = ("eng", e)
                    o.tick = cnt[e]
        sems = {}
        for e in ENGS:
            if cnt[e] > 0:
                sems[("eng", e)] = stack.enter_context(nc.semaphore("prog_" + e))
        for i, k in enumerate(self.dma_keys):
            sems[("dma", k)] = stack.enter_context(nc.semaphore("dma_%d" % i))
        self.n_sems = len(sems)
        self.max_tick = max([cnt[e] for e in ENGS] + [16 * c for c in self.dma_keys.values()] + [0])
        block = stack.enter_context(nc.Block())

        def run(eng_name, eng):
            waited = {}
            for o in self.by_eng[eng_name]:
                need = {}
                for d in o.deps:
                    p = ops[d]
                    if need.get(p.sem, 0) < p.tick:
                        need[p.sem] = p.tick
                for s, v in need.items():
                    if waited.get(s, 0) >= v:
                        continue
                    eng.wait_ge(sems[s], v)
                    waited[s] = v
                inst = o.fn(eng)
                if o.signal:
                    inst.then_inc(sems[o.sem], o.inc)

        @block.tensor
        def _(e):
            run("pe", e)

        @block.scalar
        def _(e):
            run("act", e)

        @block.vector
        def _(e):
            run("dve", e)

        @block.gpsimd
        def _(e):
            run("pool", e)

        @block.sync
        def _(e):
            run("sp", e)


class V:
    __slots__ = ("ap", "keys")

    def __init__(self, ap, keys):
        self.ap = ap
        self.keys = tuple(keys)

    def __getitem__(self, idx):
        return V(self.ap[idx], self.keys)


def _k(*vs):
    out = []
    for v in vs:
        if isinstance(v, V):
            out.extend(v.keys)
    return out


def _a(v):
    return v.ap if isinstance(v, V) else v


def build_nc():
    nc = bass.Bass("TRN2", target_bir_lowering=False)
    xs = nc.dram_tensor("xs", [MROW0 + NMAIN * TT, D], F32, kind="ExternalInput").ap()
    am_d = nc.dram_tensor("am", [128, 4], F32, kind="ExternalInput").ap()
    lsrc = [nc.dram_tensor("lsrc%d" % i, [128, 2048], F32) for i in range(2)] + [nc.dram_tensor("lsrc2", [128, 32], F32)]
    lall = [nc.dram_tensor("lall%d" % i, [512, 2048], F32) for i in range(2)] + [nc.dram_tensor("lall2", [512, 32], F32)]
    w_in = nc.dram_tensor("w_in", [D, 16400], F32, kind="ExternalInput").ap()
    w_out = nc.dram_tensor("w_out", [D, D], F32, kind="ExternalInput").ap()
    w_gu = nc.dram_tensor("w_gu", [D, 2 * FF], F32, kind="ExternalInput").ap()
    w_dn = nc.dram_tensor("w_dn", [FF, D], F32, kind="ExternalInput").ap()
    wgk_d = nc.dram_tensor("wgk", [16, 1024], F32, kind="ExternalInput").ap()
    vecs_d = nc.dram_tensor("vecs", [128, 92], F32, kind="ExternalInput").ap()
    fnwb_d = nc.dram_tensor("fnwb", [128, D], F32, kind="ExternalInput").ap()
    cst_d = nc.dram_tensor("cst", [128, 384], F32, kind="ExternalInput").ap()
    out_d = nc.dram_tensor("out", [NMAIN * TT, D], F32, kind="ExternalOutput").ap()

    with ExitStack() as st:
        def sb(name, shape, dt):
            return st.enter_context(nc.sbuf_tensor("s_" + name, shape, dt))

        xnT_t = sb("xnT", [128, NKC, TT], BF16)
        mgT_t = sb("mgT", [128, NKC, TT], BF16)
        wr_t = [sb("wr%d" % i, [128, NKC, 512], BF16) for i in range(3)]
        S_t = sb("S", [128, 8, 512], F32)
        Sbf_t = sb("Sbf", [128, 8, 512], BF16)
        cst_t = sb("cst", [128, 384], F32)
        identb_t = sb("identb", [128, 128], BF16)
        ones_t = sb("ones512", [128, 512], F32)
        vecs_t = sb("vecs", [128, 92], F32)
        negb_t = sb("negb", [128, 8], F32)
        fnwb_t = sb("fnwb", [128, D], F32)
        wgk_t = sb("wgkb", [16, 1024], BF16)
        gkl_t = sb("gkl", [16, TT], BF16)
        wgl_t = sb("wgl", [128, NKC, 16], BF16)
        uh_t = sb("uh", [128, 16, 2], F32)
        xnh_t = sb("xnh", [128, NKC, 2], BF16)
        gch_t = sb("gch", [128, 4, 2], F32)
        ss_t = sb("ss", [128, 4], F32)
        rs_t = sb("rs", [128, 4], F32)
        nbl_t = sb("nbl", [128, 4], F32)
        dl_t = sb("dl", [128, 8, 4], F32)
        am_t = sb("am", [128, 4], F32)
        LD_t = sb("LD", [128, 32], F32)
        LDall_t = sb("LDall", [128, 4, 32], F32)
        Deff_t = sb("Deff", [128, 8], F32)
        one8_t = sb("one8", [128, 8], F32)
        xnp_t = [sb("xnp%d" % i, [128, D], BF16) for i in range(2)]
        ssp_t = sb("ssp", [128, 2], F32)
        rsp_t = sb("rsp", [128, 2], F32)
        XT_t = sb("XT", [128, 8192], F32)
        FA_t = sb("FA", [128, 11264], F32)
        banks = [st.enter_context(nc.psum_tensor("ps%d" % i, [128, 512], F32)) for i in range(8)]

        def arena(name, t, lo, nbytes, dt, pat=None, **kw):
            assert lo % 4 == 0 and nbytes % 4 == 0
            ap = t[:, lo // 4:(lo + nbytes) // 4]
            if dt == BF16:
                ap = ap.bitcast(BF16)
            if pat is not None:
                ap = ap.rearrange(pat, **kw)
            return V(ap, [(name, lo, lo + nbytes)])

        xt = arena("XT", XT_t, 0, 32768, F32, "p (a b) -> p a b", a=4)
        xt_tc = [arena("XT", XT_t, tc * 8192, 8192, F32) for tc in range(4)]
        spb = arena("XT", XT_t, 0, 2048, F32)
        E1 = arena("XT", XT_t, 2048, 2048, F32)
        E2 = arena("XT", XT_t, 4096, 2048, F32)
        E3 = arena("XT", XT_t, 6144, 2048, F32)
        qd = [arena("XT", XT_t, 8192 + i * 2048, 2048, BF16, "p (a b) -> p a b", a=2) for i in range(2)]
        kd = [arena("XT", XT_t, 12288 + i * 2048, 2048, BF16, "p (a b) -> p a b", a=2) for i in range(2)]
        kteT = arena("XT", XT_t, 16384, 2048, BF16, "p (a b) -> p a b", a=2)
        kte = [arena("XT", XT_t, 18432 + i * 2048, 2048, BF16, "p (a b) -> p a b", a=4) for i in range(2)]
        vb = [arena("XT", XT_t, 22528 + i * 4096, 4096, BF16, "p (a b) -> p a b", a=4) for i in range(2)]
        actT = [arena("FA", FA_t, fc * 1024, 1024, BF16) for fc in range(NFC)]
        xn_tm = [arena("FA", FA_t, tc * 4096, 4096, BF16) for tc in range(4)]
        junk = arena("FA", FA_t, 16384, 4096, BF16)
        scT = [arena("FA", FA_t, i * 256, 256, BF16) for i in range(2)]
        o_sb = arena("FA", FA_t, 512, 8192, F32, "p (a b) -> p a b", a=4)
        o_cc = [arena("FA", FA_t, 512 + cc * 2048, 2048, F32) for cc in range(4)]
        sq = [arena("FA", FA_t, 8704 + i * 2048, 2048, F32) for i in range(2)]
        lnv = arena("FA", FA_t, 12800, 2048, F32)
        gc = [arena("FA", FA_t, 14848 + cc * 2048, 2048, F32) for cc in range(4)]
        cv = [arena("FA", FA_t, 23040 + cc * 2048, 2048, F32) for cc in range(4)]
        sg = arena("FA", FA_t, 31232, 2048, F32)
        sb2 = arena("FA", FA_t, 33280, 2048, F32)
        tmp = arena("FA", FA_t, 35328, 2048, F32)
        ubuf = [arena("FA", FA_t, 37376 + i * 2064, 2064, F32) for i in range(2)]

        ft = [V(mgT_t[:, 2 * i:2 * i + 2, :].rearrange("p a b -> p (a b)").bitcast(F32),
                [("mgT", 2 * i), ("mgT", 2 * i + 1)]) for i in range(8)]
        xpc = [V(mgT_t[:, 8 * i:8 * i + 8, :].rearrange("p a b -> p (a b)").bitcast(F32),
                 [("mgT", g) for g in range(8 * i, 8 * i + 8)]) for i in range(2)]
        xnp = [V(xnp_t[i][:], [("xnp", i)]) for i in range(2)]
        ssp = [V(ssp_t[:, i:i + 1], [("ssp", i)]) for i in range(2)]
        rsp = [V(rsp_t[:, i:i + 1], [("rsp", i)]) for i in range(2)]
        xnTalt = [arena("FA", FA_t, kc * 1024, 1024, BF16) for kc in range(NKC)]
        xnTalt_halo = V(FA_t[:, 0:4096].bitcast(BF16).rearrange("p (a b) -> p a b", a=NKC)[:, :, TT - 2:TT],
                        [("FA", 0, 16384)])
        am = V(am_t[:], ["am"])
        LD = V(LD_t[:], ["LD"])
        LDall = V(LDall_t[:], ["LDall"])
        Deff = V(Deff_t[:], ["Deff"])
        one8 = V(one8_t[:], ["one8"])
        Lbuf = [V(mgT_t[:, 8 * i:8 * i + 8, :].rearrange("p a b -> p (a b)").bitcast(F32).rearrange("p (a b) -> p a b", a=4),
                  [("mgT", g) for g in range(8 * i, 8 * i + 8)]) for i in range(2)]
        xnTalt_halo0 = V(FA_t[:, 0:4096].bitcast(BF16).rearrange("p (a b) -> p a b", a=NKC)[:, :, 126:128],
                         [("FA", 0, 16384)])
        xnT = [V(xnT_t[:, kc, :], [("xnT", kc)]) for kc in range(NKC)]
        mgT = [V(mgT_t[:, g, :], [("mgT", g)]) for g in range(NKC)]
        Sv = [V(S_t[:, i, :], [("S", i)]) for i in range(8)]
        Sbf = [V(Sbf_t[:, i, :], [("Sbf", i)]) for i in range(8)]
        ident_f = V(cst_t[:, 0:128], ["cst"])
        maskT = V(cst_t[:, 128:256], ["cst"])
        ones128 = V(cst_t[:, 256:384], ["cst"])
        identb = V(identb_t[:], ["identb"])
        ones512 = V(ones_t[:], ["ones512"])
        vecs = V(vecs_t[:], ["vecs"])
        negb = V(negb_t[:], ["negb"])
        fnwb = V(fnwb_t[:], ["fnwb"])
        wgk = V(wgk_t[:], ["wgk"])
        gkl = V(gkl_t[:], ["gkl"])
        uh = [V(uh_t[:, g, :], [("uh", g)]) for g in range(16)]
        xnh = V(xnh_t[:], ["xnh"])
        gch = [V(gch_t[:, cc, :], [("gch", cc)]) for cc in range(4)]
        ss = V(ss_t[:], ["ss"])
        rs = V(rs_t[:], ["rs"])
        nbl = V(nbl_t[:], ["nbl"])
        dl = [V(dl_t[:, i, :], [("dl", i)]) for i in range(8)]
        bankv = [V(banks[i][:], [("ps", i)]) for i in range(8)]
        bankb = [V(banks[i][:].bitcast(BF16), [("ps", i)]) for i in range(8)]

        def vcol(c):
            return vecs[:, c:c + 1]

        def emit_all(P, plan):
            state = {"bank": 0, "wi": 0, "issued": 0}
            rec = []

            def ACT(out, in_, func, bias=None, scale=None):
                kw = {}
                if bias is not None:
                    kw["bias"] = _a(bias)
                if scale is not None:
                    kw["scale"] = scale
                P.op("act", lambda e: e.activation(out=out.ap, in_=in_.ap, func=func, **kw),
                     reads=_k(in_, bias), writes=_k(out))

            def ACOPY(out, in_):
                P.op("act", lambda e: e.copy(out=out.ap, in_=in_.ap), reads=_k(in_), writes=_k(out))

            def TT_(out, in0, in1, op):
                P.op("dve", lambda e: e.tensor_tensor(out=out.ap, in0=in0.ap, in1=in1.ap, op=op),
                     reads=_k(in0, in1), writes=_k(out))

            def TS(out, in0, s1, op0):
                P.op("dve", lambda e: e.tensor_scalar(out=out.ap, in0=in0.ap, scalar1=_a(s1), scalar2=None, op0=op0),
                     reads=_k(in0, s1), writes=_k(out))

            def STT(out, in0, s, in1, op0, op1):
                P.op("dve", lambda e: e.scalar_tensor_tensor(out=out.ap, in0=in0.ap, scalar=_a(s), in1=in1.ap,
                                                             op0=op0, op1=op1),
                     reads=_k(in0, s, in1), writes=_k(out))

            def DCOPY(out, in_):
                P.op("dve", lambda e: e.tensor_copy(out=out.ap, in_=in_.ap), reads=_k(in_), writes=_k(out))

            def MM(out, lhsT, rhs, start, stop):
                P.op("pe", lambda e: e.matmul(out.ap, lhsT=lhsT.ap, rhs=rhs.ap, start=start, stop=stop),
                     reads=_k(lhsT, rhs), writes=_k(out))

            def TR(out, in_):
                P.op("pe", lambda e: e.transpose(out=out.ap, in_=in_.ap, identity=identb.ap),
                     reads=_k(in_, identb), writes=_k(out))

            def nbank():
                b = state["bank"]
                state["bank"] = (b + 1) % 8
                return b

            def issue(desc, idx):
                src, k0, nk, segs = desc
                slot = idx % 3
                dram = {"in": w_in, "out": w_out, "gu": w_gu, "dn": w_dn}[src]
                off = 0
                for (c0, n) in segs:
                    view = dram[k0 * 128:(k0 + nk) * 128, c0:c0 + n].rearrange("(kc p) c -> p kc c", p=128)
                    dst = wr_t[slot][:, 0:nk, off:off + n]
                    P.op("pool", lambda e, dst=dst, view=view: e.dma_start(out=dst, in_=view),
                         writes=[("w", slot)], dma_key=("w", slot))
                    off += n

            def wblock(src, k0, nk, c0, n, c1=None, n1=0):
                segs = ((c0, n),) if c1 is None else ((c0, n), (c1, n1))
                n = n + n1
                desc = (src, k0, nk, segs)
                i = state["wi"]
                state["wi"] = i + 1
                if plan is None:
                    rec.append(desc)
                else:
                    assert plan[i] == desc, (i, plan[i], desc)
                    while state["issued"] < min(len(plan), i + 3):
                        issue(plan[state["issued"]], state["issued"])
                        state["issued"] += 1
                slot = i % 3
                return V(wr_t[slot][:, 0:nk, 0:n], [("w", slot)])

            def group_fm(W, cc, M=128, N=TT, extra_rhs=None, X=None):
                X = xnT if X is None else X
                b = nbank()
                out = bankv[b][0:M, 0:N]
                for kc in range(NKC):
                    MM(out, W[:, kc, cc * 128:cc * 128 + M], X[kc][:, 0:N], kc == 0, kc == NKC - 1)
                if extra_rhs is not None:
                    b2 = nbank()
                    out2 = bankv[b2][0:M, 0:2]
                    for kc in range(NKC):
                        MM(out2, W[:, kc, cc * 128:cc * 128 + M], extra_rhs[:, kc, :], kc == 0, kc == NKC - 1)
                    return out, out2
                return out

            def group_tm(W, src, tc, ncol=512):
                b = nbank()
                out = bankv[b][:, 0:ncol]
                n = len(src)
                for kc in range(n):
                    MM(out, src[kc][:, tc * 128:(tc + 1) * 128], W[:, kc, 0:ncol], kc == 0, kc == n - 1)
                return out

            if True:
                P.op("sp", lambda e: e.dma_start(out=cst_t[:], in_=cst_d), writes=["cst"], dma_key="c_cst")
                P.op("sp", lambda e: e.dma_start(out=vecs_t[:], in_=vecs_d), writes=["vecs"], dma_key="c_vecs")
                P.op("sp", lambda e: e.dma_start(out=fnwb_t[:], in_=fnwb_d), writes=["fnwb"], dma_key="c_fnwb")
                P.op("pool", lambda e: e.dma_start(out=wgk_t[:], in_=wgk_d), writes=["wgk"], dma_key="c_wgk")
                P.op("pool", lambda e: e.dma_start(out=wgl_t[:],
                                                   in_=w_in[:, C_GKL:C_GKL + 16].rearrange("(kc p) c -> p kc c", p=128)),
                     writes=["wgl"], dma_key="c_wgl")
                DCOPY(identb, ident_f)
                P.op("dve", lambda e: e.memset(ones_t[:], 1.0), writes=["ones512"])
                for i in range(8):
                    P.op("dve", lambda e, i=i: e.memset(S_t[:, i, :], 0.0), writes=[("S", i)])
                TS(negb, vecs[:, 84:92], -1.0, ALU.mult)
                P.op("sp", lambda e: e.dma_start(out=am_t[:], in_=am_d), writes=["am"], dma_key="c_am")
                P.op("dve", lambda e: e.memset(LD_t[:], 0.0), writes=["LD"])
                P.op("dve", lambda e: e.memset(one8_t[:], 1.0), writes=["one8"])

            def load_x(row0):
                P.op("sp", lambda e: e.dma_start(out=xt.ap, in_=xs[row0:row0 + TT, :].rearrange("(tc p) d -> p tc d", p=128)),
                     writes=_k(xt), dma_key="x")

            def norm_to_T(wcol0, dst):
                for tc in range(4):
                    P.op("dve", lambda e, tc=tc: e.scalar_tensor_tensor(
                        out=junk.ap, in0=xt_tc[tc].ap, scalar=1.0, in1=xt_tc[tc].ap,
                        op0=ALU.mult, op1=ALU.mult, accum_out=ss_t[:, tc:tc + 1]),
                        reads=_k(xt_tc[tc]), writes=_k(junk, ss))
                ACT(rs, ss, AF.Ln, bias=EPS, scale=1.0 / D)
                ACT(rs, rs, AF.Exp, scale=-0.5)
                for tc in range(4):
                    TS(xn_tm[tc], xt_tc[tc], rs[:, tc:tc + 1], ALU.mult)
                for kc in range(NKC):
                    b = nbank()
                    for tc in range(4):
                        TR(bankb[b][:, tc * 128:(tc + 1) * 128], xn_tm[tc][:, kc * 128:(kc + 1) * 128])
                    TS(dst[kc], bankb[b][:, 0:TT], vcol(wcol0 + kc), ALU.mult)

            def s0_norm(src, j):
                P.op("dve", lambda e: e.scalar_tensor_tensor(
                    out=xnp[j].ap, in0=src.ap, scalar=1.0, in1=src.ap,
                    op0=ALU.mult, op1=ALU.mult, accum_out=ssp[j].ap),
                    reads=_k(src), writes=_k(xnp[j], ssp[j]))
                ACT(rsp[j], ssp[j], AF.Ln, bias=EPS, scale=1.0 / D)
                ACT(rsp[j], rsp[j], AF.Exp, scale=-0.5)
                TS(xnp[j], src, rsp[j], ALU.mult)

            def s0_load(row0, tc, j):
                r0 = row0 + tc * 128
                P.op("sp", lambda e: e.dma_start(out=xpc[j].ap, in_=xs[r0:r0 + 128, :]),
                     writes=_k(xpc[j]), dma_key=("xp", j))
                s0_norm(xpc[j], j)

            def s0_tr(tc, j, dst, wcol0):
                for kq in range(4):
                    b = nbank()
                    for i in range(4):
                        kc = kq * 4 + i
                        TR(bankb[b][:, i * 128:(i + 1) * 128], xnp[j][:, kc * 128:(kc + 1) * 128])
                    for i in range(4):
                        kc = kq * 4 + i
                        TS(dst[kc][:, tc * 128:(tc + 1) * 128], bankb[b][:, i * 128:(i + 1) * 128],
                           vcol(wcol0 + kc), ALU.mult)

            def final_stats():
                for tc in range(4):
                    P.op("dve", lambda e, tc=tc: e.scalar_tensor_tensor(
                        out=junk.ap, in0=xt_tc[tc].ap, scalar=1.0, in1=xt_tc[tc].ap,
                        op0=ALU.mult, op1=ALU.mult, accum_out=ss_t[:, tc:tc + 1]),
                        reads=_k(xt_tc[tc]), writes=_k(junk, ss))
                ACT(rs, ss, AF.Ln, bias=EPS, scale=1.0 / D)
                ACT(rs, rs, AF.Exp, scale=-0.5)

            def gk_low(X=None):
                W = V(wgl_t[:], ["wgl"])
                ps = group_fm(W, 0, M=16, X=X)
                ACOPY(gkl, ps)

            def decay_common(hd):
                b = nbank()
                ps = bankv[b]
                MM(ps, wgk[:, hd * 128:(hd + 1) * 128], gkl, True, True)
                ACT(spb, ps, AF.Exp, bias=negb[:, hd:hd + 1], scale=-1.0)
                ACT(spb, spb, AF.Ln, bias=1.0, scale=1.0)

            def scan(lo, hi):
                P.op("dve", lambda e: e.tensor_tensor_scan(out=E2.ap[:, lo:hi], data0=ones512.ap[:, lo:hi],
                                                           data1=spb.ap[:, lo:hi], initial=0.0,
                                                           op0=ALU.mult, op1=ALU.add),
                     reads=_k(spb, ones512), writes=_k(E2))

            def kt_and_v(h, pp, Wv, X=None):
                X = xnT if X is None else X
                for tc in range(4):
                    ps = group_tm(Wv, X, tc)
                    ACOPY(vb[pp][:, tc, :], ps)
                b = nbank()
                for tc in range(4):
                    for dkc in range(2):
                        TR(bankb[b][:, (tc * 2 + dkc) * 128:(tc * 2 + dkc + 1) * 128],
                           kteT[:, dkc, tc * 128:(tc + 1) * 128])
                P.op("act", lambda e: e.copy(out=kte[pp].ap, in_=bankb[b].ap.rearrange("p (a b) -> p a b", a=4)),
                     reads=_k(bankb[b]), writes=_k(kte[pp]))

            def prefix_tile(pt, last):
                X = xnT if pt % 2 == 0 else xnTalt
                Xn = xnT if (pt + 1) % 2 == 0 or last else xnTalt
                nrow0 = MROW0 if last else XROW0 + (pt + 1) * TT
                gk_low(X)
                for h in range(4):
                    s0_load(nrow0, h, h % 2)
                    if h >= 1:
                        s0_tr(h - 1, (h - 1) % 2, Xn, 0)
                    Wk = wblock("in", 0, NKC, C_K + h * 256, 256)
                    for dkc in range(2):
                        hd = h * 2 + dkc
                        decay_common(hd)
                        scan(0, TT)
                        TS(nbl[:, 0:1], E2[:, TT - 1:TT], -1.0 / 16, ALU.mult)
                        TT_(LD[:, hd:hd + 1], LD[:, hd:hd + 1], nbl[:, 0:1], ALU.add)
                        ACT(E3, E2, AF.Exp, bias=nbl[:, 0:1], scale=1.0 / 16)
                        ACT(dl[hd][:, 0:1], nbl[:, 0:1], AF.Exp)
                        ps = group_fm(Wk, dkc, X=X)
                        TT_(kteT[:, dkc, :], ps, E3, ALU.mult)
                    Wv = wblock("in", 0, NKC, C_V + h * 512, 512)
                    kt_and_v(h, 0, Wv, X)
                    for dkc in range(2):
                        hd = h * 2 + dkc
                        b = nbank()
                        for tc in range(4):
                            MM(bankv[b], kte[0][:, tc, dkc * 128:(dkc + 1) * 128], vb[0][:, tc, :], tc == 0, tc == 3)
                        STT(Sv[hd], Sv[hd], dl[hd][:, 0:1], bankv[b], ALU.mult, ALU.add)
                s0_tr(3, 1, Xn, 0)

            def qkv(h, pp):
                Wqk = wblock("in", 0, NKC, C_Q + h * 256, 256, C_K + h * 256, 256)
                for dkc in range(2):
                    hd = h * 2 + dkc
                    decay_common(hd)
                    for tc in range(4):
                        scan(tc * 128, (tc + 1) * 128)
                    ACT(E1, E2, AF.Exp, scale=-1.0 / 16)
                    TS(nbl, V(E2.ap.rearrange("p (a b) -> p a b", a=4)[:, :, 127], E2.keys), -1.0 / 16, ALU.mult)
                    for tc in range(4):
                        ACT(E3[:, tc * 128:(tc + 1) * 128], E2[:, tc * 128:(tc + 1) * 128], AF.Exp,
                            bias=nbl[:, tc:tc + 1], scale=1.0 / 16)
                    ACT(E2, E2, AF.Exp, scale=1.0 / 16)
                    ACT(dl[hd], nbl, AF.Exp)
                    ps = group_fm(Wqk, dkc)
                    STT(qd[pp][:, dkc, :], ps, 1.0 / 16, E1, ALU.mult, ALU.mult)
                    ps = group_fm(Wqk, 2 + dkc)
                    TT_(kd[pp][:, dkc, :], ps, E2, ALU.mult)
                    TT_(kteT[:, dkc, :], ps, E3, ALU.mult)
                Wv = wblock("in", 0, NKC, C_V + h * 512, 512)
                kt_and_v(h, pp, Wv)

            def gla(h, pp):
                for tc in range(4):
                    tsl = slice(tc * 128, (tc + 1) * 128)
                    b = nbank()
                    sc = bankv[b][:, 0:128]
                    for dkc in range(2):
                        MM(sc, kd[pp][:, dkc, tsl], qd[pp][:, dkc, tsl], dkc == 0, dkc == 1)
                    s_ = scT[tc % 2]
                    TT_(s_, sc, maskT, ALU.mult)
                    b = nbank()
                    for dvc in range(4):
                        dsl = slice(dvc * 128, (dvc + 1) * 128)
                        o = bankv[b][:, dsl]
                        MM(o, vb[pp][:, tc, dsl], s_, True, False)
                        MM(o, Sbf[h * 2][:, dsl], qd[pp][:, 0, tsl], False, False)
                        MM(o, Sbf[h * 2 + 1][:, dsl], qd[pp][:, 1, tsl], False, True)
                    P.op("act", lambda e, b=b, tsl=tsl: e.copy(out=o_sb.ap[:, :, tsl],
                                                               in_=bankv[b].ap.rearrange("p (a b) -> p a b", a=4)),
                         reads=_k(bankv[b]), writes=_k(o_sb))
                    for dkc in range(2):
                        hd = h * 2 + dkc
                        b = nbank()
                        MM(bankv[b], kte[pp][:, tc, dkc * 128:(dkc + 1) * 128], vb[pp][:, tc, :], True, True)
                        STT(Sv[hd], Sv[hd], dl[hd][:, tc:tc + 1], bankv[b], ALU.mult, ALU.add)
                        ACOPY(Sbf[hd], Sv[hd])
                b = nbank()
                for dvc in range(4):
                    ACT(sq[dvc % 2], o_cc[dvc], AF.Square)
                    MM(bankv[b], ones128, sq[dvc % 2], dvc == 0, dvc == 3)
                ACT(lnv, bankv[b], AF.Ln, bias=EPS, scale=1.0 / 512)
                ACT(lnv, lnv, AF.Exp, scale=-0.5)
                for dvc in range(4):
                    STT(o_cc[dvc], o_cc[dvc], vcol(80 + dvc), lnv, ALU.mult, ALU.mult)

            def bmix(h, first_tile):
                Wgc = wblock("in", 0, NKC, C_GC + h * 512, 512)
                for cc in range(4):
                    if first_tile:
                        ps, psh = group_fm(Wgc, cc, extra_rhs=xnh)
                        ACOPY(gch[cc], psh)
                    else:
                        ps = group_fm(Wgc, cc)
                    ACOPY(gc[cc], ps)
                Wxc = wblock("in", 0, NKC, C_XC + h * 512, 512)
                for cc in range(4):
                    g = h * 4 + cc
                    u = ubuf[cc % 2]
                    if first_tile:
                        ps, psh = group_fm(Wxc, cc, extra_rhs=xnh)
                        TT_(uh[g], psh, gch[cc], ALU.mult)
                    else:
                        ps = group_fm(Wxc, cc)
                    DCOPY(u[:, 0:2], uh[g])
                    TT_(u[:, 2:514], ps, gc[cc], ALU.mult)
                    TS(cv[cc], u[:, 2:514], vcol(32 + g * 3 + 2), ALU.mult)
                    STT(cv[cc], u[:, 1:513], vcol(32 + g * 3 + 1), cv[cc], ALU.mult, ALU.add)
                    STT(cv[cc], u[:, 0:512], vcol(32 + g * 3 + 0), cv[cc], ALU.mult, ALU.add)
                    DCOPY(uh[g], u[:, 512:514])
                Wgb = wblock("in", 0, NKC, C_GB + h * 512, 512)
                for cc in range(4):
                    ps = group_fm(Wgb, cc)
                    TT_(cv[cc], ps, cv[cc], ALU.mult)

            def bmix_gate(h):
                Wgo = wblock("in", 0, NKC, C_GO + h * 512, 512)
                for cc in range(4):
                    ps = group_fm(Wgo, cc)
                    ACT(sg, ps, AF.Sigmoid)
                    TT_(tmp, ps, sg, ALU.mult)
                    TT_(o_cc[cc], tmp, o_cc[cc], ALU.mult)
                Wma = wblock("in", 0, NKC, C_MA + h * 512, 512)
                for cc in range(4):
                    ps = group_fm(Wma, cc)
                    ACT(sg, ps, AF.Sigmoid)
                    TT_(o_cc[cc], o_cc[cc], sg, ALU.mult)
                Wmb = wblock("in", 0, NKC, C_MB + h * 512, 512)
                for cc in range(4):
                    g = h * 4 + cc
                    ps = group_fm(Wmb, cc)
                    ACT(sb2, ps, AF.Sigmoid)
                    TT_(cv[cc], cv[cc], sb2, ALU.mult)
                    TT_(mgT[g], cv[cc], o_cc[cc], ALU.add)

            def exchange_start():
                P.op("sp", lambda e: e.dma_start(out=lsrc[0].ap(), in_=S_t[:, 0:4, :].rearrange("p a b -> p (a b)")),
                     reads=[("S", i) for i in range(4)], writes=["lsrc0"], dma_key="ls0")
                P.op("sp", lambda e: e.dma_start(out=lsrc[1].ap(), in_=S_t[:, 4:8, :].rearrange("p a b -> p (a b)")),
                     reads=[("S", i) for i in range(4, 8)], writes=["lsrc1"], dma_key="ls1")
                P.op("sp", lambda e: e.dma_start(out=lsrc[2].ap(), in_=LD_t[:]), reads=["LD"], writes=["lsrc2"],
                     dma_key="ls2")
                prior = [o.id for o in P.by_eng["pool"] if o.dma][-3:]
                ccs = []
                for i in range(3):
                    ccs.append(P.op("pool", lambda e, i=i: e.collective_compute(
                        "AllGather", ALU.bypass, replica_groups=[[0, 1, 2, 3], [4, 5, 6, 7]],
                        ins=[lsrc[i].ap()], outs=[lall[i].ap()]),
                        reads=["lsrc%d" % i], writes=["lall%d" % i], dma_key="cc%d" % i, inc=1,
                        extra_deps=prior))
                P.op("pool", lambda e: None, extra_deps=ccs)

            def combine():
                for i in range(8):
                    P.op("dve", lambda e, i=i: e.memset(S_t[:, i, :], 0.0), writes=[("S", i)])
                P.op("sp", lambda e: e.dma_start(out=LDall_t[:], in_=lall[2].ap().rearrange("(r p) c -> p r c", p=128)),
                     reads=["lall2"], writes=["LDall"], dma_key="ld_all")
                ACT(LDall, LDall, AF.Exp)
                for j in range(3):
                    for half in range(2):
                        Lb = Lbuf[half]
                        P.op("sp", lambda e, j=j, Lb=Lb, half=half: e.dma_start(
                            out=Lb.ap, in_=lall[half].ap()[j * 128:(j + 1) * 128, :].rearrange("p (a b) -> p a b", a=4)),
                            reads=["lall%d" % half], writes=_k(Lb), dma_key=("lb", half))
                    TS(Deff, LDall[:, j, 0:8], -1.0, ALU.add)
                    STT(Deff, Deff, am[:, j:j + 1], one8, ALU.mult, ALU.add)
                    for hd in range(8):
                        TS(Sv[hd], Sv[hd], Deff[:, hd:hd + 1], ALU.mult)
                        STT(Sv[hd], Lbuf[hd // 4][:, hd % 4, :], am[:, j:j + 1], Sv[hd], ALU.mult, ALU.add)
                for i in range(8):
                    ACOPY(Sbf[i], Sv[i])

            def main_tile(mt):
                row0 = MROW0 + mt * TT
                gk_low()
                qkv(0, 0)
                for h in range(4):
                    if h + 1 < 4:
                        qkv(h + 1, (h + 1) % 2)
                    bmix(h, mt == 0)
                    if USE_CC and mt == 0 and h == 0:
                        combine()
                    gla(h, h % 2)
                    if h == 3:
                        load_x(row0)
                    bmix_gate(h)
                for db in range(4):
                    Wo = wblock("out", 0, NKC, db * 512, 512)
                    for tc in range(4):
                        ps = group_tm(Wo, mgT, tc)
                        dsl = slice(db * 512, (db + 1) * 512)
                        TT_(xt_tc[tc][:, dsl], ps, xt_tc[tc][:, dsl], ALU.add)
                        if db == 3:
                            s0_norm(xt_tc[tc], tc % 2)
                            if tc >= 1:
                                s0_tr(tc - 1, (tc - 1) % 2, xnT, 16)
                s0_tr(3, 1, xnT, 16)
                for fb in range(11):
                    Wg = wblock("gu", 0, NKC, fb * 512, 512)
                    for cc in range(4):
                        ps = group_fm(Wg, cc)
                        ACT(ft[4 + cc % 2], ps, AF.Sigmoid)
                        TT_(ft[cc], ps, ft[4 + cc % 2], ALU.mult)
                    Wu = wblock("gu", 0, NKC, FF + fb * 512, 512)
                    for cc in range(4):
                        ps = group_fm(Wu, cc)
                        TT_(actT[fb * 4 + cc], ps, ft[cc], ALU.mult)
                for db in range(4):
                    bs = [nbank() for _ in range(4)]
                    subs = [(0, 16), (16, 16), (32, 12)]
                    for (k0, nk) in subs:
                        Wd = wblock("dn", k0, nk, db * 512, 512)
                        for tc in range(4):
                            for j in range(nk):
                                fc = k0 + j
                                MM(bankv[bs[tc]], actT[fc][:, tc * 128:(tc + 1) * 128], Wd[:, j, :],
                                   fc == 0, fc == NFC - 1)
                    for tc in range(4):
                        dsl = slice(db * 512, (db + 1) * 512)
                        TT_(xt_tc[tc][:, dsl], bankv[bs[tc]], xt_tc[tc][:, dsl], ALU.add)
                        if db == 3 and mt == NMAIN - 1:
                            j = tc % 2
                            P.op("dve", lambda e, tc=tc, j=j: e.scalar_tensor_tensor(
                                out=xnp[j].ap, in0=xt_tc[tc].ap, scalar=1.0, in1=xt_tc[tc].ap,
                                op0=ALU.mult, op1=ALU.mult, accum_out=ssp[j].ap),
                                reads=_k(xt_tc[tc]), writes=_k(xnp[j], ssp[j]))
                            ACT(rsp[j], ssp[j], AF.Ln, bias=EPS, scale=1.0 / D)
                            ACT(rsp[j], rsp[j], AF.Exp, scale=-0.5)
                            STT(xt_tc[tc], xt_tc[tc], rsp[j], fnwb, ALU.mult, ALU.mult)
                            r0 = mt * TT + tc * 128
                            P.op("sp", lambda e, tc=tc, r0=r0: e.dma_start(out=out_d[r0:r0 + 128, :], in_=xt_tc[tc].ap),
                                 reads=_k(xt_tc[tc]), dma_key=("o", mt))
                    if mt + 1 < NMAIN:
                        s0_load(row0 + TT, db, db % 2)
                        if db >= 1:
                            s0_tr(db - 1, (db - 1) % 2, xnT, 0)
                if mt + 1 < NMAIN:
                    s0_tr(3, 1, xnT, 0)
                if mt == NMAIN - 1:
                    return
                final_stats()
                for tc in range(4):
                    STT(xt_tc[tc], xt_tc[tc], rs[:, tc:tc + 1], fnwb, ALU.mult, ALU.mult)
                P.op("sp", lambda e: e.dma_start(out=out_d[mt * TT:(mt + 1) * TT, :].rearrange("(tc p) d -> p tc d", p=128),
                                                 in_=xt.ap),
                     reads=_k(xt), dma_key=("o", mt))

            s0_load(0, 0, 0)
            s0_tr(0, 0, xnTalt, 0)
            DCOPY(xnh, xnTalt_halo0)
            for tc in range(4):
                s0_load(XROW0, tc, (tc + 1) % 2)
                s0_tr(tc, (tc + 1) % 2, xnT, 0)
            for pt in range(NPRE):
                prefix_tile(pt, pt == NPRE - 1)
            if USE_CC:
                exchange_start()
            else:
                for i in range(8):
                    ACOPY(Sbf[i], Sv[i])
            for mt in range(NMAIN):
                main_tile(mt)
            P.op("sp", lambda e: None, extra_deps=[o.id for o in P.ops if o.dma and o.eng == "sp"])
            return rec

        plan = emit_all(Prog(nc), None)
        P = Prog(nc)
        emit_all(P, plan)
        P.emit(st)
        print("ops", len(P.ops), "sems", P.n_sems, "max_tick", P.max_tick, "wblocks", len(plan))
    return nc


def _consts():
    c = np.zeros((128, 384), np.float32)
    c[:, 0:128] = np.eye(128, dtype=np.float32)
    j = np.arange(128)[:, None]
    i = np.arange(128)[None, :]
    c[:, 128:256] = (j <= i).astype(np.float32)
    c[:, 256:384] = 1.0
    return c


def kernel(x, mix_norm_w, w_in, w_gk_up, b_gk_up, gla_norm_w, conv_w, w_out,
           ffn_norm_w, w_gate_up, w_down, final_norm_w):
    x = np.asarray(x, np.float32)
    B, S, _ = x.shape
    vecs = np.zeros((128, 92), np.float32)
    vecs[:, 0:16] = np.asarray(mix_norm_w, np.float32)[0].reshape(16, 128).T
    vecs[:, 16:32] = np.asarray(ffn_norm_w, np.float32)[0].reshape(16, 128).T
    cw = np.asarray(conv_w, np.float32)[0]
    vecs[:, 32:80] = cw.reshape(3, 16, 128).transpose(2, 1, 0).reshape(128, 48)
    vecs[:, 80:84] = np.asarray(gla_norm_w, np.float32)[0].reshape(4, 128).T
    vecs[:, 84:92] = np.asarray(b_gk_up, np.float32)[0].reshape(8, 128).T
    fnwb = np.ascontiguousarray(np.broadcast_to(np.asarray(final_norm_w, np.float32)[None, :], (128, D)))
    shared = {
        "w_in": np.ascontiguousarray(np.asarray(w_in, np.float32)[0]),
        "w_out": np.ascontiguousarray(np.asarray(w_out, np.float32)[0]),
        "w_gu": np.ascontiguousarray(np.asarray(w_gate_up, np.float32)[0]),
        "w_dn": np.ascontiguousarray(np.asarray(w_down, np.float32)[0]),
        "wgk": np.ascontiguousarray(np.asarray(w_gk_up, np.float32)[0]),
        "vecs": vecs, "fnwb": fnwb, "cst": _consts(),
    }
    in_maps = []
    own = NMAIN * TT
    for c in range(8):
        b, p = c // 4, c % 4
        xs = np.zeros((MROW0 + own, D), np.float32)
        start = p * own
        if start > 0:
            xs[0:XROW0] = x[b, start - XROW0:start]
            if not USE_CC:
                xs[MROW0 - start:MROW0] = x[b, 0:start]
        xs[MROW0:] = x[b, start:start + own]
        am = np.zeros((128, 4), np.float32)
        am[:, 0:p] = 1.0
        m = dict(shared)
        m["xs"] = xs
        m["am"] = am
        in_maps.append(m)
    nc = build_nc()
    res = run_bass_kernel_spmd(nc, in_maps, core_ids=list(range(8)))
    out = np.zeros((B, S, D), np.float32)
    for c in range(8):
        b, p = c // 4, c % 4
        out[b, p * own:(p + 1) * own] = res.results[c]["out"]
    return out
```

```python
import numpy as np
from contextlib import ExitStack
import concourse.bass as bass
import concourse.mybir as mybir
from concourse.bass_utils import run_bass_kernel_spmd
from concourse.alu_op_type import AluOpType as ALU

F32 = mybir.dt.float32
BF16 = mybir.dt.bfloat16
AF = mybir.ActivationFunctionType

D = 2048
NKC = 16
TT = 512
NTC = 4
FF = 5632
NFC = 44
USE_CC = True
NPRE = 4 if USE_CC else 12
XROW0 = 128
MROW0 = XROW0 if USE_CC else XROW0 + NPRE * TT
NMAIN = 4
EPS = 1e-6
C_Q, C_K, C_V, C_GO, C_GKL, C_GB, C_GC, C_XC, C_MA, C_MB = 0, 1024, 2048, 4096, 6144, 6160, 8208, 10256, 12304, 14352
ENGS = ("pe", "act", "dve", "pool", "sp")
ARENAS = ("XT", "FA")


class Op:
    __slots__ = ("id", "eng", "fn", "dma", "deps", "sem", "tick", "signal", "inc")

    def __init__(self, id, eng, fn, dma):
        self.id = id
        self.eng = eng
        self.fn = fn
        self.dma = dma
        self.deps = set()
        self.sem = None
        self.tick = None
        self.signal = False
        self.inc = 16 if dma else 1


class Prog:
    def __init__(self, nc):
        self.nc = nc
        self.ops = []
        self.by_eng = {e: [] for e in ENGS}
        self.last_writer = {}
        self.readers = {}
        self.arena_keys = {a: [] for a in ARENAS}
        self.dma_keys = {}

    def _expand(self, k):
        if isinstance(k, tuple) and k and k[0] in ARENAS:
            lst = self.arena_keys[k[0]]
            if k not in self.last_writer and k not in self.readers:
                lst.append(k)
                self.readers[k] = []
            return [o for o in lst if o[1] < k[2] and k[1] < o[2]]
        return [k]

    def op(self, eng, fn, reads=(), writes=(), dma_key=None, extra_deps=(), inc=None):
        o = Op(len(self.ops), eng, fn, dma_key is not None)
        deps = o.deps
        for r in reads:
            for k in self._expand(r):
                w = self.last_writer.get(k)
                if w is not None:
                    deps.add(w)
        for w_ in writes:
            for k in self._expand(w_):
                w = self.last_writer.get(k)
                if w is not None:
                    deps.add(w)
                rl = self.readers.get(k)
                if rl:
                    deps.update(rl)
        deps.update(extra_deps)
        deps.discard(o.id)
        for r in reads:
            self.readers.setdefault(r, []).append(o.id)
        for w_ in writes:
            for k in self._expand(w_):
                self.last_writer[k] = o.id
                self.readers[k] = []
        if dma_key is not None:
            c = self.dma_keys.get(dma_key, 0) + 1
            self.dma_keys[dma_key] = c
            if inc is not None:
                o.inc = inc
            o.sem = ("dma", dma_key)
            o.tick = o.inc * c
            o.signal = True
        self.ops.append(o)
        self.by_eng[eng].append(o)
        return o.id

    def emit(self, stack):
        nc = self.nc
        ops = self.ops
        for o in ops:
            nd = set()
            for d in o.deps:
                p = ops[d]
                if (not p.dma) and p.eng == o.eng and o.eng == "pe":
                    continue
                nd.add(d)
            o.deps = nd
            for d in nd:
                ops[d].signal = True
        cnt = {e: 0 for e in ENGS}
        for e in ENGS:
            for o in self.by_eng[e]:
                if o.dma:
                    continue
                if o.signal:
                    cnt[e] += 1
                    o.sem = ("eng", e)
                    o.tick = cnt[e]
        sems = {}
        for e in ENGS:
            if cnt[e] > 0:
                sems[("eng", e)] = stack.enter_context(nc.semaphore("prog_" + e))
        for i, k in enumerate(self.dma_keys):
            sems[("dma", k)] = stack.enter_context(nc.semaphore("dma_%d" % i))
        self.n_sems = len(sems)
        self.max_tick = max([cnt[e] for e in ENGS] + [16 * c for c in self.dma_keys.values()] + [0])
        block = stack.enter_context(nc.Block())

        def run(eng_name, eng):
            waited = {}
            for o in self.by_eng[eng_name]:
                need = {}
                for d in o.deps:
                    p = ops[d]
                    if need.get(p.sem, 0) < p.tick:
                        need[p.sem] = p.tick
                for s, v in need.items():
                    if waited.get(s, 0) >= v:
                        continue
                    eng.wait_ge(sems[s], v)
                    waited[s] = v
                inst = o.fn(eng)
                if o.signal:
                    inst.then_inc(sems[o.sem], o.inc)

        @block.tensor
        def _(e):
            run("pe", e)

        @block.scalar
        def _(e):
            run("act", e)

        @block.vector
        def _(e):
            run("dve", e)

        @block.gpsimd
        def _(e):
            run("pool", e)

        @block.sync
        def _(e):
            run("sp", e)


class V:
    __slots__ = ("ap", "keys")

    def __init__(self, ap, keys):
        self.ap = ap
        self.keys = tuple(keys)

    def __getitem__(self, idx):
        return V(self.ap[idx], self.keys)


def _k(*vs):
    out = []
    for v in vs:
        if isinstance(v, V):
            out.extend(v.keys)
    return out


def _a(v):
    return v.ap if isinstance(v, V) else v


def build_nc():
    nc = bass.Bass("TRN2", target_bir_lowering=False)
    xs = nc.dram_tensor("xs", [MROW0 + NMAIN * TT, D], F32, kind="ExternalInput").ap()
    am_d = nc.dram_tensor("am", [128, 4], F32, kind="ExternalInput").ap()
    lsrc = [nc.dram_tensor("lsrc%d" % i, [128, 2048], F32) for i in range(2)] + [nc.dram_tensor("lsrc2", [128, 32], F32)]
    lall = [nc.dram_tensor("lall%d" % i, [512, 2048], F32) for i in range(2)] + [nc.dram_tensor("lall2", [512, 32], F32)]
    w_in = nc.dram_tensor("w_in", [D, 16400], F32, kind="ExternalInput").ap()
    w_out = nc.dram_tensor("w_out", [D, D], F32, kind="ExternalInput").ap()
    w_gu = nc.dram_tensor("w_gu", [D, 2 * FF], F32, kind="ExternalInput").ap()
    w_dn = nc.dram_tensor("w_dn", [FF, D], F32, kind="ExternalInput").ap()
    wgk_d = nc.dram_tensor("wgk", [16, 1024], F32, kind="ExternalInput").ap()
    vecs_d = nc.dram_tensor("vecs", [128, 92], F32, kind="ExternalInput").ap()
    fnwb_d = nc.dram_tensor("fnwb", [128, D], F32, kind="ExternalInput").ap()
    cst_d = nc.dram_tensor("cst", [128, 384], F32, kind="ExternalInput").ap()
    out_d = nc.dram_tensor("out", [NMAIN * TT, D], F32, kind="ExternalOutput").ap()

    with ExitStack() as st:
        def sb(name, shape, dt):
            return st.enter_context(nc.sbuf_tensor("s_" + name, shape, dt))

        xnT_t = sb("xnT", [128, NKC, TT], BF16)
        mgT_t = sb("mgT", [128, NKC, TT], BF16)
        wr_t = [sb("wr%d" % i, [128, NKC, 512], BF16) for i in range(3)]
        S_t = sb("S", [128, 8, 512], F32)
        Sbf_t = sb("Sbf", [128, 8, 512], BF16)
        cst_t = sb("cst", [128, 384], F32)
        identb_t = sb("identb", [128, 128], BF16)
        ones_t = sb("ones512", [128, 512], F32)
        vecs_t = sb("vecs", [128, 92], F32)
        negb_t = sb("negb", [128, 8], F32)
        fnwb_t = sb("fnwb", [128, D], F32)
        wgk_t = sb("wgkb", [16, 1024], BF16)
        gkl_t = sb("gkl", [16, TT], BF16)
        wgl_t = sb("wgl", [128, NKC, 16], BF16)
        uh_t = sb("uh", [128, 16, 2], F32)
        xnh_t = sb("xnh", [128, NKC, 2], BF16)
        gch_t = sb("gch", [128, 4, 2], F32)
        ss_t = sb("ss", [128, 4], F32)
        rs_t = sb("rs", [128, 4], F32)
        nbl_t = sb("nbl", [128, 4], F32)
        dl_t = sb("dl", [128, 8, 4], F32)
        am_t = sb("am", [128, 4], F32)
        LD_t = sb("LD", [128, 32], F32)
        LDall_t = sb("LDall", [128, 4, 32], F32)
        Deff_t = sb("Deff", [128, 8], F32)
        one8_t = sb("one8", [128, 8], F32)
        xnp_t = [sb("xnp%d" % i, [128, D], BF16) for i in range(2)]
        ssp_t = sb("ssp", [128, 2], F32)
        rsp_t = sb("rsp", [128, 2], F32)
        XT_t = sb("XT", [128, 8192], F32)
        FA_t = sb("FA", [128, 11264], F32)
        banks = [st.enter_context(nc.psum_tensor("ps%d" % i, [128, 512], F32)) for i in range(8)]

        def arena(name, t, lo, nbytes, dt, pat=None, **kw):
            assert lo % 4 == 0 and nbytes % 4 == 0
            ap = t[:, lo // 4:(lo + nbytes) // 4]
            if dt == BF16:
                ap = ap.bitcast(BF16)
            if pat is not None:
                ap = ap.rearrange(pat, **kw)
            return V(ap, [(name, lo, lo + nbytes)])

        xt = arena("XT", XT_t, 0, 32768, F32, "p (a b) -> p a b", a=4)
        xt_tc = [arena("XT", XT_t, tc * 8192, 8192, F32) for tc in range(4)]
        spb = arena("XT", XT_t, 0, 2048, F32)
        E1 = arena("XT", XT_t, 2048, 2048, F32)
        E2 = arena("XT", XT_t, 4096, 2048, F32)
        E3 = arena("XT", XT_t, 6144, 2048, F32)
        qd = [arena("XT", XT_t, 8192 + i * 2048, 2048, BF16, "p (a b) -> p a b", a=2) for i in range(2)]
        kd = [arena("XT", XT_t, 12288 + i * 2048, 2048, BF16, "p (a b) -> p a b", a=2) for i in range(2)]
        kteT = arena("XT", XT_t, 16384, 2048, BF16, "p (a b) -> p a b", a=2)
        kte = [arena("XT", XT_t, 18432 + i * 2048, 2048, BF16, "p (a b) -> p a b", a=4) for i in range(2)]
        vb = [arena("XT", XT_t, 22528 + i * 4096, 4096, BF16, "p (a b) -> p a b", a=4) for i in range(2)]
        actT = [arena("FA", FA_t, fc * 1024, 1024, BF16) for fc in range(NFC)]
        xn_tm = [arena("FA", FA_t, tc * 4096, 4096, BF16) for tc in range(4)]
        junk = arena("FA", FA_t, 16384, 4096, BF16)
        scT = [arena("FA", FA_t, i * 256, 256, BF16) for i in range(2)]
        o_sb = arena("FA", FA_t, 512, 8192, F32, "p (a b) -> p a b", a=4)
        o_cc = [arena("FA", FA_t, 512 + cc * 2048, 2048, F32) for cc in range(4)]
        sq = [arena("FA", FA_t, 8704 + i * 2048, 2048, F32) for i in range(2)]
        lnv = arena("FA", FA_t, 12800, 2048, F32)
        gc = [arena("FA", FA_t, 14848 + cc * 2048, 2048, F32) for cc in range(4)]
        cv = [arena("FA", FA_t, 23040 + cc * 2048, 2048, F32) for cc in range(4)]
        sg = arena("FA", FA_t, 31232, 2048, F32)
        sb2 = arena("FA", FA_t, 33280, 2048, F32)
        tmp = arena("FA", FA_t, 35328, 2048, F32)
        ubuf = [arena("FA", FA_t, 37376 + i * 2064, 2064, F32) for i in range(2)]

        ft = [V(mgT_t[:, 2 * i:2 * i + 2, :].rearrange("p a b -> p (a b)").bitcast(F32),
                [("mgT", 2 * i), ("mgT", 2 * i + 1)]) for i in range(8)]
        xpc = [V(mgT_t[:, 8 * i:8 * i + 8, :].rearrange("p a b -> p (a b)").bitcast(F32),
                 [("mgT", g) for g in range(8 * i, 8 * i + 8)]) for i in range(2)]
        xnp = [V(xnp_t[i][:], [("xnp", i)]) for i in range(2)]
        ssp = [V(ssp_t[:, i:i + 1], [("ssp", i)]) for i in range(2)]
        rsp = [V(rsp_t[:, i:i + 1], [("rsp", i)]) for i in range(2)]
        xnTalt = [arena("FA", FA_t, kc * 1024, 1024, BF16) for kc in range(NKC)]
        xnTalt_halo = V(FA_t[:, 0:4096].bitcast(BF16).rearrange("p (a b) -> p a b", a=NKC)[:, :, TT - 2:TT],
                        [("FA", 0, 16384)])
        am = V(am_t[:], ["am"])
        LD = V(LD_t[:], ["LD"])
        LDall = V(LDall_t[:], ["LDall"])
        Deff = V(Deff_t[:], ["Deff"])
        one8 = V(one8_t[:], ["one8"])
        Lbuf = [V(mgT_t[:, 8 * i:8 * i + 8, :].rearrange("p a b -> p (a b)").bitcast(F32).rearrange("p (a b) -> p a b", a=4),
                  [("mgT", g) for g in range(8 * i, 8 * i + 8)]) for i in range(2)]
        xnTalt_halo0 = V(FA_t[:, 0:4096].bitcast(BF16).rearrange("p (a b) -> p a b", a=NKC)[:, :, 126:128],
                         [("FA", 0, 16384)])
        xnT = [V(xnT_t[:, kc, :], [("xnT", kc)]) for kc in range(NKC)]
        mgT = [V(mgT_t[:, g, :], [("mgT", g)]) for g in range(NKC)]
        Sv = [V(S_t[:, i, :], [("S", i)]) for i in range(8)]
        Sbf = [V(Sbf_t[:, i, :], [("Sbf", i)]) for i in range(8)]
        ident_f = V(cst_t[:, 0:128], ["cst"])
        maskT = V(cst_t[:, 128:256], ["cst"])
        ones128 = V(cst_t[:, 256:384], ["cst"])
        identb = V(identb_t[:], ["identb"])
        ones512 = V(ones_t[:], ["ones512"])
        vecs = V(vecs_t[:], ["vecs"])
        negb = V(negb_t[:], ["negb"])
        fnwb = V(fnwb_t[:], ["fnwb"])
        wgk = V(wgk_t[:], ["wgk"])
        gkl = V(gkl_t[:], ["gkl"])
        uh = [V(uh_t[:, g, :], [("uh", g)]) for g in range(16)]
        xnh = V(xnh_t[:], ["xnh"])
        gch = [V(gch_t[:, cc, :], [("gch", cc)]) for cc in range(4)]
        ss = V(ss_t[:], ["ss"])
        rs = V(rs_t[:], ["rs"])
        nbl = V(nbl_t[:], ["nbl"])
        dl = [V(dl_t[:, i, :], [("dl", i)]) for i in range(8)]
        bankv = [V(banks[i][:], [("ps", i)]) for i in range(8)]
        bankb = [V(banks[i][:].bitcast(BF16), [("ps", i)]) for i in range(8)]

        def vcol(c):
            return vecs[:, c:c + 1]

        def emit_all(P, plan):
            state = {"bank": 0, "wi": 0, "issued": 0}
            rec = []

            def ACT(out, in_, func, bias=None, scale=None):
                kw = {}
                if bias is not None:
                    kw["bias"] = _a(bias)
                if scale is not None:
                    kw["scale"] = scale
                P.op("act", lambda e: e.activation(out=out.ap, in_=in_.ap, func=func, **kw),
                     reads=_k(in_, bias), writes=_k(out))

            def ACOPY(out, in_):
                P.op("act", lambda e: e.copy(out=out.ap, in_=in_.ap), reads=_k(in_), writes=_k(out))

            def TT_(out, in0, in1, op):
                P.op("dve", lambda e: e.tensor_tensor(out=out.ap, in0=in0.ap, in1=in1.ap, op=op),
                     reads=_k(in0, in1), writes=_k(out))

            def TS(out, in0, s1, op0):
                P.op("dve", lambda e: e.tensor_scalar(out=out.ap, in0=in0.ap, scalar1=_a(s1), scalar2=None, op0=op0),
                     reads=_k(in0, s1), writes=_k(out))

            def STT(out, in0, s, in1, op0, op1):
                P.op("dve", lambda e: e.scalar_tensor_tensor(out=out.ap, in0=in0.ap, scalar=_a(s), in1=in1.ap,
                                                             op0=op0, op1=op1),
                     reads=_k(in0, s, in1), writes=_k(out))

            def DCOPY(out, in_):
                P.op("dve", lambda e: e.tensor_copy(out=out.ap, in_=in_.ap), reads=_k(in_), writes=_k(out))

            def MM(out, lhsT, rhs, start, stop):
                P.op("pe", lambda e: e.matmul(out.ap, lhsT=lhsT.ap, rhs=rhs.ap, start=start, stop=stop),
                     reads=_k(lhsT, rhs), writes=_k(out))

            def TR(out, in_):
                P.op("pe", lambda e: e.transpose(out=out.ap, in_=in_.ap, identity=identb.ap),
                     reads=_k(in_, identb), writes=_k(out))

            def nbank():
                b = state["bank"]
                state["bank"] = (b + 1) % 8
                return b

            def issue(desc, idx):
                src, k0, nk, segs = desc
                slot = idx % 3
                dram = {"in": w_in, "out": w_out, "gu": w_gu, "dn": w_dn}[src]
                off = 0
                for (c0, n) in segs:
                    view = dram[k0 * 128:(k0 + nk) * 128, c0:c0 + n].rearrange("(kc p) c -> p kc c", p=128)
                    dst = wr_t[slot][:, 0:nk, off:off + n]
                    P.op("pool", lambda e, dst=dst, view=view: e.dma_start(out=dst, in_=view),
                         writes=[("w", slot)], dma_key=("w", slot))
                    off += n

            def wblock(src, k0, nk, c0, n, c1=None, n1=0):
                segs = ((c0, n),) if c1 is None else ((c0, n), (c1, n1))
                n = n + n1
                desc = (src, k0, nk, segs)
                i = state["wi"]
                state["wi"] = i + 1
                if plan is None:
                    rec.append(desc)
                else:
                    assert plan[i] == desc, (i, plan[i], desc)
                    while state["issued"] < min(len(plan), i + 3):
                        issue(plan[state["issued"]], state["issued"])
                        state["issued"] += 1
                slot = i % 3
                return V(wr_t[slot][:, 0:nk, 0:n], [("w", slot)])

            def group_fm(W, cc, M=128, N=TT, extra_rhs=None, X=None):
                X = xnT if X is None else X
                b = nbank()
                out = bankv[b][0:M, 0:N]
                for kc in range(NKC):
                    MM(out, W[:, kc, cc * 128:cc * 128 + M], X[kc][:, 0:N], kc == 0, kc == NKC - 1)
                if extra_rhs is not None:
                    b2 = nbank()
                    out2 = bankv[b2][0:M, 0:2]
                    for kc in range(NKC):
                        MM(out2, W[:, kc, cc * 128:cc * 128 + M], extra_rhs[:, kc, :], kc == 0, kc == NKC - 1)
                    return out, out2
                return out

            def group_tm(W, src, tc, ncol=512):
                b = nbank()
                out = bankv[b][:, 0:ncol]
                n = len(src)
                for kc in range(n):
                    MM(out, src[kc][:, tc * 128:(tc + 1) * 128], W[:, kc, 0:ncol], kc == 0, kc == n - 1)
                return out

            if True:
                P.op("sp", lambda e: e.dma_start(out=cst_t[:], in_=cst_d), writes=["cst"], dma_key="c_cst")
                P.op("sp", lambda e: e.dma_start(out=vecs_t[:], in_=vecs_d), writes=["vecs"], dma_key="c_vecs")
                P.op("sp", lambda e: e.dma_start(out=fnwb_t[:], in_=fnwb_d), writes=["fnwb"], dma_key="c_fnwb")
                P.op("pool", lambda e: e.dma_start(out=wgk_t[:], in_=wgk_d), writes=["wgk"], dma_key="c_wgk")
                P.op("pool", lambda e: e.dma_start(out=wgl_t[:],
                                                   in_=w_in[:, C_GKL:C_GKL + 16].rearrange("(kc p) c -> p kc c", p=128)),
                     writes=["wgl"], dma_key="c_wgl")
                DCOPY(identb, ident_f)
                P.op("dve", lambda e: e.memset(ones_t[:], 1.0), writes=["ones512"])
                for i in range(8):
                    P.op("dve", lambda e, i=i: e.memset(S_t[:, i, :], 0.0), writes=[("S", i)])
                TS(negb, vecs[:, 84:92], -1.0, ALU.mult)
                P.op("sp", lambda e: e.dma_start(out=am_t[:], in_=am_d), writes=["am"], dma_key="c_am")
                P.op("dve", lambda e: e.memset(LD_t[:], 0.0), writes=["LD"])
                P.op("dve", lambda e: e.memset(one8_t[:], 1.0), writes=["one8"])

            def load_x(row0):
                P.op("sp", lambda e: e.dma_start(out=xt.ap, in_=xs[row0:row0 + TT, :].rearrange("(tc p) d -> p tc d", p=128)),
                     writes=_k(xt), dma_key="x")

            def norm_to_T(wcol0, dst):
                for tc in range(4):
                    P.op("dve", lambda e, tc=tc: e.scalar_tensor_tensor(
                        out=junk.ap, in0=xt_tc[tc].ap, scalar=1.0, in1=xt_tc[tc].ap,
                        op0=ALU.mult, op1=ALU.mult, accum_out=ss_t[:, tc:tc + 1]),
                        reads=_k(xt_tc[tc]), writes=_k(junk, ss))
                ACT(rs, ss, AF.Ln, bias=EPS, scale=1.0 / D)
                ACT(rs, rs, AF.Exp, scale=-0.5)
                for tc in range(4):
                    TS(xn_tm[tc], xt_tc[tc], rs[:, tc:tc + 1], ALU.mult)
                for kc in range(NKC):
                    b = nbank()
                    for tc in range(4):
                        TR(bankb[b][:, tc * 128:(tc + 1) * 128], xn_tm[tc][:, kc * 128:(kc + 1) * 128])
                    TS(dst[kc], bankb[b][:, 0:TT], vcol(wcol0 + kc), ALU.mult)

            def s0_norm(src, j):
                P.op("dve", lambda e: e.scalar_tensor_tensor(
                    out=xnp[j].ap, in0=src.ap, scalar=1.0, in1=src.ap,
                    op0=ALU.mult, op1=ALU.mult, accum_out=ssp[j].ap),
                    reads=_k(src), writes=_k(xnp[j], ssp[j]))
                ACT(rsp[j], ssp[j], AF.Ln, bias=EPS, scale=1.0 / D)
                ACT(rsp[j], rsp[j], AF.Exp, scale=-0.5)
                TS(xnp[j], src, rsp[j], ALU.mult)

            def s0_load(row0, tc, j):
                r0 = row0 + tc * 128
                P.op("sp", lambda e: e.dma_start(out=xpc[j].ap, in_=xs[r0:r0 + 128, :]),
                     writes=_k(xpc[j]), dma_key=("xp", j))
                s0_norm(xpc[j], j)

            def s0_tr(tc, j, dst, wcol0):
                for kq in range(4):
                    b = nbank()
                    for i in range(4):
                        kc = kq * 4 + i
                        TR(bankb[b][:, i * 128:(i + 1) * 128], xnp[j][:, kc * 128:(kc + 1) * 128])
                    for i in range(4):
                        kc = kq * 4 + i
                        TS(dst[kc][:, tc * 128:(tc + 1) * 128], bankb[b][:, i * 128:(i + 1) * 128],
                           vcol(wcol0 + kc), ALU.mult)

            def final_stats():
                for tc in range(4):
                    P.op("dve", lambda e, tc=tc: e.scalar_tensor_tensor(
                        out=junk.ap, in0=xt_tc[tc].ap, scalar=1.0, in1=xt_tc[tc].ap,
                        op0=ALU.mult, op1=ALU.mult, accum_out=ss_t[:, tc:tc + 1]),
                        reads=_k(xt_tc[tc]), writes=_k(junk, ss))
                ACT(rs, ss, AF.Ln, bias=EPS, scale=1.0 / D)
                ACT(rs, rs, AF.Exp, scale=-0.5)

            def gk_low(X=None):
                W = V(wgl_t[:], ["wgl"])
                ps = group_fm(W, 0, M=16, X=X)
                ACOPY(gkl, ps)

            def decay_common(hd):
                b = nbank()
                ps = bankv[b]
                MM(ps, wgk[:, hd * 128:(hd + 1) * 128], gkl, True, True)
                ACT(spb, ps, AF.Exp, bias=negb[:, hd:hd + 1], scale=-1.0)
                ACT(spb, spb, AF.Ln, bias=1.0, scale=1.0)

            def scan(lo, hi):
                P.op("dve", lambda e: e.tensor_tensor_scan(out=E2.ap[:, lo:hi], data0=ones512.ap[:, lo:hi],
                                                           data1=spb.ap[:, lo:hi], initial=0.0,
                                                           op0=ALU.mult, op1=ALU.add),
                     reads=_k(spb, ones512), writes=_k(E2))

            def kt_and_v(h, pp, Wv, X=None):
                X = xnT if X is None else X
                for tc in range(4):
                    ps = group_tm(Wv, X, tc)
                    ACOPY(vb[pp][:, tc, :], ps)
                b = nbank()
                for tc in range(4):
                    for dkc in range(2):
                        TR(bankb[b][:, (tc * 2 + dkc) * 128:(tc * 2 + dkc + 1) * 128],
                           kteT[:, dkc, tc * 128:(tc + 1) * 128])
                P.op("act", lambda e: e.copy(out=kte[pp].ap, in_=bankb[b].ap.rearrange("p (a b) -> p a b", a=4)),
                     reads=_k(bankb[b]), writes=_k(kte[pp]))

            def prefix_tile(pt, last):
                X = xnT if pt % 2 == 0 else xnTalt
                Xn = xnT if (pt + 1) % 2 == 0 or last else xnTalt
                nrow0 = MROW0 if last else XROW0 + (pt + 1) * TT
                gk_low(X)
                for h in range(4):
                    s0_load(nrow0, h, h % 2)
                    if h >= 1:
                        s0_tr(h - 1, (h - 1) % 2, Xn, 0)
                    Wk = wblock("in", 0, NKC, C_K + h * 256, 256)
                    for dkc in range(2):
                        hd = h * 2 + dkc
                        decay_common(hd)
                        scan(0, TT)
                        TS(nbl[:, 0:1], E2[:, TT - 1:TT], -1.0 / 16, ALU.mult)
                        TT_(LD[:, hd:hd + 1], LD[:, hd:hd + 1], nbl[:, 0:1], ALU.add)
                        ACT(E3, E2, AF.Exp, bias=nbl[:, 0:1], scale=1.0 / 16)
                        ACT(dl[hd][:, 0:1], nbl[:, 0:1], AF.Exp)
                        ps = group_fm(Wk, dkc, X=X)
                        TT_(kteT[:, dkc, :], ps, E3, ALU.mult)
                    Wv = wblock("in", 0, NKC, C_V + h * 512, 512)
                    kt_and_v(h, 0, Wv, X)
                    for dkc in range(2):
                        hd = h * 2 + dkc
                        b = nbank()
                        for tc in range(4):
                            MM(bankv[b], kte[0][:, tc, dkc * 128:(dkc + 1) * 128], vb[0][:, tc, :], tc == 0, tc == 3)
                        STT(Sv[hd], Sv[hd], dl[hd][:, 0:1], bankv[b], ALU.mult, ALU.add)
                s0_tr(3, 1, Xn, 0)

            def qkv(h, pp):
                Wqk = wblock("in", 0, NKC, C_Q + h * 256, 256, C_K + h * 256, 256)
                for dkc in range(2):
                    hd = h * 2 + dkc
                    decay_common(hd)
                    for tc in range(4):
                        scan(tc * 128, (tc + 1) * 128)
                    ACT(E1, E2, AF.Exp, scale=-1.0 / 16)
                    TS(nbl, V(E2.ap.rearrange("p (a b) -> p a b", a=4)[:, :, 127], E2.keys), -1.0 / 16, ALU.mult)
                    for tc in range(4):
                        ACT(E3[:, tc * 128:(tc + 1) * 128], E2[:, tc * 128:(tc + 1) * 128], AF.Exp,
                            bias=nbl[:, tc:tc + 1], scale=1.0 / 16)
                    ACT(E2, E2, AF.Exp, scale=1.0 / 16)
                    ACT(dl[hd], nbl, AF.Exp)
                    ps = group_fm(Wqk, dkc)
                    STT(qd[pp][:, dkc, :], ps, 1.0 / 16, E1, ALU.mult, ALU.mult)
                    ps = group_fm(Wqk, 2 + dkc)
                    TT_(kd[pp][:, dkc, :], ps, E2, ALU.mult)
                    TT_(kteT[:, dkc, :], ps, E3, ALU.mult)
                Wv = wblock("in", 0, NKC, C_V + h * 512, 512)
                kt_and_v(h, pp, Wv)

            def gla(h, pp):
                for tc in range(4):
                    tsl = slice(tc * 128, (tc + 1) * 128)
                    b = nbank()
                    sc = bankv[b][:, 0:128]
                    for dkc in range(2):
                        MM(sc, kd[pp][:, dkc, tsl], qd[pp][:, dkc, tsl], dkc == 0, dkc == 1)
                    s_ = scT[tc % 2]
                    TT_(s_, sc, maskT, ALU.mult)
                    b = nbank()
                    for dvc in range(4):
                        dsl = slice(dvc * 128, (dvc + 1) * 128)
                        o = bankv[b][:, dsl]
                        MM(o, vb[pp][:, tc, dsl], s_, True, False)
                        MM(o, Sbf[h * 2][:, dsl], qd[pp][:, 0, tsl], False, False)
                        MM(o, Sbf[h * 2 + 1][:, dsl], qd[pp][:, 1, tsl], False, True)
                    P.op("act", lambda e, b=b, tsl=tsl: e.copy(out=o_sb.ap[:, :, tsl],
                                                               in_=bankv[b].ap.rearrange("p (a b) -> p a b", a=4)),
                         reads=_k(bankv[b]), writes=_k(o_sb))
                    for dkc in range(2):
                        hd = h * 2 + dkc
                        b = nbank()
                        MM(bankv[b], kte[pp][:, tc, dkc * 128:(dkc + 1) * 128], vb[pp][:, tc, :], True, True)
                        STT(Sv[hd], Sv[hd], dl[hd][:, tc:tc + 1], bankv[b], ALU.mult, ALU.add)
                        ACOPY(Sbf[hd], Sv[hd])
                b = nbank()
                for dvc in range(4):
                    ACT(sq[dvc % 2], o_cc[dvc], AF.Square)
                    MM(bankv[b], ones128, sq[dvc % 2], dvc == 0, dvc == 3)
                ACT(lnv, bankv[b], AF.Ln, bias=EPS, scale=1.0 / 512)
                ACT(lnv, lnv, AF.Exp, scale=-0.5)
                for dvc in range(4):
                    STT(o_cc[dvc], o_cc[dvc], vcol(80 + dvc), lnv, ALU.mult, ALU.mult)

            def bmix(h, first_tile):
                Wgc = wblock("in", 0, NKC, C_GC + h * 512, 512)
                for cc in range(4):
                    if first_tile:
                        ps, psh = group_fm(Wgc, cc, extra_rhs=xnh)
                        ACOPY(gch[cc], psh)
                    else:
                        ps = group_fm(Wgc, cc)
                    ACOPY(gc[cc], ps)
                Wxc = wblock("in", 0, NKC, C_XC + h * 512, 512)
                for cc in range(4):
                    g = h * 4 + cc
                    u = ubuf[cc % 2]
                    if first_tile:
                        ps, psh = group_fm(Wxc, cc, extra_rhs=xnh)
                        TT_(uh[g], psh, gch[cc], ALU.mult)
                    else:
                        ps = group_fm(Wxc, cc)
                    DCOPY(u[:, 0:2], uh[g])
                    TT_(u[:, 2:514], ps, gc[cc], ALU.mult)
                    TS(cv[cc], u[:, 2:514], vcol(32 + g * 3 + 2), ALU.mult)
                    STT(cv[cc], u[:, 1:513], vcol(32 + g * 3 + 1), cv[cc], ALU.mult, ALU.add)
                    STT(cv[cc], u[:, 0:512], vcol(32 + g * 3 + 0), cv[cc], ALU.mult, ALU.add)
                    DCOPY(uh[g], u[:, 512:514])
                Wgb = wblock("in", 0, NKC, C_GB + h * 512, 512)
                for cc in range(4):
                    ps = group_fm(Wgb, cc)
                    TT_(cv[cc], ps, cv[cc], ALU.mult)

            def bmix_gate(h):
                Wgo = wblock("in", 0, NKC, C_GO + h * 512, 512)
                for cc in range(4):
                    ps = group_fm(Wgo, cc)
                    ACT(sg, ps, AF.Sigmoid)
                    TT_(tmp, ps, sg, ALU.mult)
                    TT_(o_cc[cc], tmp, o_cc[cc], ALU.mult)
                Wma = wblock("in", 0, NKC, C_MA + h * 512, 512)
                for cc in range(4):
                    ps = group_fm(Wma, cc)
                    ACT(sg, ps, AF.Sigmoid)
                    TT_(o_cc[cc], o_cc[cc], sg, ALU.mult)
                Wmb = wblock("in", 0, NKC, C_MB + h * 512, 512)
                for cc in range(4):
                    g = h * 4 + cc
                    ps = group_fm(Wmb, cc)
                    ACT(sb2, ps, AF.Sigmoid)
                    TT_(cv[cc], cv[cc], sb2, ALU.mult)
                    TT_(mgT[g], cv[cc], o_cc[cc], ALU.add)

            def exchange_start():
                P.op("sp", lambda e: e.dma_start(out=lsrc[0].ap(), in_=S_t[:, 0:4, :].rearrange("p a b -> p (a b)")),
                     reads=[("S", i) for i in range(4)], writes=["lsrc0"], dma_key="ls0")
                P.op("sp", lambda e: e.dma_start(out=lsrc[1].ap(), in_=S_t[:, 4:8, :].rearrange("p a b -> p (a b)")),
                     reads=[("S", i) for i in range(4, 8)], writes=["lsrc1"], dma_key="ls1")
                P.op("sp", lambda e: e.dma_start(out=lsrc[2].ap(), in_=LD_t[:]), reads=["LD"], writes=["lsrc2"],
                     dma_key="ls2")
                prior = [o.id for o in P.by_eng["pool"] if o.dma][-3:]
                ccs = []
                for i in range(3):
                    ccs.append(P.op("pool", lambda e, i=i: e.collective_compute(
                        "AllGather", ALU.bypass, replica_groups=[[0, 1, 2, 3], [4, 5, 6, 7]],
                        ins=[lsrc[i].ap()], outs=[lall[i].ap()]),
                        reads=["lsrc%d" % i], writes=["lall%d" % i], dma_key="cc%d" % i, inc=1,
                        extra_deps=prior))

            def combine():
                for i in range(8):
                    P.op("dve", lambda e, i=i: e.memset(S_t[:, i, :], 0.0), writes=[("S", i)])
                P.op("sp", lambda e: e.dma_start(out=LDall_t[:], in_=lall[2].ap().rearrange("(r p) c -> p r c", p=128)),
                     reads=["lall2"], writes=["LDall"], dma_key="ld_all")
                ACT(LDall, LDall, AF.Exp)
                for j in range(3):
                    for half in range(2):
                        Lb = Lbuf[half]
                        P.op("sp", lambda e, j=j, Lb=Lb, half=half: e.dma_start(
                            out=Lb.ap, in_=lall[half].ap()[j * 128:(j + 1) * 128, :].rearrange("p (a b) -> p a b", a=4)),
                            reads=["lall%d" % half], writes=_k(Lb), dma_key=("lb", half))
                    TS(Deff, LDall[:, j, 0:8], -1.0, ALU.add)
                    STT(Deff, Deff, am[:, j:j + 1], one8, ALU.mult, ALU.add)
                    for hd in range(8):
                        TS(Sv[hd], Sv[hd], Deff[:, hd:hd + 1], ALU.mult)
                        STT(Sv[hd], Lbuf[hd // 4][:, hd % 4, :], am[:, j:j + 1], Sv[hd], ALU.mult, ALU.add)
                for i in range(8):
                    ACOPY(Sbf[i], Sv[i])

            def main_tile(mt):
                row0 = MROW0 + mt * TT
                gk_low()
                qkv(0, 0)
                for h in range(4):
                    if h + 1 < 4:
                        qkv(h + 1, (h + 1) % 2)
                    bmix(h, mt == 0)
                    if USE_CC and mt == 0 and h == 0:
                        combine()
                    gla(h, h % 2)
                    if h == 3:
                        load_x(row0)
                    bmix_gate(h)
                for db in range(4):
                    Wo = wblock("out", 0, NKC, db * 512, 512)
                    for tc in range(4):
                        ps = group_tm(Wo, mgT, tc)
                        dsl = slice(db * 512, (db + 1) * 512)
                        TT_(xt_tc[tc][:, dsl], ps, xt_tc[tc][:, dsl], ALU.add)
                        if db == 3:
                            s0_norm(xt_tc[tc], tc % 2)
                            if tc >= 1:
                                s0_tr(tc - 1, (tc - 1) % 2, xnT, 16)
                s0_tr(3, 1, xnT, 16)
                for fb in range(11):
                    Wg = wblock("gu", 0, NKC, fb * 512, 512)
                    for cc in range(4):
                        ps = group_fm(Wg, cc)
                        ACT(ft[4 + cc % 2], ps, AF.Sigmoid)
                        TT_(ft[cc], ps, ft[4 + cc % 2], ALU.mult)
                    Wu = wblock("gu", 0, NKC, FF + fb * 512, 512)
                    for cc in range(4):
                        ps = group_fm(Wu, cc)
                        TT_(actT[fb * 4 + cc], ps, ft[cc], ALU.mult)
                for db in range(4):
                    bs = [nbank() for _ in range(4)]
                    subs = [(0, 16), (16, 16), (32, 12)]
                    for (k0, nk) in subs:
                        Wd = wblock("dn", k0, nk, db * 512, 512)
                        for tc in range(4):
                            for j in range(nk):
                                fc = k0 + j
                                MM(bankv[bs[tc]], actT[fc][:, tc * 128:(tc + 1) * 128], Wd[:, j, :],
                                   fc == 0, fc == NFC - 1)
                    for tc in range(4):
                        dsl = slice(db * 512, (db + 1) * 512)
                        TT_(xt_tc[tc][:, dsl], bankv[bs[tc]], xt_tc[tc][:, dsl], ALU.add)
                        if db == 3 and mt == NMAIN - 1:
                            j = tc % 2
                            P.op("dve", lambda e, tc=tc, j=j: e.scalar_tensor_tensor(
                                out=xnp[j].ap, in0=xt_tc[tc].ap, scalar=1.0, in1=xt_tc[tc].ap,
                                op0=ALU.mult, op1=ALU.mult, accum_out=ssp[j].ap),
                                reads=_k(xt_tc[tc]), writes=_k(xnp[j], ssp[j]))
                            ACT(rsp[j], ssp[j], AF.Ln, bias=EPS, scale=1.0 / D)
                            ACT(rsp[j], rsp[j], AF.Exp, scale=-0.5)
                            STT(xt_tc[tc], xt_tc[tc], rsp[j], fnwb, ALU.mult, ALU.mult)
                            r0 = mt * TT + tc * 128
                            P.op("sp", lambda e, tc=tc, r0=r0: e.dma_start(out=out_d[r0:r0 + 128, :], in_=xt_tc[tc].ap),
                                 reads=_k(xt_tc[tc]), dma_key=("o", mt))
                    if mt + 1 < NMAIN:
                        s0_load(row0 + TT, db, db % 2)
                        if db >= 1:
                            s0_tr(db - 1, (db - 1) % 2, xnT, 0)
                if mt + 1 < NMAIN:
                    s0_tr(3, 1, xnT, 0)
                if mt == NMAIN - 1:
                    return
                final_stats()
                for tc in range(4):
                    STT(xt_tc[tc], xt_tc[tc], rs[:, tc:tc + 1], fnwb, ALU.mult, ALU.mult)
                P.op("sp", lambda e: e.dma_start(out=out_d[mt * TT:(mt + 1) * TT, :].rearrange("(tc p) d -> p tc d", p=128),
                                                 in_=xt.ap),
                     reads=_k(xt), dma_key=("o", mt))

            s0_load(0, 0, 0)
            s0_tr(0, 0, xnTalt, 0)
            DCOPY(xnh, xnTalt_halo0)
            for tc in range(4):
                s0_load(XROW0, tc, (tc + 1) % 2)
                s0_tr(tc, (tc + 1) % 2, xnT, 0)
            for pt in range(NPRE):
                prefix_tile(pt, pt == NPRE - 1)
            if USE_CC:
                exchange_start()
            else:
                for i in range(8):
                    ACOPY(Sbf[i], Sv[i])
            for mt in range(NMAIN):
                main_tile(mt)
            P.op("sp", lambda e: None, extra_deps=[o.id for o in P.ops if o.dma and o.eng == "sp"])
            return rec

        plan = emit_all(Prog(nc), None)
        P = Prog(nc)
        emit_all(P, plan)
        P.emit(st)
        print("ops", len(P.ops), "sems", P.n_sems, "max_tick", P.max_tick, "wblocks", len(plan))
    return nc


def _consts():
    c = np.zeros((128, 384), np.float32)
    c[:, 0:128] = np.eye(128, dtype=np.float32)
    j = np.arange(128)[:, None]
    i = np.arange(128)[None, :]
    c[:, 128:256] = (j <= i).astype(np.float32)
    c[:, 256:384] = 1.0
    return c


def kernel(x, mix_norm_w, w_in, w_gk_up, b_gk_up, gla_norm_w, conv_w, w_out,
           ffn_norm_w, w_gate_up, w_down, final_norm_w):
    x = np.asarray(x, np.float32)
    B, S, _ = x.shape
    vecs = np.zeros((128, 92), np.float32)
    vecs[:, 0:16] = np.asarray(mix_norm_w, np.float32)[0].reshape(16, 128).T
    vecs[:, 16:32] = np.asarray(ffn_norm_w, np.float32)[0].reshape(16, 128).T
    cw = np.asarray(conv_w, np.float32)[0]
    vecs[:, 32:80] = cw.reshape(3, 16, 128).transpose(2, 1, 0).reshape(128, 48)
    vecs[:, 80:84] = np.asarray(gla_norm_w, np.float32)[0].reshape(4, 128).T
    vecs[:, 84:92] = np.asarray(b_gk_up, np.float32)[0].reshape(8, 128).T
    fnwb = np.ascontiguousarray(np.broadcast_to(np.asarray(final_norm_w, np.float32)[None, :], (128, D)))
    shared = {
        "w_in": np.ascontiguousarray(np.asarray(w_in, np.float32)[0]),
        "w_out": np.ascontiguousarray(np.asarray(w_out, np.float32)[0]),
        "w_gu": np.ascontiguousarray(np.asarray(w_gate_up, np.float32)[0]),
        "w_dn": np.ascontiguousarray(np.asarray(w_down, np.float32)[0]),
        "wgk": np.ascontiguousarray(np.asarray(w_gk_up, np.float32)[0]),
        "vecs": vecs, "fnwb": fnwb, "cst": _consts(),
    }
    in_maps = []
    own = NMAIN * TT
    for c in range(8):
        b, p = c // 4, c % 4
        xs = np.zeros((MROW0 + own, D), np.float32)
        start = p * own
        if start > 0:
            xs[0:XROW0] = x[b, start - XROW0:start]
            if not USE_CC:
                xs[MROW0 - start:MROW0] = x[b, 0:start]
        xs[MROW0:] = x[b, start:start + own]
        am = np.zeros((128, 4), np.float32)
        am[:, 0:p] = 1.0
        m = dict(shared)
        m["xs"] = xs
        m["am"] = am
        in_maps.append(m)
    nc = build_nc()
    res = run_bass_kernel_spmd(nc, in_maps, core_ids=list(range(8)))
    out = np.zeros((B, S, D), np.float32)
    for c in range(8):
        b, p = c // 4, c % 4
        out[b, p * own:(p + 1) * own] = res.results[c]["out"]
    return out
```

```python
import numpy as np
from contextlib import ExitStack
import concourse.bass as bass
import concourse.mybir as mybir
from concourse.bass_utils import run_bass_kernel_spmd
from concourse.alu_op_type import AluOpType as ALU

F32 = mybir.dt.float32
BF16 = mybir.dt.bfloat16
AF = mybir.ActivationFunctionType

D = 2048
NKC = 16
TT = 512
NTC = 4
FF = 5632
NFC = 44
USE_CC = True
NPRE = 4 if USE_CC else 12
XROW0 = 128
MROW0 = XROW0 if USE_CC else XROW0 + NPRE * TT
NMAIN = 4
EPS = 1e-6
C_Q, C_K, C_V, C_GO, C_GKL, C_GB, C_GC, C_XC, C_MA, C_MB = 0, 1024, 2048, 4096, 6144, 6160, 8208, 10256, 12304, 14352
ENGS = ("pe", "act", "dve", "pool", "sp")
ARENAS = ("XT", "FA")


class Op:
    __slots__ = ("id", "eng", "fn", "dma", "deps", "sem", "tick", "signal", "inc")

    def __init__(self, id, eng, fn, dma):
        self.id = id
        self.eng = eng
        self.fn = fn
        self.dma = dma
        self.deps = set()
        self.sem = None
        self.tick = None
        self.signal = False
        self.inc = 16 if dma else 1


class Prog:
    def __init__(self, nc):
        self.nc = nc
        self.ops = []
        self.by_eng = {e: [] for e in ENGS}
        self.last_writer = {}
        self.readers = {}
        self.arena_keys = {a: [] for a in ARENAS}
        self.dma_keys = {}

    def _expand(self, k):
        if isinstance(k, tuple) and k and k[0] in ARENAS:
            lst = self.arena_keys[k[0]]
            if k not in self.last_writer and k not in self.readers:
                lst.append(k)
                self.readers[k] = []
            return [o for o in lst if o[1] < k[2] and k[1] < o[2]]
        return [k]

    def op(self, eng, fn, reads=(), writes=(), dma_key=None, extra_deps=(), inc=None):
        o = Op(len(self.ops), eng, fn, dma_key is not None)
        deps = o.deps
        for r in reads:
            for k in self._expand(r):
                w = self.last_writer.get(k)
                if w is not None:
                    deps.add(w)
        for w_ in writes:
            for k in self._expand(w_):
                w = self.last_writer.get(k)
                if w is not None:
                    deps.add(w)
                rl = self.readers.get(k)
                if rl:
                    deps.update(rl)
        deps.update(extra_deps)
        deps.discard(o.id)
        for r in reads:
            self.readers.setdefault(r, []).append(o.id)
        for w_ in writes:
            for k in self._expand(w_):
                self.last_writer[k] = o.id
                self.readers[k] = []
        if dma_key is not None:
            c = self.dma_keys.get(dma_key, 0) + 1
            self.dma_keys[dma_key] = c
            if inc is not None:
                o.inc = inc
            o.sem = ("dma", dma_key)
            o.tick = o.inc * c
            o.signal = True
        self.ops.append(o)
        self.by_eng[eng].append(o)
        return o.id

    def emit(self, stack):
        nc = self.nc
        ops = self.ops
        for o in ops:
            nd = set()
            for d in o.deps:
                p = ops[d]
                if (not p.dma) and p.eng == o.eng and o.eng == "pe":
                    continue
                nd.add(d)
            o.deps = nd
            for d in nd:
                ops[d].signal = True
        cnt = {e: 0 for e in ENGS}
        for e in ENGS:
            for o in self.by_eng[e]:
                if o.dma:
                    continue
                if o.signal:
                    cnt[e] += 1
                    o.sem = ("eng", e)
                    o.tick = cnt[e]
        sems = {}
        for e in ENGS:
            if cnt[e] > 0:
                sems[("eng", e)] = stack.enter_context(nc.semaphore("prog_" + e))
        for i, k in enumerate(self.dma_keys):
            sems[("dma", k)] = stack.enter_context(nc.semaphore("dma_%d" % i))
        self.n_sems = len(sems)
        self.max_tick = max([cnt[e] for e in ENGS] + [16 * c for c in self.dma_keys.values()] + [0])
        block = stack.enter_context(nc.Block())

        def run(eng_name, eng):
            waited = {}
            for o in self.by_eng[eng_name]:
                need = {}
                for d in o.deps:
                    p = ops[d]
                    if need.get(p.sem, 0) < p.tick:
                        need[p.sem] = p.tick
                for s, v in need.items():
                    if waited.get(s, 0) >= v:
                        continue
                    eng.wait_ge(sems[s], v)
                    waited[s] = v
                inst = o.fn(eng)
                if o.signal:
                    inst.then_inc(sems[o.sem], o.inc)

        @block.tensor
        def _(e):
            run("pe", e)

        @block.scalar
        def _(e):
            run("act", e)

        @block.vector
        def _(e):
            run("dve", e)

        @block.gpsimd
        def _(e):
            run("pool", e)

        @block.sync
        def _(e):
            run("sp", e)


class V:
    __slots__ = ("ap", "keys")

    def __init__(self, ap, keys):
        self.ap = ap
        self.keys = tuple(keys)

    def __getitem__(self, idx):
        return V(self.ap[idx], self.keys)


def _k(*vs):
    out = []
    for v in vs:
        if isinstance(v, V):
            out.extend(v.keys)
    return out


def _a(v):
    return v.ap if isinstance(v, V) else v


def build_nc():
    nc = bass.Bass("TRN2", target_bir_lowering=False)
    xs = nc.dram_tensor("xs", [MROW0 + NMAIN * TT, D], F32, kind="ExternalInput").ap()
    am_d = nc.dram_tensor("am", [128, 4], F32, kind="ExternalInput").ap()
    lsrc = [nc.dram_tensor("lsrc%d" % i, [128, 2048], F32) for i in range(2)] + [nc.dram_tensor("lsrc2", [128, 32], F32)]
    lall = [nc.dram_tensor("lall%d" % i, [512, 2048], F32) for i in range(2)] + [nc.dram_tensor("lall2", [512, 32], F32)]
    w_in = nc.dram_tensor("w_in", [D, 16400], F32, kind="ExternalInput").ap()
    w_out = nc.dram_tensor("w_out", [D, D], F32, kind="ExternalInput").ap()
    w_gu = nc.dram_tensor("w_gu", [D, 2 * FF], F32, kind="ExternalInput").ap()
    w_dn = nc.dram_tensor("w_dn", [FF, D], F32, kind="ExternalInput").ap()
    wgk_d = nc.dram_tensor("wgk", [16, 1024], F32, kind="ExternalInput").ap()
    vecs_d = nc.dram_tensor("vecs", [128, 92], F32, kind="ExternalInput").ap()
    fnwb_d = nc.dram_tensor("fnwb", [128, D], F32, kind="ExternalInput").ap()
    cst_d = nc.dram_tensor("cst", [128, 384], F32, kind="ExternalInput").ap()
    out_d = nc.dram_tensor("out", [NMAIN * TT, D], F32, kind="ExternalOutput").ap()

    with ExitStack() as st:
        def sb(name, shape, dt):
            return st.enter_context(nc.sbuf_tensor("s_" + name, shape, dt))

        xnT_t = sb("xnT", [128, NKC, TT], BF16)
        mgT_t = sb("mgT", [128, NKC, TT], BF16)
        wr_t = [sb("wr%d" % i, [128, NKC, 512], BF16) for i in range(3)]
        S_t = sb("S", [128, 8, 512], F32)
        Sbf_t = sb("Sbf", [128, 8, 512], BF16)
        cst_t = sb("cst", [128, 384], F32)
        identb_t = sb("identb", [128, 128], BF16)
        ones_t = sb("ones512", [128, 512], F32)
        vecs_t = sb("vecs", [128, 92], F32)
        negb_t = sb("negb", [128, 8], F32)
        fnwb_t = sb("fnwb", [128, D], F32)
        wgk_t = sb("wgkb", [16, 1024], BF16)
        gkl_t = sb("gkl", [16, TT], BF16)
        wgl_t = sb("wgl", [128, NKC, 16], BF16)
        uh_t = sb("uh", [128, 16, 2], F32)
        xnh_t = sb("xnh", [128, NKC, 2], BF16)
        gch_t = sb("gch", [128, 4, 2], F32)
        ss_t = sb("ss", [128, 4], F32)
        rs_t = sb("rs", [128, 4], F32)
        nbl_t = sb("nbl", [128, 4], F32)
        dl_t = sb("dl", [128, 8, 4], F32)
        am_t = sb("am", [128, 4], F32)
        LD_t = sb("LD", [128, 32], F32)
        LDall_t = sb("LDall", [128, 4, 32], F32)
        Deff_t = sb("Deff", [128, 8], F32)
        one8_t = sb("one8", [128, 8], F32)
        xnp_t = [sb("xnp%d" % i, [128, D], BF16) for i in range(2)]
        ssp_t = sb("ssp", [128, 2], F32)
        rsp_t = sb("rsp", [128, 2], F32)
        XT_t = sb("XT", [128, 8192], F32)
        FA_t = sb("FA", [128, 11264], F32)
        banks = [st.enter_context(nc.psum_tensor("ps%d" % i, [128, 512], F32)) for i in range(8)]

        def arena(name, t, lo, nbytes, dt, pat=None, **kw):
            assert lo % 4 == 0 and nbytes % 4 == 0
            ap = t[:, lo // 4:(lo + nbytes) // 4]
            if dt == BF16:
                ap = ap.bitcast(BF16)
            if pat is not None:
                ap = ap.rearrange(pat, **kw)
            return V(ap, [(name, lo, lo + nbytes)])

        xt = arena("XT", XT_t, 0, 32768, F32, "p (a b) -> p a b", a=4)
        xt_tc = [arena("XT", XT_t, tc * 8192, 8192, F32) for tc in range(4)]
        spb = arena("XT", XT_t, 0, 2048, F32)
        E1 = arena("XT", XT_t, 2048, 2048, F32)
        E2 = arena("XT", XT_t, 4096, 2048, F32)
        E3 = arena("XT", XT_t, 6144, 2048, F32)
        qd = [arena("XT", XT_t, 8192 + i * 2048, 2048, BF16, "p (a b) -> p a b", a=2) for i in range(2)]
        kd = [arena("XT", XT_t, 12288 + i * 2048, 2048, BF16, "p (a b) -> p a b", a=2) for i in range(2)]
        kteT = arena("XT", XT_t, 16384, 2048, BF16, "p (a b) -> p a b", a=2)
        kte = [arena("XT", XT_t, 18432 + i * 2048, 2048, BF16, "p (a b) -> p a b", a=4) for i in range(2)]
        vb = [arena("XT", XT_t, 22528 + i * 4096, 4096, BF16, "p (a b) -> p a b", a=4) for i in range(2)]
        actT = [arena("FA", FA_t, fc * 1024, 1024, BF16) for fc in range(NFC)]
        xn_tm = [arena("FA", FA_t, tc * 4096, 4096, BF16) for tc in range(4)]
        junk = arena("FA", FA_t, 16384, 4096, BF16)
        scT = [arena("FA", FA_t, i * 256, 256, BF16) for i in range(2)]
        o_sb = arena("FA", FA_t, 512, 8192, F32, "p (a b) -> p a b", a=4)
        o_cc = [arena("FA", FA_t, 512 + cc * 2048, 2048, F32) for cc in range(4)]
        sq = [arena("FA", FA_t, 8704 + i * 2048, 2048, F32) for i in range(2)]
        lnv = arena("FA", FA_t, 12800, 2048, F32)
        gc = [arena("FA", FA_t, 14848 + cc * 2048, 2048, F32) for cc in range(4)]
        cv = [arena("FA", FA_t, 23040 + cc * 2048, 2048, F32) for cc in range(4)]
        sg = arena("FA", FA_t, 31232, 2048, F32)
        sb2 = arena("FA", FA_t, 33280, 2048, F32)
        tmp = arena("FA", FA_t, 35328, 2048, F32)
        ubuf = [arena("FA", FA_t, 37376 + i * 2064, 2064, F32) for i in range(2)]

        ft = [V(mgT_t[:, 2 * i:2 * i + 2, :].rearrange("p a b -> p (a b)").bitcast(F32),
                [("mgT", 2 * i), ("mgT", 2 * i + 1)]) for i in range(8)]
        xpc = [V(mgT_t[:, 8 * i:8 * i + 8, :].rearrange("p a b -> p (a b)").bitcast(F32),
                 [("mgT", g) for g in range(8 * i, 8 * i + 8)]) for i in range(2)]
        xnp = [V(xnp_t[i][:], [("xnp", i)]) for i in range(2)]
        ssp = [V(ssp_t[:, i:i + 1], [("ssp", i)]) for i in range(2)]
        rsp = [V(rsp_t[:, i:i + 1], [("rsp", i)]) for i in range(2)]
        xnTalt = [arena("FA", FA_t, kc * 1024, 1024, BF16) for kc in range(NKC)]
        xnTalt_halo = V(FA_t[:, 0:4096].bitcast(BF16).rearrange("p (a b) -> p a b", a=NKC)[:, :, TT - 2:TT],
                        [("FA", 0, 16384)])
        am = V(am_t[:], ["am"])
        LD = V(LD_t[:], ["LD"])
        LDall = V(LDall_t[:], ["LDall"])
        Deff = V(Deff_t[:], ["Deff"])
        one8 = V(one8_t[:], ["one8"])
        Lbuf = [V(mgT_t[:, 8 * i:8 * i + 8, :].rearrange("p a b -> p (a b)").bitcast(F32).rearrange("p (a b) -> p a b", a=4),
                  [("mgT", g) for g in range(8 * i, 8 * i + 8)]) for i in range(2)]
        xnTalt_halo0 = V(FA_t[:, 0:4096].bitcast(BF16).rearrange("p (a b) -> p a b", a=NKC)[:, :, 126:128],
                         [("FA", 0, 16384)])
        xnT = [V(xnT_t[:, kc, :], [("xnT", kc)]) for kc in range(NKC)]
        mgT = [V(mgT_t[:, g, :], [("mgT", g)]) for g in range(NKC)]
        Sv = [V(S_t[:, i, :], [("S", i)]) for i in range(8)]
        Sbf = [V(Sbf_t[:, i, :], [("Sbf", i)]) for i in range(8)]
        ident_f = V(cst_t[:, 0:128], ["cst"])
        maskT = V(cst_t[:, 128:256], ["cst"])
        ones128 = V(cst_t[:, 256:384], ["cst"])
        identb = V(identb_t[:], ["identb"])
        ones512 = V(ones_t[:], ["ones512"])
        vecs = V(vecs_t[:], ["vecs"])
        negb = V(negb_t[:], ["negb"])
        fnwb = V(fnwb_t[:], ["fnwb"])
        wgk = V(wgk_t[:], ["wgk"])
        gkl = V(gkl_t[:], ["gkl"])
        uh = [V(uh_t[:, g, :], [("uh", g)]) for g in range(16)]
        xnh = V(xnh_t[:], ["xnh"])
        gch = [V(gch_t[:, cc, :], [("gch", cc)]) for cc in range(4)]
        ss = V(ss_t[:], ["ss"])
        rs = V(rs_t[:], ["rs"])
        nbl = V(nbl_t[:], ["nbl"])
        dl = [V(dl_t[:, i, :], [("dl", i)]) for i in range(8)]
        bankv = [V(banks[i][:], [("ps", i)]) for i in range(8)]
        bankb = [V(banks[i][:].bitcast(BF16), [("ps", i)]) for i in range(8)]

        def vcol(c):
            return vecs[:, c:c + 1]

        def emit_all(P, plan):
            state = {"bank": 0, "wi": 0, "issued": 0}
            rec = []

            def ACT(out, in_, func, bias=None, scale=None):
                kw = {}
                if bias is not None:
                    kw["bias"] = _a(bias)
                if scale is not None:
                    kw["scale"] = scale
                P.op("act", lambda e: e.activation(out=out.ap, in_=in_.ap, func=func, **kw),
                     reads=_k(in_, bias), writes=_k(out))

            def ACOPY(out, in_):
                P.op("act", lambda e: e.copy(out=out.ap, in_=in_.ap), reads=_k(in_), writes=_k(out))

            def TT_(out, in0, in1, op):
                P.op("dve", lambda e: e.tensor_tensor(out=out.ap, in0=in0.ap, in1=in1.ap, op=op),
                     reads=_k(in0, in1), writes=_k(out))

            def TS(out, in0, s1, op0):
                P.op("dve", lambda e: e.tensor_scalar(out=out.ap, in0=in0.ap, scalar1=_a(s1), scalar2=None, op0=op0),
                     reads=_k(in0, s1), writes=_k(out))

            def STT(out, in0, s, in1, op0, op1):
                P.op("dve", lambda e: e.scalar_tensor_tensor(out=out.ap, in0=in0.ap, scalar=_a(s), in1=in1.ap,
                                                             op0=op0, op1=op1),
                     reads=_k(in0, s, in1), writes=_k(out))

            def DCOPY(out, in_):
                P.op("dve", lambda e: e.tensor_copy(out=out.ap, in_=in_.ap), reads=_k(in_), writes=_k(out))

            def MM(out, lhsT, rhs, start, stop):
                P.op("pe", lambda e: e.matmul(out.ap, lhsT=lhsT.ap, rhs=rhs.ap, start=start, stop=stop),
                     reads=_k(lhsT, rhs), writes=_k(out))

            def TR(out, in_):
                P.op("pe", lambda e: e.transpose(out=out.ap, in_=in_.ap, identity=identb.ap),
                     reads=_k(in_, identb), writes=_k(out))

            def nbank():
                b = state["bank"]
                state["bank"] = (b + 1) % 8
                return b

            def issue(desc, idx):
                src, k0, nk, segs = desc
                slot = idx % 3
                dram = {"in": w_in, "out": w_out, "gu": w_gu, "dn": w_dn}[src]
                off = 0
                for (c0, n) in segs:
                    view = dram[k0 * 128:(k0 + nk) * 128, c0:c0 + n].rearrange("(kc p) c -> p kc c", p=128)
                    dst = wr_t[slot][:, 0:nk, off:off + n]
                    P.op("pool", lambda e, dst=dst, view=view: e.dma_start(out=dst, in_=view),
                         writes=[("w", slot)], dma_key=("w", slot))
                    off += n

            def wblock(src, k0, nk, c0, n, c1=None, n1=0):
                segs = ((c0, n),) if c1 is None else ((c0, n), (c1, n1))
                n = n + n1
                desc = (src, k0, nk, segs)
                i = state["wi"]
                state["wi"] = i + 1
                if plan is None:
                    rec.append(desc)
                else:
                    assert plan[i] == desc, (i, plan[i], desc)
                    while state["issued"] < min(len(plan), i + 3):
                        issue(plan[state["issued"]], state["issued"])
                        state["issued"] += 1
                slot = i % 3
                return V(wr_t[slot][:, 0:nk, 0:n], [("w", slot)])

            def group_fm(W, cc, M=128, N=TT, extra_rhs=None, X=None):
                X = xnT if X is None else X
                b = nbank()
                out = bankv[b][0:M, 0:N]
                for kc in range(NKC):
                    MM(out, W[:, kc, cc * 128:cc * 128 + M], X[kc][:, 0:N], kc == 0, kc == NKC - 1)
                if extra_rhs is not None:
                    b2 = nbank()
                    out2 = bankv[b2][0:M, 0:2]
                    for kc in range(NKC):
                        MM(out2, W[:, kc, cc * 128:cc * 128 + M], extra_rhs[:, kc, :], kc == 0, kc == NKC - 1)
                    return out, out2
                return out

            def group_tm(W, src, tc, ncol=512):
                b = nbank()
                out = bankv[b][:, 0:ncol]
                n = len(src)
                for kc in range(n):
                    MM(out, src[kc][:, tc * 128:(tc + 1) * 128], W[:, kc, 0:ncol], kc == 0, kc == n - 1)
                return out

            if True:
                P.op("sp", lambda e: e.dma_start(out=cst_t[:], in_=cst_d), writes=["cst"], dma_key="c_cst")
                P.op("sp", lambda e: e.dma_start(out=vecs_t[:], in_=vecs_d), writes=["vecs"], dma_key="c_vecs")
                P.op("sp", lambda e: e.dma_start(out=fnwb_t[:], in_=fnwb_d), writes=["fnwb"], dma_key="c_fnwb")
                P.op("pool", lambda e: e.dma_start(out=wgk_t[:], in_=wgk_d), writes=["wgk"], dma_key="c_wgk")
                P.op("pool", lambda e: e.dma_start(out=wgl_t[:],
                                                   in_=w_in[:, C_GKL:C_GKL + 16].rearrange("(kc p) c -> p kc c", p=128)),
                     writes=["wgl"], dma_key="c_wgl")
                DCOPY(identb, ident_f)
                P.op("dve", lambda e: e.memset(ones_t[:], 1.0), writes=["ones512"])
                for i in range(8):
                    P.op("dve", lambda e, i=i: e.memset(S_t[:, i, :], 0.0), writes=[("S", i)])
                TS(negb, vecs[:, 84:92], -1.0, ALU.mult)
                P.op("sp", lambda e: e.dma_start(out=am_t[:], in_=am_d), writes=["am"], dma_key="c_am")
                P.op("dve", lambda e: e.memset(LD_t[:], 0.0), writes=["LD"])
                P.op("dve", lambda e: e.memset(one8_t[:], 1.0), writes=["one8"])

            def load_x(row0):
                P.op("sp", lambda e: e.dma_start(out=xt.ap, in_=xs[row0:row0 + TT, :].rearrange("(tc p) d -> p tc d", p=128)),
                     writes=_k(xt), dma_key="x")

            def norm_to_T(wcol0, dst):
                for tc in range(4):
                    P.op("dve", lambda e, tc=tc: e.scalar_tensor_tensor(
                        out=junk.ap, in0=xt_tc[tc].ap, scalar=1.0, in1=xt_tc[tc].ap,
                        op0=ALU.mult, op1=ALU.mult, accum_out=ss_t[:, tc:tc + 1]),
                        reads=_k(xt_tc[tc]), writes=_k(junk, ss))
                ACT(rs, ss, AF.Ln, bias=EPS, scale=1.0 / D)
                ACT(rs, rs, AF.Exp, scale=-0.5)
                for tc in range(4):
                    TS(xn_tm[tc], xt_tc[tc], rs[:, tc:tc + 1], ALU.mult)
                for kc in range(NKC):
                    b = nbank()
                    for tc in range(4):
                        TR(bankb[b][:, tc * 128:(tc + 1) * 128], xn_tm[tc][:, kc * 128:(kc + 1) * 128])
                    TS(dst[kc], bankb[b][:, 0:TT], vcol(wcol0 + kc), ALU.mult)

            def s0_norm(src, j):
                P.op("dve", lambda e: e.scalar_tensor_tensor(
                    out=xnp[j].ap, in0=src.ap, scalar=1.0, in1=src.ap,
                    op0=ALU.mult, op1=ALU.mult, accum_out=ssp[j].ap),
                    reads=_k(src), writes=_k(xnp[j], ssp[j]))
                ACT(rsp[j], ssp[j], AF.Ln, bias=EPS, scale=1.0 / D)
                ACT(rsp[j], rsp[j], AF.Exp, scale=-0.5)
                TS(xnp[j], src, rsp[j], ALU.mult)

            def s0_load(row0, tc, j):
                r0 = row0 + tc * 128
                P.op("sp", lambda e: e.dma_start(out=xpc[j].ap, in_=xs[r0:r0 + 128, :]),
                     writes=_k(xpc[j]), dma_key=("xp", j))
                s0_norm(xpc[j], j)

            def s0_tr(tc, j, dst, wcol0):
                for kq in range(4):
                    b = nbank()
                    for i in range(4):
                        kc = kq * 4 + i
                        TR(bankb[b][:, i * 128:(i + 1) * 128], xnp[j][:, kc * 128:(kc + 1) * 128])
                    for i in range(4):
                        kc = kq * 4 + i
                        TS(dst[kc][:, tc * 128:(tc + 1) * 128], bankb[b][:, i * 128:(i + 1) * 128],
                           vcol(wcol0 + kc), ALU.mult)

            def final_stats():
                for tc in range(4):
                    P.op("dve", lambda e, tc=tc: e.scalar_tensor_tensor(
                        out=junk.ap, in0=xt_tc[tc].ap, scalar=1.0, in1=xt_tc[tc].ap,
                        op0=ALU.mult, op1=ALU.mult, accum_out=ss_t[:, tc:tc + 1]),
                        reads=_k(xt_tc[tc]), writes=_k(junk, ss))
                ACT(rs, ss, AF.Ln, bias=EPS, scale=1.0 / D)
                ACT(rs, rs, AF.Exp, scale=-0.5)

            def gk_low(X=None):
                W = V(wgl_t[:], ["wgl"])
                ps = group_fm(W, 0, M=16, X=X)
                ACOPY(gkl, ps)

            def decay_common(hd):
                b = nbank()
                ps = bankv[b]
                MM(ps, wgk[:, hd * 128:(hd + 1) * 128], gkl, True, True)
                ACT(spb, ps, AF.Exp, bias=negb[:, hd:hd + 1], scale=-1.0)
                ACT(spb, spb, AF.Ln, bias=1.0, scale=1.0)

            def scan(lo, hi):
                P.op("dve", lambda e: e.tensor_tensor_scan(out=E2.ap[:, lo:hi], data0=ones512.ap[:, lo:hi],
                                                           data1=spb.ap[:, lo:hi], initial=0.0,
                                                           op0=ALU.mult, op1=ALU.add),
                     reads=_k(spb, ones512), writes=_k(E2))

            def kt_and_v(h, pp, Wv, X=None):
                X = xnT if X is None else X
                for tc in range(4):
                    ps = group_tm(Wv, X, tc)
                    ACOPY(vb[pp][:, tc, :], ps)
                b = nbank()
                for tc in range(4):
                    for dkc in range(2):
                        TR(bankb[b][:, (tc * 2 + dkc) * 128:(tc * 2 + dkc + 1) * 128],
                           kteT[:, dkc, tc * 128:(tc + 1) * 128])
                P.op("act", lambda e: e.copy(out=kte[pp].ap, in_=bankb[b].ap.rearrange("p (a b) -> p a b", a=4)),
                     reads=_k(bankb[b]), writes=_k(kte[pp]))

            def prefix_tile(pt, last):
                X = xnT if pt % 2 == 0 else xnTalt
                Xn = xnT if (pt + 1) % 2 == 0 or last else xnTalt
                nrow0 = MROW0 if last else XROW0 + (pt + 1) * TT
                gk_low(X)
                for h in range(4):
                    s0_load(nrow0, h, h % 2)
                    if h >= 1:
                        s0_tr(h - 1, (h - 1) % 2, Xn, 0)
                    Wk = wblock("in", 0, NKC, C_K + h * 256, 256)
                    for dkc in range(2):
                        hd = h * 2 + dkc
                        decay_common(hd)
                        scan(0, TT)
                        TS(nbl[:, 0:1], E2[:, TT - 1:TT], -1.0 / 16, ALU.mult)
                        TT_(LD[:, hd:hd + 1], LD[:, hd:hd + 1], nbl[:, 0:1], ALU.add)
                        ACT(E3, E2, AF.Exp, bias=nbl[:, 0:1], scale=1.0 / 16)
                        ACT(dl[hd][:, 0:1], nbl[:, 0:1], AF.Exp)
                        ps = group_fm(Wk, dkc, X=X)
                        TT_(kteT[:, dkc, :], ps, E3, ALU.mult)
                    Wv = wblock("in", 0, NKC, C_V + h * 512, 512)
                    kt_and_v(h, 0, Wv, X)
                    for dkc in range(2):
                        hd = h * 2 + dkc
                        b = nbank()
                        for tc in range(4):
                            MM(bankv[b], kte[0][:, tc, dkc * 128:(dkc + 1) * 128], vb[0][:, tc, :], tc == 0, tc == 3)
                        STT(Sv[hd], Sv[hd], dl[hd][:, 0:1], bankv[b], ALU.mult, ALU.add)
                s0_tr(3, 1, Xn, 0)

            def qkv(h, pp):
                Wqk = wblock("in", 0, NKC, C_Q + h * 256, 256, C_K + h * 256, 256)
                for dkc in range(2):
                    hd = h * 2 + dkc
                    decay_common(hd)
                    for tc in range(4):
                        scan(tc * 128, (tc + 1) * 128)
                    ACT(E1, E2, AF.Exp, scale=-1.0 / 16)
                    TS(nbl, V(E2.ap.rearrange("p (a b) -> p a b", a=4)[:, :, 127], E2.keys), -1.0 / 16, ALU.mult)
                    for tc in range(4):
                        ACT(E3[:, tc * 128:(tc + 1) * 128], E2[:, tc * 128:(tc + 1) * 128], AF.Exp,
                            bias=nbl[:, tc:tc + 1], scale=1.0 / 16)
                    ACT(E2, E2, AF.Exp, scale=1.0 / 16)
                    ACT(dl[hd], nbl, AF.Exp)
                    ps = group_fm(Wqk, dkc)
                    STT(qd[pp][:, dkc, :], ps, 1.0 / 16, E1, ALU.mult, ALU.mult)
                    ps = group_fm(Wqk, 2 + dkc)
                    TT_(kd[pp][:, dkc, :], ps, E2, ALU.mult)
                    TT_(kteT[:, dkc, :], ps, E3, ALU.mult)
                Wv = wblock("in", 0, NKC, C_V + h * 512, 512)
                kt_and_v(h, pp, Wv)

            def gla_A(h, pp, tc):
                tsl = slice(tc * 128, (tc + 1) * 128)
                b = nbank()
                sc = bankv[b][:, 0:128]
                for dkc in range(2):
                    MM(sc, kd[pp][:, dkc, tsl], qd[pp][:, dkc, tsl], dkc == 0, dkc == 1)
                TT_(scT[tc % 2], sc, maskT, ALU.mult)

            def gla_B(h, pp, tc):
                tsl = slice(tc * 128, (tc + 1) * 128)
                s_ = scT[tc % 2]
                b = nbank()
                for dvc in range(4):
                    dsl = slice(dvc * 128, (dvc + 1) * 128)
                    o = bankv[b][:, dsl]
                    MM(o, vb[pp][:, tc, dsl], s_, True, False)
                    MM(o, Sbf[h * 2][:, dsl], qd[pp][:, 0, tsl], False, False)
                    MM(o, Sbf[h * 2 + 1][:, dsl], qd[pp][:, 1, tsl], False, True)
                P.op("act", lambda e, b=b, tsl=tsl: e.copy(out=o_sb.ap[:, :, tsl],
                                                           in_=bankv[b].ap.rearrange("p (a b) -> p a b", a=4)),
                     reads=_k(bankv[b]), writes=_k(o_sb))
                for dkc in range(2):
                    hd = h * 2 + dkc
                    b = nbank()
                    MM(bankv[b], kte[pp][:, tc, dkc * 128:(dkc + 1) * 128], vb[pp][:, tc, :], True, True)
                    STT(Sv[hd], Sv[hd], dl[hd][:, tc:tc + 1], bankv[b], ALU.mult, ALU.add)
                    ACOPY(Sbf[hd], Sv[hd])

            def gla_norm(h):
                b = nbank()
                for dvc in range(4):
                    ACT(sq[dvc % 2], o_cc[dvc], AF.Square)
                    MM(bankv[b], ones128, sq[dvc % 2], dvc == 0, dvc == 3)
                ACT(lnv, bankv[b], AF.Ln, bias=EPS, scale=1.0 / 512)
                ACT(lnv, lnv, AF.Exp, scale=-0.5)
                for dvc in range(4):
                    STT(o_cc[dvc], o_cc[dvc], vcol(80 + dvc), lnv, ALU.mult, ALU.mult)

            def gla(h, pp):
                for tc in range(4):
                    gla_A(h, pp, tc)
                    gla_B(h, pp, tc)
                gla_norm(h)

            def bmix(h, first_tile, hook=None):
                cnt = [0]

                def tick():
                    cnt[0] += 1
                    if hook is not None:
                        hook(cnt[0])
                Wgc = wblock("in", 0, NKC, C_GC + h * 512, 512)
                for cc in range(4):
                    if first_tile:
                        ps, psh = group_fm(Wgc, cc, extra_rhs=xnh)
                        ACOPY(gch[cc], psh)
                    else:
                        ps = group_fm(Wgc, cc)
                    ACOPY(gc[cc], ps)
                    tick()
                Wxc = wblock("in", 0, NKC, C_XC + h * 512, 512)
                for cc in range(4):
                    g = h * 4 + cc
                    u = ubuf[cc % 2]
                    if first_tile:
                        ps, psh = group_fm(Wxc, cc, extra_rhs=xnh)
                        TT_(uh[g], psh, gch[cc], ALU.mult)
                    else:
                        ps = group_fm(Wxc, cc)
                    DCOPY(u[:, 0:2], uh[g])
                    TT_(u[:, 2:514], ps, gc[cc], ALU.mult)
                    TS(cv[cc], u[:, 2:514], vcol(32 + g * 3 + 2), ALU.mult)
                    STT(cv[cc], u[:, 1:513], vcol(32 + g * 3 + 1), cv[cc], ALU.mult, ALU.add)
                    STT(cv[cc], u[:, 0:512], vcol(32 + g * 3 + 0), cv[cc], ALU.mult, ALU.add)
                    DCOPY(uh[g], u[:, 512:514])
                    tick()
                Wgb = wblock("in", 0, NKC, C_GB + h * 512, 512)
                for cc in range(4):
                    ps = group_fm(Wgb, cc)
                    TT_(cv[cc], ps, cv[cc], ALU.mult)
                    tick()

            def bmix_gate(h):
                Wgo = wblock("in", 0, NKC, C_GO + h * 512, 512)
                for cc in range(4):
                    ps = group_fm(Wgo, cc)
                    ACT(sg, ps, AF.Sigmoid)
                    TT_(tmp, ps, sg, ALU.mult)
                    TT_(o_cc[cc], tmp, o_cc[cc], ALU.mult)
                Wma = wblock("in", 0, NKC, C_MA + h * 512, 512)
                for cc in range(4):
                    ps = group_fm(Wma, cc)
                    ACT(sg, ps, AF.Sigmoid)
                    TT_(o_cc[cc], o_cc[cc], sg, ALU.mult)
                Wmb = wblock("in", 0, NKC, C_MB + h * 512, 512)
                for cc in range(4):
                    g = h * 4 + cc
                    ps = group_fm(Wmb, cc)
                    ACT(sb2, ps, AF.Sigmoid)
                    TT_(cv[cc], cv[cc], sb2, ALU.mult)
                    TT_(mgT[g], cv[cc], o_cc[cc], ALU.add)

            def exchange_start():
                P.op("sp", lambda e: e.dma_start(out=lsrc[0].ap(), in_=S_t[:, 0:4, :].rearrange("p a b -> p (a b)")),
                     reads=[("S", i) for i in range(4)], writes=["lsrc0"], dma_key="ls0")
                P.op("sp", lambda e: e.dma_start(out=lsrc[1].ap(), in_=S_t[:, 4:8, :].rearrange("p a b -> p (a b)")),
                     reads=[("S", i) for i in range(4, 8)], writes=["lsrc1"], dma_key="ls1")
                P.op("sp", lambda e: e.dma_start(out=lsrc[2].ap(), in_=LD_t[:]), reads=["LD"], writes=["lsrc2"],
                     dma_key="ls2")
                prior = [o.id for o in P.by_eng["pool"] if o.dma][-3:]
                ccs = []
                for i in range(3):
                    ccs.append(P.op("pool", lambda e, i=i: e.collective_compute(
                        "AllGather", ALU.bypass, replica_groups=[[0, 1, 2, 3], [4, 5, 6, 7]],
                        ins=[lsrc[i].ap()], outs=[lall[i].ap()]),
                        reads=["lsrc%d" % i], writes=["lall%d" % i], dma_key="cc%d" % i, inc=1,
                        extra_deps=prior))

            def combine():
                for i in range(8):
                    P.op("dve", lambda e, i=i: e.memset(S_t[:, i, :], 0.0), writes=[("S", i)])
                P.op("sp", lambda e: e.dma_start(out=LDall_t[:], in_=lall[2].ap().rearrange("(r p) c -> p r c", p=128)),
                     reads=["lall2"], writes=["LDall"], dma_key="ld_all")
                ACT(LDall, LDall, AF.Exp)
                for j in range(3):
                    for half in range(2):
                        Lb = Lbuf[half]
                        P.op("sp", lambda e, j=j, Lb=Lb, half=half: e.dma_start(
                            out=Lb.ap, in_=lall[half].ap()[j * 128:(j + 1) * 128, :].rearrange("p (a b) -> p a b", a=4)),
                            reads=["lall%d" % half], writes=_k(Lb), dma_key=("lb", half))
                    TS(Deff, LDall[:, j, 0:8], -1.0, ALU.add)
                    STT(Deff, Deff, am[:, j:j + 1], one8, ALU.mult, ALU.add)
                    for hd in range(8):
                        TS(Sv[hd], Sv[hd], Deff[:, hd:hd + 1], ALU.mult)
                        STT(Sv[hd], Lbuf[hd // 4][:, hd % 4, :], am[:, j:j + 1], Sv[hd], ALU.mult, ALU.add)
                for i in range(8):
                    ACOPY(Sbf[i], Sv[i])

            def main_tile(mt):
                row0 = MROW0 + mt * TT
                gk_low()
                qkv(0, 0)
                for h in range(4):
                    if h + 1 < 4:
                        qkv(h + 1, (h + 1) % 2)
                    pp = h % 2
                    if USE_CC and mt == 0 and h == 0:
                        bmix(h, True)
                        combine()
                        gla(h, pp)
                    else:
                        gla_A(h, pp, 0)

                        def hook(n, h=h, pp=pp):
                            if n % 3 == 0:
                                tc = n // 3 - 1
                                gla_B(h, pp, tc)
                                if tc + 1 < 4:
                                    gla_A(h, pp, tc + 1)
                        bmix(h, mt == 0, hook)
                        gla_norm(h)
                    if h == 3:
                        load_x(row0)
                    bmix_gate(h)
                for db in range(4):
                    Wo = wblock("out", 0, NKC, db * 512, 512)
                    for tc in range(4):
                        ps = group_tm(Wo, mgT, tc)
                        dsl = slice(db * 512, (db + 1) * 512)
                        TT_(xt_tc[tc][:, dsl], ps, xt_tc[tc][:, dsl], ALU.add)
                        if db == 3:
                            s0_norm(xt_tc[tc], tc % 2)
                            if tc >= 1:
                                s0_tr(tc - 1, (tc - 1) % 2, xnT, 16)
                s0_tr(3, 1, xnT, 16)
                for fb in range(11):
                    Wg = wblock("gu", 0, NKC, fb * 512, 512)
                    for cc in range(4):
                        ps = group_fm(Wg, cc)
                        ACT(ft[4 + cc % 2], ps, AF.Sigmoid)
                        TT_(ft[cc], ps, ft[4 + cc % 2], ALU.mult)
                    Wu = wblock("gu", 0, NKC, FF + fb * 512, 512)
                    for cc in range(4):
                        ps = group_fm(Wu, cc)
                        TT_(actT[fb * 4 + cc], ps, ft[cc], ALU.mult)
                for db in range(4):
                    bs = [nbank() for _ in range(4)]
                    subs = [(0, 16), (16, 16), (32, 12)]
                    for (k0, nk) in subs:
                        Wd = wblock("dn", k0, nk, db * 512, 512)
                        for tc in range(4):
                            for j in range(nk):
                                fc = k0 + j
                                MM(bankv[bs[tc]], actT[fc][:, tc * 128:(tc + 1) * 128], Wd[:, j, :],
                                   fc == 0, fc == NFC - 1)
                    for tc in range(4):
                        dsl = slice(db * 512, (db + 1) * 512)
                        TT_(xt_tc[tc][:, dsl], bankv[bs[tc]], xt_tc[tc][:, dsl], ALU.add)
                    if mt + 1 < NMAIN:
                        s0_load(row0 + TT, db, db % 2)
                        if db >= 1:
                            s0_tr(db - 1, (db - 1) % 2, xnT, 0)
                if mt + 1 < NMAIN:
                    s0_tr(3, 1, xnT, 0)
                final_stats()
                for tc in range(4):
                    STT(xt_tc[tc], xt_tc[tc], rs[:, tc:tc + 1], fnwb, ALU.mult, ALU.mult)
                P.op("sp", lambda e: e.dma_start(out=out_d[mt * TT:(mt + 1) * TT, :].rearrange("(tc p) d -> p tc d", p=128),
                                                 in_=xt.ap),
                     reads=_k(xt), dma_key=("o", mt))

            s0_load(0, 0, 0)
            s0_tr(0, 0, xnTalt, 0)
            DCOPY(xnh, xnTalt_halo0)
            for tc in range(4):
                s0_load(XROW0, tc, (tc + 1) % 2)
                s0_tr(tc, (tc + 1) % 2, xnT, 0)
            for pt in range(NPRE):
                prefix_tile(pt, pt == NPRE - 1)
            if USE_CC:
                exchange_start()
            else:
                for i in range(8):
                    ACOPY(Sbf[i], Sv[i])
            for mt in range(NMAIN):
                main_tile(mt)
            P.op("sp", lambda e: None, extra_deps=[o.id for o in P.ops if o.dma and o.eng == "sp"])
            return rec

        plan = emit_all(Prog(nc), None)
        P = Prog(nc)
        emit_all(P, plan)
        P.emit(st)
        print("ops", len(P.ops), "sems", P.n_sems, "max_tick", P.max_tick, "wblocks", len(plan))
    return nc


def _consts():
    c = np.zeros((128, 384), np.float32)
    c[:, 0:128] = np.eye(128, dtype=np.float32)
    j = np.arange(128)[:, None]
    i = np.arange(128)[None, :]
    c[:, 128:256] = (j <= i).astype(np.float32)
    c[:, 256:384] = 1.0
    return c


def kernel(x, mix_norm_w, w_in, w_gk_up, b_gk_up, gla_norm_w, conv_w, w_out,
           ffn_norm_w, w_gate_up, w_down, final_norm_w):
    x = np.asarray(x, np.float32)
    B, S, _ = x.shape
    vecs = np.zeros((128, 92), np.float32)
    vecs[:, 0:16] = np.asarray(mix_norm_w, np.float32)[0].reshape(16, 128).T
    vecs[:, 16:32] = np.asarray(ffn_norm_w, np.float32)[0].reshape(16, 128).T
    cw = np.asarray(conv_w, np.float32)[0]
    vecs[:, 32:80] = cw.reshape(3, 16, 128).transpose(2, 1, 0).reshape(128, 48)
    vecs[:, 80:84] = np.asarray(gla_norm_w, np.float32)[0].reshape(4, 128).T
    vecs[:, 84:92] = np.asarray(b_gk_up, np.float32)[0].reshape(8, 128).T
    fnwb = np.ascontiguousarray(np.broadcast_to(np.asarray(final_norm_w, np.float32)[None, :], (128, D)))
    shared = {
        "w_in": np.ascontiguousarray(np.asarray(w_in, np.float32)[0]),
        "w_out": np.ascontiguousarray(np.asarray(w_out, np.float32)[0]),
        "w_gu": np.ascontiguousarray(np.asarray(w_gate_up, np.float32)[0]),
        "w_dn": np.ascontiguousarray(np.asarray(w_down, np.float32)[0]),
        "wgk": np.ascontiguousarray(np.asarray(w_gk_up, np.float32)[0]),
        "vecs": vecs, "fnwb": fnwb, "cst": _consts(),
    }
    in_maps = []
    own = NMAIN * TT
    for c in range(8):
        b, p = c // 4, c % 4
        xs = np.zeros((MROW0 + own, D), np.float32)
        start = p * own
        if start > 0:
            xs[0:XROW0] = x[b, start - XROW0:start]
            if not USE_CC:
                xs[MROW0 - start:MROW0] = x[b, 0:start]
        xs[MROW0:] = x[b, start:start + own]
        am = np.zeros((128, 4), np.float32)
        am[:, 0:p] = 1.0
        m = dict(shared)
        m["xs"] = xs
        m["am"] = am
        in_maps.append(m)
    nc = build_nc()
    res = run_bass_kernel_spmd(nc, in_maps, core_ids=list(range(8)))
    out = np.zeros((B, S, D), np.float32)
    for c in range(8):
        b, p = c // 4, c % 4
        out[b, p * own:(p + 1) * own] = res.results[c]["out"]
    return out
```

```python
import numpy as np
from contextlib import ExitStack
import concourse.bass as bass
import concourse.mybir as mybir
from concourse.bass_utils import run_bass_kernel_spmd
from concourse.alu_op_type import AluOpType as ALU

F32 = mybir.dt.float32
BF16 = mybir.dt.bfloat16
AF = mybir.ActivationFunctionType

D = 2048
NKC = 16
TT = 512
NTC = 4
FF = 5632
NFC = 44
USE_CC = True
NPRE = 4 if USE_CC else 12
XROW0 = 128
MROW0 = XROW0 if USE_CC else XROW0 + NPRE * TT
NMAIN = 4
EPS = 1e-6
C_Q, C_K, C_V, C_GO, C_GKL, C_GB, C_GC, C_XC, C_MA, C_MB = 0, 1024, 2048, 4096, 6144, 6160, 8208, 10256, 12304, 14352
ENGS = ("pe", "act", "dve", "pool", "sp")
ARENAS = ("XT", "FA")


class Op:
    __slots__ = ("id", "eng", "fn", "dma", "deps", "sem", "tick", "signal", "inc")

    def __init__(self, id, eng, fn, dma):
        self.id = id
        self.eng = eng
        self.fn = fn
        self.dma = dma
        self.deps = set()
        self.sem = None
        self.tick = None
        self.signal = False
        self.inc = 16 if dma else 1


class Prog:
    def __init__(self, nc):
        self.nc = nc
        self.ops = []
        self.by_eng = {e: [] for e in ENGS}
        self.last_writer = {}
        self.readers = {}
        self.arena_keys = {a: [] for a in ARENAS}
        self.dma_keys = {}

    def _expand(self, k):
        if isinstance(k, tuple) and k and k[0] in ARENAS:
            lst = self.arena_keys[k[0]]
            if k not in self.last_writer and k not in self.readers:
                lst.append(k)
                self.readers[k] = []
            return [o for o in lst if o[1] < k[2] and k[1] < o[2]]
        return [k]

    def op(self, eng, fn, reads=(), writes=(), dma_key=None, extra_deps=(), inc=None):
        o = Op(len(self.ops), eng, fn, dma_key is not None)
        deps = o.deps
        for r in reads:
            for k in self._expand(r):
                w = self.last_writer.get(k)
                if w is not None:
                    deps.add(w)
        for w_ in writes:
            for k in self._expand(w_):
                w = self.last_writer.get(k)
                if w is not None:
                    deps.add(w)
                rl = self.readers.get(k)
                if rl:
                    deps.update(rl)
        deps.update(extra_deps)
        deps.discard(o.id)
        for r in reads:
            self.readers.setdefault(r, []).append(o.id)
        for w_ in writes:
            for k in self._expand(w_):
                self.last_writer[k] = o.id
                self.readers[k] = []
        if dma_key is not None:
            c = self.dma_keys.get(dma_key, 0) + 1
            self.dma_keys[dma_key] = c
            if inc is not None:
                o.inc = inc
            o.sem = ("dma", dma_key)
            o.tick = o.inc * c
            o.signal = True
        self.ops.append(o)
        self.by_eng[eng].append(o)
        return o.id

    def emit(self, stack):
        nc = self.nc
        ops = self.ops
        for o in ops:
            nd = set()
            for d in o.deps:
                p = ops[d]
                if (not p.dma) and p.eng == o.eng and o.eng == "pe":
                    continue
                nd.add(d)
            o.deps = nd
            for d in nd:
                ops[d].signal = True
        cnt = {e: 0 for e in ENGS}
        for e in ENGS:
            for o in self.by_eng[e]:
                if o.dma:
                    continue
                if o.signal:
                    cnt[e] += 1
                    o.sem = ("eng", e)
                    o.tick = cnt[e]
        sems = {}
        for e in ENGS:
            if cnt[e] > 0:
                sems[("eng", e)] = stack.enter_context(nc.semaphore("prog_" + e))
        for i, k in enumerate(self.dma_keys):
            sems[("dma", k)] = stack.enter_context(nc.semaphore("dma_%d" % i))
        self.n_sems = len(sems)
        self.max_tick = max([cnt[e] for e in ENGS] + [16 * c for c in self.dma_keys.values()] + [0])
        block = stack.enter_context(nc.Block())

        def run(eng_name, eng):
            waited = {}
            for o in self.by_eng[eng_name]:
                need = {}
                for d in o.deps:
                    p = ops[d]
                    if need.get(p.sem, 0) < p.tick:
                        need[p.sem] = p.tick
                for s, v in need.items():
                    if waited.get(s, 0) >= v:
                        continue
                    eng.wait_ge(sems[s], v)
                    waited[s] = v
                inst = o.fn(eng)
                if o.signal:
                    inst.then_inc(sems[o.sem], o.inc)

        @block.tensor
        def _(e):
            run("pe", e)

        @block.scalar
        def _(e):
            run("act", e)

        @block.vector
        def _(e):
            run("dve", e)

        @block.gpsimd
        def _(e):
            run("pool", e)

        @block.sync
        def _(e):
            run("sp", e)


class V:
    __slots__ = ("ap", "keys")

    def __init__(self, ap, keys):
        self.ap = ap
        self.keys = tuple(keys)

    def __getitem__(self, idx):
        return V(self.ap[idx], self.keys)


def _k(*vs):
    out = []
    for v in vs:
        if isinstance(v, V):
            out.extend(v.keys)
    return out


def _a(v):
    return v.ap if isinstance(v, V) else v


def build_nc():
    nc = bass.Bass("TRN2", target_bir_lowering=False)
    xs = nc.dram_tensor("xs", [MROW0 + NMAIN * TT, D], F32, kind="ExternalInput").ap()
    am_d = nc.dram_tensor("am", [128, 4], F32, kind="ExternalInput").ap()
    lsrc = [nc.dram_tensor("lsrc%d" % i, [128, 2048], F32) for i in range(2)] + [nc.dram_tensor("lsrc2", [128, 32], F32)]
    lall = [nc.dram_tensor("lall%d" % i, [512, 2048], F32) for i in range(2)] + [nc.dram_tensor("lall2", [512, 32], F32)]
    w_in = nc.dram_tensor("w_in", [D, 16400], F32, kind="ExternalInput").ap()
    w_out = nc.dram_tensor("w_out", [D, D], F32, kind="ExternalInput").ap()
    w_gu = nc.dram_tensor("w_gu", [D, 2 * FF], F32, kind="ExternalInput").ap()
    w_dn = nc.dram_tensor("w_dn", [FF, D], F32, kind="ExternalInput").ap()
    wgk_d = nc.dram_tensor("wgk", [16, 1024], F32, kind="ExternalInput").ap()
    vecs_d = nc.dram_tensor("vecs", [128, 92], F32, kind="ExternalInput").ap()
    fnwb_d = nc.dram_tensor("fnwb", [128, D], F32, kind="ExternalInput").ap()
    cst_d = nc.dram_tensor("cst", [128, 384], F32, kind="ExternalInput").ap()
    out_d = nc.dram_tensor("out", [NMAIN * TT, D], F32, kind="ExternalOutput").ap()

    with ExitStack() as st:
        def sb(name, shape, dt):
            return st.enter_context(nc.sbuf_tensor("s_" + name, shape, dt))

        xnT_t = sb("xnT", [128, NKC, TT], BF16)
        mgT_t = sb("mgT", [128, NKC, TT], BF16)
        wr_t = [sb("wr%d" % i, [128, NKC, 512], BF16) for i in range(3)]
        S_t = sb("S", [128, 8, 512], F32)
        Sbf_t = sb("Sbf", [128, 8, 512], BF16)
        cst_t = sb("cst", [128, 384], F32)
        identb_t = sb("identb", [128, 128], BF16)
        ones_t = sb("ones512", [128, 512], F32)
        vecs_t = sb("vecs", [128, 92], F32)
        negb_t = sb("negb", [128, 8], F32)
        fnwb_t = sb("fnwb", [128, D], F32)
        wgk_t = sb("wgkb", [16, 1024], BF16)
        gkl_t = sb("gkl", [16, TT], BF16)
        wgl_t = sb("wgl", [128, NKC, 16], BF16)
        uh_t = sb("uh", [128, 16, 2], F32)
        xnh_t = sb("xnh", [128, NKC, 2], BF16)
        gch_t = sb("gch", [128, 4, 2], F32)
        ss_t = sb("ss", [128, 4], F32)
        rs_t = sb("rs", [128, 4], F32)
        nbl_t = sb("nbl", [128, 4], F32)
        dl_t = sb("dl", [128, 8, 4], F32)
        am_t = sb("am", [128, 4], F32)
        LD_t = sb("LD", [128, 32], F32)
        LDall_t = sb("LDall", [128, 4, 32], F32)
        Deff_t = sb("Deff", [128, 8], F32)
        one8_t = sb("one8", [128, 8], F32)
        xnp_t = [sb("xnp%d" % i, [128, D], BF16) for i in range(2)]
        ssp_t = sb("ssp", [128, 2], F32)
        rsp_t = sb("rsp", [128, 2], F32)
        XT_t = sb("XT", [128, 8192], F32)
        FA_t = sb("FA", [128, 11264], F32)
        banks = [st.enter_context(nc.psum_tensor("ps%d" % i, [128, 512], F32)) for i in range(8)]

        def arena(name, t, lo, nbytes, dt, pat=None, **kw):
            assert lo % 4 == 0 and nbytes % 4 == 0
            ap = t[:, lo // 4:(lo + nbytes) // 4]
            if dt == BF16:
                ap = ap.bitcast(BF16)
            if pat is not None:
                ap = ap.rearrange(pat, **kw)
            return V(ap, [(name, lo, lo + nbytes)])

        xt = arena("XT", XT_t, 0, 32768, F32, "p (a b) -> p a b", a=4)
        xt_tc = [arena("XT", XT_t, tc * 8192, 8192, F32) for tc in range(4)]
        spb = arena("XT", XT_t, 0, 2048, F32)
        E1 = arena("XT", XT_t, 2048, 2048, F32)
        E2 = arena("XT", XT_t, 4096, 2048, F32)
        E3 = arena("XT", XT_t, 6144, 2048, F32)
        qd = [arena("XT", XT_t, 8192 + i * 2048, 2048, BF16, "p (a b) -> p a b", a=2) for i in range(2)]
        kd = [arena("XT", XT_t, 12288 + i * 2048, 2048, BF16, "p (a b) -> p a b", a=2) for i in range(2)]
        kteT = arena("XT", XT_t, 16384, 2048, BF16, "p (a b) -> p a b", a=2)
        kte = [arena("XT", XT_t, 18432 + i * 2048, 2048, BF16, "p (a b) -> p a b", a=4) for i in range(2)]
        vb = [arena("XT", XT_t, 22528 + i * 4096, 4096, BF16, "p (a b) -> p a b", a=4) for i in range(2)]
        actT = [arena("FA", FA_t, fc * 1024, 1024, BF16) for fc in range(NFC)]
        xn_tm = [arena("FA", FA_t, tc * 4096, 4096, BF16) for tc in range(4)]
        junk = arena("FA", FA_t, 16384, 4096, BF16)
        scT = [arena("FA", FA_t, i * 256, 256, BF16) for i in range(2)]
        o_sb = arena("FA", FA_t, 512, 8192, F32, "p (a b) -> p a b", a=4)
        o_cc = [arena("FA", FA_t, 512 + cc * 2048, 2048, F32) for cc in range(4)]
        sq = [arena("FA", FA_t, 8704 + i * 2048, 2048, F32) for i in range(2)]
        lnv = arena("FA", FA_t, 12800, 2048, F32)
        gc = [arena("FA", FA_t, 14848 + cc * 2048, 2048, F32) for cc in range(4)]
        cv = [arena("FA", FA_t, 23040 + cc * 2048, 2048, F32) for cc in range(4)]
        sg = arena("FA", FA_t, 31232, 2048, F32)
        sb2 = arena("FA", FA_t, 33280, 2048, F32)
        tmp = arena("FA", FA_t, 35328, 2048, F32)
        ubuf = [arena("FA", FA_t, 37376 + i * 2064, 2064, F32) for i in range(2)]

        ft = [V(mgT_t[:, 2 * i:2 * i + 2, :].rearrange("p a b -> p (a b)").bitcast(F32),
                [("mgT", 2 * i), ("mgT", 2 * i + 1)]) for i in range(8)]
        xpc = [V(mgT_t[:, 8 * i:8 * i + 8, :].rearrange("p a b -> p (a b)").bitcast(F32),
                 [("mgT", g) for g in range(8 * i, 8 * i + 8)]) for i in range(2)]
        xnp = [V(xnp_t[i][:], [("xnp", i)]) for i in range(2)]
        ssp = [V(ssp_t[:, i:i + 1], [("ssp", i)]) for i in range(2)]
        rsp = [V(rsp_t[:, i:i + 1], [("rsp", i)]) for i in range(2)]
        xnTalt = [arena("FA", FA_t, kc * 1024, 1024, BF16) for kc in range(NKC)]
        xnTalt_halo = V(FA_t[:, 0:4096].bitcast(BF16).rearrange("p (a b) -> p a b", a=NKC)[:, :, TT - 2:TT],
                        [("FA", 0, 16384)])
        am = V(am_t[:], ["am"])
        LD = V(LD_t[:], ["LD"])
        LDall = V(LDall_t[:], ["LDall"])
        Deff = V(Deff_t[:], ["Deff"])
        one8 = V(one8_t[:], ["one8"])
        Lbuf = [V(mgT_t[:, 8 * i:8 * i + 8, :].rearrange("p a b -> p (a b)").bitcast(F32).rearrange("p (a b) -> p a b", a=4),
                  [("mgT", g) for g in range(8 * i, 8 * i + 8)]) for i in range(2)]
        xnTalt_halo0 = V(FA_t[:, 0:4096].bitcast(BF16).rearrange("p (a b) -> p a b", a=NKC)[:, :, 126:128],
                         [("FA", 0, 16384)])
        xnT = [V(xnT_t[:, kc, :], [("xnT", kc)]) for kc in range(NKC)]
        mgT = [V(mgT_t[:, g, :], [("mgT", g)]) for g in range(NKC)]
        Sv = [V(S_t[:, i, :], [("S", i)]) for i in range(8)]
        Sbf = [V(Sbf_t[:, i, :], [("Sbf", i)]) for i in range(8)]
        ident_f = V(cst_t[:, 0:128], ["cst"])
        maskT = V(cst_t[:, 128:256], ["cst"])
        ones128 = V(cst_t[:, 256:384], ["cst"])
        identb = V(identb_t[:], ["identb"])
        ones512 = V(ones_t[:], ["ones512"])
        vecs = V(vecs_t[:], ["vecs"])
        negb = V(negb_t[:], ["negb"])
        fnwb = V(fnwb_t[:], ["fnwb"])
        wgk = V(wgk_t[:], ["wgk"])
        gkl = V(gkl_t[:], ["gkl"])
        uh = [V(uh_t[:, g, :], [("uh", g)]) for g in range(16)]
        xnh = V(xnh_t[:], ["xnh"])
        gch = [V(gch_t[:, cc, :], [("gch", cc)]) for cc in range(4)]
        ss = V(ss_t[:], ["ss"])
        rs = V(rs_t[:], ["rs"])
        nbl = V(nbl_t[:], ["nbl"])
        dl = [V(dl_t[:, i, :], [("dl", i)]) for i in range(8)]
        bankv = [V(banks[i][:], [("ps", i)]) for i in range(8)]
        bankb = [V(banks[i][:].bitcast(BF16), [("ps", i)]) for i in range(8)]

        def vcol(c):
            return vecs[:, c:c + 1]

        def emit_all(P, plan):
            state = {"bank": 0, "wi": 0, "issued": 0}
            rec = []

            def ACT(out, in_, func, bias=None, scale=None):
                kw = {}
                if bias is not None:
                    kw["bias"] = _a(bias)
                if scale is not None:
                    kw["scale"] = scale
                P.op("act", lambda e: e.activation(out=out.ap, in_=in_.ap, func=func, **kw),
                     reads=_k(in_, bias), writes=_k(out))

            def ACOPY(out, in_):
                P.op("act", lambda e: e.copy(out=out.ap, in_=in_.ap), reads=_k(in_), writes=_k(out))

            def TT_(out, in0, in1, op):
                P.op("dve", lambda e: e.tensor_tensor(out=out.ap, in0=in0.ap, in1=in1.ap, op=op),
                     reads=_k(in0, in1), writes=_k(out))

            def TS(out, in0, s1, op0):
                P.op("dve", lambda e: e.tensor_scalar(out=out.ap, in0=in0.ap, scalar1=_a(s1), scalar2=None, op0=op0),
                     reads=_k(in0, s1), writes=_k(out))

            def STT(out, in0, s, in1, op0, op1):
                P.op("dve", lambda e: e.scalar_tensor_tensor(out=out.ap, in0=in0.ap, scalar=_a(s), in1=in1.ap,
                                                             op0=op0, op1=op1),
                     reads=_k(in0, s, in1), writes=_k(out))

            def DCOPY(out, in_):
                P.op("dve", lambda e: e.tensor_copy(out=out.ap, in_=in_.ap), reads=_k(in_), writes=_k(out))

            def MM(out, lhsT, rhs, start, stop):
                P.op("pe", lambda e: e.matmul(out.ap, lhsT=lhsT.ap, rhs=rhs.ap, start=start, stop=stop),
                     reads=_k(lhsT, rhs), writes=_k(out))

            def TR(out, in_):
                P.op("pe", lambda e: e.transpose(out=out.ap, in_=in_.ap, identity=identb.ap),
                     reads=_k(in_, identb), writes=_k(out))

            def nbank():
                b = state["bank"]
                state["bank"] = (b + 1) % 8
                return b

            def issue(desc, idx):
                src, k0, nk, segs = desc
                slot = idx % 3
                dram = {"in": w_in, "out": w_out, "gu": w_gu, "dn": w_dn}[src]
                off = 0
                prev = list(P.readers.get(("w", slot), ()))
                lw = P.last_writer.get(("w", slot))
                if lw is not None:
                    prev.append(lw)
                for si, (c0, n) in enumerate(segs):
                    view = dram[k0 * 128:(k0 + nk) * 128, c0:c0 + n].rearrange("(kc p) c -> p kc c", p=128)
                    dst = wr_t[slot][:, 0:nk, off:off + n]
                    key = ("w", slot) if si == 0 else ("w2", slot)
                    P.op("pool", lambda e, dst=dst, view=view: e.dma_start(out=dst, in_=view),
                         writes=[key], dma_key=key, extra_deps=prev if si > 0 else ())
                    off += n

            def wblock(src, k0, nk, c0, n, c1=None, n1=0):
                segs = ((c0, n),) if c1 is None else ((c0, n), (c1, n1))
                n = n + n1
                desc = (src, k0, nk, segs)
                i = state["wi"]
                state["wi"] = i + 1
                if plan is None:
                    rec.append(desc)
                else:
                    assert plan[i] == desc, (i, plan[i], desc)
                    while state["issued"] < min(len(plan), i + 3):
                        issue(plan[state["issued"]], state["issued"])
                        state["issued"] += 1
                slot = i % 3
                return V(wr_t[slot][:, 0:nk, 0:n], [("w", slot)] + ([("w2", slot)] if len(segs) > 1 else []))

            def group_fm(W, cc, M=128, N=TT, extra_rhs=None, X=None):
                X = xnT if X is None else X
                b = nbank()
                out = bankv[b][0:M, 0:N]
                for kc in range(NKC):
                    MM(out, W[:, kc, cc * 128:cc * 128 + M], X[kc][:, 0:N], kc == 0, kc == NKC - 1)
                if extra_rhs is not None:
                    b2 = nbank()
                    out2 = bankv[b2][0:M, 0:2]
                    for kc in range(NKC):
                        MM(out2, W[:, kc, cc * 128:cc * 128 + M], extra_rhs[:, kc, :], kc == 0, kc == NKC - 1)
                    return out, out2
                return out

            def group_tm(W, src, tc, ncol=512):
                b = nbank()
                out = bankv[b][:, 0:ncol]
                n = len(src)
                for kc in range(n):
                    MM(out, src[kc][:, tc * 128:(tc + 1) * 128], W[:, kc, 0:ncol], kc == 0, kc == n - 1)
                return out

            if True:
                P.op("sp", lambda e: e.dma_start(out=cst_t[:], in_=cst_d), writes=["cst"], dma_key="c_cst")
                P.op("sp", lambda e: e.dma_start(out=vecs_t[:], in_=vecs_d), writes=["vecs"], dma_key="c_vecs")
                P.op("sp", lambda e: e.dma_start(out=fnwb_t[:], in_=fnwb_d), writes=["fnwb"], dma_key="c_fnwb")
                P.op("pool", lambda e: e.dma_start(out=wgk_t[:], in_=wgk_d), writes=["wgk"], dma_key="c_wgk")
                P.op("pool", lambda e: e.dma_start(out=wgl_t[:],
                                                   in_=w_in[:, C_GKL:C_GKL + 16].rearrange("(kc p) c -> p kc c", p=128)),
                     writes=["wgl"], dma_key="c_wgl")
                DCOPY(identb, ident_f)
                P.op("dve", lambda e: e.memset(ones_t[:], 1.0), writes=["ones512"])
                for i in range(8):
                    P.op("dve", lambda e, i=i: e.memset(S_t[:, i, :], 0.0), writes=[("S", i)])
                TS(negb, vecs[:, 84:92], -1.0, ALU.mult)
                P.op("sp", lambda e: e.dma_start(out=am_t[:], in_=am_d), writes=["am"], dma_key="c_am")
                P.op("dve", lambda e: e.memset(LD_t[:], 0.0), writes=["LD"])
                P.op("dve", lambda e: e.memset(one8_t[:], 1.0), writes=["one8"])

            def load_x(row0):
                P.op("sp", lambda e: e.dma_start(out=xt.ap, in_=xs[row0:row0 + TT, :].rearrange("(tc p) d -> p tc d", p=128)),
                     writes=_k(xt), dma_key="x")

            def norm_to_T(wcol0, dst):
                for tc in range(4):
                    P.op("dve", lambda e, tc=tc: e.scalar_tensor_tensor(
                        out=junk.ap, in0=xt_tc[tc].ap, scalar=1.0, in1=xt_tc[tc].ap,
                        op0=ALU.mult, op1=ALU.mult, accum_out=ss_t[:, tc:tc + 1]),
                        reads=_k(xt_tc[tc]), writes=_k(junk, ss))
                ACT(rs, ss, AF.Ln, bias=EPS, scale=1.0 / D)
                ACT(rs, rs, AF.Exp, scale=-0.5)
                for tc in range(4):
                    TS(xn_tm[tc], xt_tc[tc], rs[:, tc:tc + 1], ALU.mult)
                for kc in range(NKC):
                    b = nbank()
                    for tc in range(4):
                        TR(bankb[b][:, tc * 128:(tc + 1) * 128], xn_tm[tc][:, kc * 128:(kc + 1) * 128])
                    TS(dst[kc], bankb[b][:, 0:TT], vcol(wcol0 + kc), ALU.mult)

            def s0_norm(src, j):
                P.op("dve", lambda e: e.scalar_tensor_tensor(
                    out=xnp[j].ap, in0=src.ap, scalar=1.0, in1=src.ap,
                    op0=ALU.mult, op1=ALU.mult, accum_out=ssp[j].ap),
                    reads=_k(src), writes=_k(xnp[j], ssp[j]))
                ACT(rsp[j], ssp[j], AF.Ln, bias=EPS, scale=1.0 / D)
                ACT(rsp[j], rsp[j], AF.Exp, scale=-0.5)
                TS(xnp[j], src, rsp[j], ALU.mult)

            def s0_load(row0, tc, j):
                r0 = row0 + tc * 128
                P.op("sp", lambda e: e.dma_start(out=xpc[j].ap, in_=xs[r0:r0 + 128, :]),
                     writes=_k(xpc[j]), dma_key=("xp", j))
                s0_norm(xpc[j], j)

            def s0_tr(tc, j, dst, wcol0):
                for kq in range(4):
                    b = nbank()
                    for i in range(4):
                        kc = kq * 4 + i
                        TR(bankb[b][:, i * 128:(i + 1) * 128], xnp[j][:, kc * 128:(kc + 1) * 128])
                    for i in range(4):
                        kc = kq * 4 + i
                        TS(dst[kc][:, tc * 128:(tc + 1) * 128], bankb[b][:, i * 128:(i + 1) * 128],
                           vcol(wcol0 + kc), ALU.mult)

            def final_stats():
                for tc in range(4):
                    P.op("dve", lambda e, tc=tc: e.scalar_tensor_tensor(
                        out=junk.ap, in0=xt_tc[tc].ap, scalar=1.0, in1=xt_tc[tc].ap,
                        op0=ALU.mult, op1=ALU.mult, accum_out=ss_t[:, tc:tc + 1]),
                        reads=_k(xt_tc[tc]), writes=_k(junk, ss))
                ACT(rs, ss, AF.Ln, bias=EPS, scale=1.0 / D)
                ACT(rs, rs, AF.Exp, scale=-0.5)

            def gk_low(X=None):
                W = V(wgl_t[:], ["wgl"])
                ps = group_fm(W, 0, M=16, X=X)
                ACOPY(gkl, ps)

            def decay_common(hd):
                b = nbank()
                ps = bankv[b]
                MM(ps, wgk[:, hd * 128:(hd + 1) * 128], gkl, True, True)
                ACT(spb, ps, AF.Exp, bias=negb[:, hd:hd + 1], scale=-1.0)
                ACT(spb, spb, AF.Ln, bias=1.0, scale=1.0)

            def scan(lo, hi):
                P.op("dve", lambda e: e.tensor_tensor_scan(out=E2.ap[:, lo:hi], data0=ones512.ap[:, lo:hi],
                                                           data1=spb.ap[:, lo:hi], initial=0.0,
                                                           op0=ALU.mult, op1=ALU.add),
                     reads=_k(spb, ones512), writes=_k(E2))

            def kt_and_v(h, pp, Wv, X=None):
                X = xnT if X is None else X
                for tc in range(4):
                    ps = group_tm(Wv, X, tc)
                    ACOPY(vb[pp][:, tc, :], ps)
                b = nbank()
                for tc in range(4):
                    for dkc in range(2):
                        TR(bankb[b][:, (tc * 2 + dkc) * 128:(tc * 2 + dkc + 1) * 128],
                           kteT[:, dkc, tc * 128:(tc + 1) * 128])
                P.op("act", lambda e: e.copy(out=kte[pp].ap, in_=bankb[b].ap.rearrange("p (a b) -> p a b", a=4)),
                     reads=_k(bankb[b]), writes=_k(kte[pp]))

            def prefix_tile(pt, last):
                X = xnT if pt % 2 == 0 else xnTalt
                Xn = xnT if (pt + 1) % 2 == 0 or last else xnTalt
                nrow0 = MROW0 if last else XROW0 + (pt + 1) * TT
                gk_low(X)
                for h in range(4):
                    s0_load(nrow0, h, h % 2)
                    if h >= 1:
                        s0_tr(h - 1, (h - 1) % 2, Xn, 0)
                    Wk = wblock("in", 0, NKC, C_K + h * 256, 256)
                    for dkc in range(2):
                        hd = h * 2 + dkc
                        decay_common(hd)
                        scan(0, TT)
                        TS(nbl[:, 0:1], E2[:, TT - 1:TT], -1.0 / 16, ALU.mult)
                        TT_(LD[:, hd:hd + 1], LD[:, hd:hd + 1], nbl[:, 0:1], ALU.add)
                        ACT(E3, E2, AF.Exp, bias=nbl[:, 0:1], scale=1.0 / 16)
                        ACT(dl[hd][:, 0:1], nbl[:, 0:1], AF.Exp)
                        ps = group_fm(Wk, dkc, X=X)
                        TT_(kteT[:, dkc, :], ps, E3, ALU.mult)
                    Wv = wblock("in", 0, NKC, C_V + h * 512, 512)
                    kt_and_v(h, 0, Wv, X)
                    for dkc in range(2):
                        hd = h * 2 + dkc
                        b = nbank()
                        for tc in range(4):
                            MM(bankv[b], kte[0][:, tc, dkc * 128:(dkc + 1) * 128], vb[0][:, tc, :], tc == 0, tc == 3)
                        STT(Sv[hd], Sv[hd], dl[hd][:, 0:1], bankv[b], ALU.mult, ALU.add)
                s0_tr(3, 1, Xn, 0)

            def qkv(h, pp):
                Wqk = wblock("in", 0, NKC, C_Q + h * 256, 256, C_K + h * 256, 256)
                for dkc in range(2):
                    hd = h * 2 + dkc
                    decay_common(hd)
                    for tc in range(4):
                        scan(tc * 128, (tc + 1) * 128)
                    ACT(E1, E2, AF.Exp, scale=-1.0 / 16)
                    TS(nbl, V(E2.ap.rearrange("p (a b) -> p a b", a=4)[:, :, 127], E2.keys), -1.0 / 16, ALU.mult)
                    for tc in range(4):
                        ACT(E3[:, tc * 128:(tc + 1) * 128], E2[:, tc * 128:(tc + 1) * 128], AF.Exp,
                            bias=nbl[:, tc:tc + 1], scale=1.0 / 16)
                    ACT(E2, E2, AF.Exp, scale=1.0 / 16)
                    ACT(dl[hd], nbl, AF.Exp)
                    ps = group_fm(Wqk, dkc)
                    STT(qd[pp][:, dkc, :], ps, 1.0 / 16, E1, ALU.mult, ALU.mult)
                    ps = group_fm(Wqk, 2 + dkc)
                    TT_(kd[pp][:, dkc, :], ps, E2, ALU.mult)
                    TT_(kteT[:, dkc, :], ps, E3, ALU.mult)
                Wv = wblock("in", 0, NKC, C_V + h * 512, 512)
                kt_and_v(h, pp, Wv)

            def gla_A(h, pp, tc):
                tsl = slice(tc * 128, (tc + 1) * 128)
                b = nbank()
                sc = bankv[b][:, 0:128]
                for dkc in range(2):
                    MM(sc, kd[pp][:, dkc, tsl], qd[pp][:, dkc, tsl], dkc == 0, dkc == 1)
                TT_(scT[tc % 2], sc, maskT, ALU.mult)

            def gla_B(h, pp, tc):
                tsl = slice(tc * 128, (tc + 1) * 128)
                s_ = scT[tc % 2]
                b = nbank()
                for dvc in range(4):
                    dsl = slice(dvc * 128, (dvc + 1) * 128)
                    o = bankv[b][:, dsl]
                    MM(o, vb[pp][:, tc, dsl], s_, True, False)
                    MM(o, Sbf[h * 2][:, dsl], qd[pp][:, 0, tsl], False, False)
                    MM(o, Sbf[h * 2 + 1][:, dsl], qd[pp][:, 1, tsl], False, True)
                P.op("act", lambda e, b=b, tsl=tsl: e.copy(out=o_sb.ap[:, :, tsl],
                                                           in_=bankv[b].ap.rearrange("p (a b) -> p a b", a=4)),
                     reads=_k(bankv[b]), writes=_k(o_sb))
                for dkc in range(2):
                    hd = h * 2 + dkc
                    b = nbank()
                    MM(bankv[b], kte[pp][:, tc, dkc * 128:(dkc + 1) * 128], vb[pp][:, tc, :], True, True)
                    STT(Sv[hd], Sv[hd], dl[hd][:, tc:tc + 1], bankv[b], ALU.mult, ALU.add)
                    ACOPY(Sbf[hd], Sv[hd])

            def gla_norm(h):
                b = nbank()
                for dvc in range(4):
                    ACT(sq[dvc % 2], o_cc[dvc], AF.Square)
                    MM(bankv[b], ones128, sq[dvc % 2], dvc == 0, dvc == 3)
                ACT(lnv, bankv[b], AF.Ln, bias=EPS, scale=1.0 / 512)
                ACT(lnv, lnv, AF.Exp, scale=-0.5)
                for dvc in range(4):
                    STT(o_cc[dvc], o_cc[dvc], vcol(80 + dvc), lnv, ALU.mult, ALU.mult)

            def gla(h, pp):
                for tc in range(4):
                    gla_A(h, pp, tc)
                    gla_B(h, pp, tc)
                gla_norm(h)

            def bmix(h, first_tile, hook=None):
                cnt = [0]

                def tick():
                    cnt[0] += 1
                    if hook is not None:
                        hook(cnt[0])
                Wgc = wblock("in", 0, NKC, C_GC + h * 512, 512)
                for cc in range(4):
                    if first_tile:
                        ps, psh = group_fm(Wgc, cc, extra_rhs=xnh)
                        ACOPY(gch[cc], psh)
                    else:
                        ps = group_fm(Wgc, cc)
                    ACOPY(gc[cc], ps)
                    tick()
                Wxc = wblock("in", 0, NKC, C_XC + h * 512, 512)
                for cc in range(4):
                    g = h * 4 + cc
                    u = ubuf[cc % 2]
                    if first_tile:
                        ps, psh = group_fm(Wxc, cc, extra_rhs=xnh)
                        TT_(uh[g], psh, gch[cc], ALU.mult)
                    else:
                        ps = group_fm(Wxc, cc)
                    DCOPY(u[:, 0:2], uh[g])
                    TT_(u[:, 2:514], ps, gc[cc], ALU.mult)
                    TS(cv[cc], u[:, 2:514], vcol(32 + g * 3 + 2), ALU.mult)
                    STT(cv[cc], u[:, 1:513], vcol(32 + g * 3 + 1), cv[cc], ALU.mult, ALU.add)
                    STT(cv[cc], u[:, 0:512], vcol(32 + g * 3 + 0), cv[cc], ALU.mult, ALU.add)
                    DCOPY(uh[g], u[:, 512:514])
                    tick()
                Wgb = wblock("in", 0, NKC, C_GB + h * 512, 512)
                for cc in range(4):
                    ps = group_fm(Wgb, cc)
                    TT_(cv[cc], ps, cv[cc], ALU.mult)
                    tick()

            def bmix_gate(h):
                Wgo = wblock("in", 0, NKC, C_GO + h * 512, 512)
                for cc in range(4):
                    ps = group_fm(Wgo, cc)
                    ACT(sg, ps, AF.Sigmoid)
                    TT_(tmp, ps, sg, ALU.mult)
                    TT_(o_cc[cc], tmp, o_cc[cc], ALU.mult)
                Wma = wblock("in", 0, NKC, C_MA + h * 512, 512)
                for cc in range(4):
                    ps = group_fm(Wma, cc)
                    ACT(sg, ps, AF.Sigmoid)
                    TT_(o_cc[cc], o_cc[cc], sg, ALU.mult)
                Wmb = wblock("in", 0, NKC, C_MB + h * 512, 512)
                for cc in range(4):
                    g = h * 4 + cc
                    ps = group_fm(Wmb, cc)
                    ACT(sb2, ps, AF.Sigmoid)
                    TT_(cv[cc], cv[cc], sb2, ALU.mult)
                    TT_(mgT[g], cv[cc], o_cc[cc], ALU.add)

            def exchange_start():
                P.op("sp", lambda e: e.dma_start(out=lsrc[0].ap(), in_=S_t[:, 0:4, :].rearrange("p a b -> p (a b)")),
                     reads=[("S", i) for i in range(4)], writes=["lsrc0"], dma_key="ls0")
                P.op("sp", lambda e: e.dma_start(out=lsrc[1].ap(), in_=S_t[:, 4:8, :].rearrange("p a b -> p (a b)")),
                     reads=[("S", i) for i in range(4, 8)], writes=["lsrc1"], dma_key="ls1")
                P.op("sp", lambda e: e.dma_start(out=lsrc[2].ap(), in_=LD_t[:]), reads=["LD"], writes=["lsrc2"],
                     dma_key="ls2")
                prior = [o.id for o in P.by_eng["pool"] if o.dma][-3:]
                ccs = []
                for i in range(3):
                    ccs.append(P.op("pool", lambda e, i=i: e.collective_compute(
                        "AllGather", ALU.bypass, replica_groups=[[0, 1, 2, 3], [4, 5, 6, 7]],
                        ins=[lsrc[i].ap()], outs=[lall[i].ap()]),
                        reads=["lsrc%d" % i], writes=["lall%d" % i], dma_key="cc%d" % i, inc=1,
                        extra_deps=prior))

            def combine():
                for i in range(8):
                    P.op("dve", lambda e, i=i: e.memset(S_t[:, i, :], 0.0), writes=[("S", i)])
                P.op("sp", lambda e: e.dma_start(out=LDall_t[:], in_=lall[2].ap().rearrange("(r p) c -> p r c", p=128)),
                     reads=["lall2"], writes=["LDall"], dma_key="ld_all")
                ACT(LDall, LDall, AF.Exp)
                for j in range(3):
                    for half in range(2):
                        Lb = Lbuf[half]
                        P.op("sp", lambda e, j=j, Lb=Lb, half=half: e.dma_start(
                            out=Lb.ap, in_=lall[half].ap()[j * 128:(j + 1) * 128, :].rearrange("p (a b) -> p a b", a=4)),
                            reads=["lall%d" % half], writes=_k(Lb), dma_key=("lb", half))
                    TS(Deff, LDall[:, j, 0:8], -1.0, ALU.add)
                    STT(Deff, Deff, am[:, j:j + 1], one8, ALU.mult, ALU.add)
                    for hd in range(8):
                        TS(Sv[hd], Sv[hd], Deff[:, hd:hd + 1], ALU.mult)
                        STT(Sv[hd], Lbuf[hd // 4][:, hd % 4, :], am[:, j:j + 1], Sv[hd], ALU.mult, ALU.add)
                for i in range(8):
                    ACOPY(Sbf[i], Sv[i])

            def main_tile(mt):
                row0 = MROW0 + mt * TT
                gk_low()
                qkv(0, 0)
                for h in range(4):
                    if h + 1 < 4:
                        qkv(h + 1, (h + 1) % 2)
                    pp = h % 2
                    if USE_CC and mt == 0 and h == 0:
                        bmix(h, True)
                        combine()
                        gla(h, pp)
                    else:
                        gla_A(h, pp, 0)

                        def hook(n, h=h, pp=pp):
                            if n % 3 == 0:
                                tc = n // 3 - 1
                                gla_B(h, pp, tc)
                                if tc + 1 < 4:
                                    gla_A(h, pp, tc + 1)
                        bmix(h, mt == 0, hook)
                        gla_norm(h)
                    if h == 3:
                        load_x(row0)
                    bmix_gate(h)
                for db in range(4):
                    Wo = wblock("out", 0, NKC, db * 512, 512)
                    for tc in range(4):
                        ps = group_tm(Wo, mgT, tc)
                        dsl = slice(db * 512, (db + 1) * 512)
                        TT_(xt_tc[tc][:, dsl], ps, xt_tc[tc][:, dsl], ALU.add)
                        if db == 3:
                            s0_norm(xt_tc[tc], tc % 2)
                            if tc >= 1:
                                s0_tr(tc - 1, (tc - 1) % 2, xnT, 16)
                s0_tr(3, 1, xnT, 16)
                for fb in range(11):
                    Wg = wblock("gu", 0, NKC, fb * 512, 512)
                    for cc in range(4):
                        ps = group_fm(Wg, cc)
                        ACT(ft[4 + cc % 2], ps, AF.Sigmoid)
                        TT_(ft[cc], ps, ft[4 + cc % 2], ALU.mult)
                    Wu = wblock("gu", 0, NKC, FF + fb * 512, 512)
                    for cc in range(4):
                        ps = group_fm(Wu, cc)
                        TT_(actT[fb * 4 + cc], ps, ft[cc], ALU.mult)
                for db in range(4):
                    bs = [nbank() for _ in range(4)]
                    subs = [(0, 16), (16, 16), (32, 12)]
                    for (k0, nk) in subs:
                        Wd = wblock("dn", k0, nk, db * 512, 512)
                        for tc in range(4):
                            for j in range(nk):
                                fc = k0 + j
                                MM(bankv[bs[tc]], actT[fc][:, tc * 128:(tc + 1) * 128], Wd[:, j, :],
                                   fc == 0, fc == NFC - 1)
                    for tc in range(4):
                        dsl = slice(db * 512, (db + 1) * 512)
                        TT_(xt_tc[tc][:, dsl], bankv[bs[tc]], xt_tc[tc][:, dsl], ALU.add)
                    if mt + 1 < NMAIN:
                        s0_load(row0 + TT, db, db % 2)
                        if db >= 1:
                            s0_tr(db - 1, (db - 1) % 2, xnT, 0)
                if mt + 1 < NMAIN:
                    s0_tr(3, 1, xnT, 0)
                final_stats()
                for tc in range(4):
                    STT(xt_tc[tc], xt_tc[tc], rs[:, tc:tc + 1], fnwb, ALU.mult, ALU.mult)
                P.op("sp", lambda e: e.dma_start(out=out_d[mt * TT:(mt + 1) * TT, :].rearrange("(tc p) d -> p tc d", p=128),
                                                 in_=xt.ap),
                     reads=_k(xt), dma_key=("o", mt))

            s0_load(0, 0, 0)
            s0_tr(0, 0, xnTalt, 0)
            DCOPY(xnh, xnTalt_halo0)
            for tc in range(4):
                s0_load(XROW0, tc, (tc + 1) % 2)
                s0_tr(tc, (tc + 1) % 2, xnT, 0)
            for pt in range(NPRE):
                prefix_tile(pt, pt == NPRE - 1)
            if USE_CC:
                exchange_start()
            else:
                for i in range(8):
                    ACOPY(Sbf[i], Sv[i])
            for mt in range(NMAIN):
                main_tile(mt)
            P.op("sp", lambda e: None, extra_deps=[o.id for o in P.ops if o.dma and o.eng == "sp"])
            return rec

        plan = emit_all(Prog(nc), None)
        P = Prog(nc)
        emit_all(P, plan)
        P.emit(st)
        print("ops", len(P.ops), "sems", P.n_sems, "max_tick", P.max_tick, "wblocks", len(plan))
    return nc


def _consts():
    c = np.zeros((128, 384), np.float32)
    c[:, 0:128] = np.eye(128, dtype=np.float32)
    j = np.arange(128)[:, None]
    i = np.arange(128)[None, :]
    c[:, 128:256] = (j <= i).astype(np.float32)
    c[:, 256:384] = 1.0
    return c


def kernel(x, mix_norm_w, w_in, w_gk_up, b_gk_up, gla_norm_w, conv_w, w_out,
           ffn_norm_w, w_gate_up, w_down, final_norm_w):
    x = np.asarray(x, np.float32)
    B, S, _ = x.shape
    vecs = np.zeros((128, 92), np.float32)
    vecs[:, 0:16] = np.asarray(mix_norm_w, np.float32)[0].reshape(16, 128).T
    vecs[:, 16:32] = np.asarray(ffn_norm_w, np.float32)[0].reshape(16, 128).T
    cw = np.asarray(conv_w, np.float32)[0]
    vecs[:, 32:80] = cw.reshape(3, 16, 128).transpose(2, 1, 0).reshape(128, 48)
    vecs[:, 80:84] = np.asarray(gla_norm_w, np.float32)[0].reshape(4, 128).T
    vecs[:, 84:92] = np.asarray(b_gk_up, np.float32)[0].reshape(8, 128).T
    fnwb = np.ascontiguousarray(np.broadcast_to(np.asarray(final_norm_w, np.float32)[None, :], (128, D)))
    shared = {
        "w_in": np.ascontiguousarray(np.asarray(w_in, np.float32)[0]),
        "w_out": np.ascontiguousarray(np.asarray(w_out, np.float32)[0]),
        "w_gu": np.ascontiguousarray(np.asarray(w_gate_up, np.float32)[0]),
        "w_dn": np.ascontiguousarray(np.asarray(w_down, np.float32)[0]),
        "wgk": np.ascontiguousarray(np.asarray(w_gk_up, np.float32)[0]),
        "vecs": vecs, "fnwb": fnwb, "cst": _consts(),
    }
    in_maps = []
    own = NMAIN * TT
    for c in range(8):
        b, p = c // 4, c % 4
        xs = np.zeros((MROW0 + own, D), np.float32)
        start = p * own
        if start > 0:
            xs[0:XROW0] = x[b, start - XROW0:start]
            if not USE_CC:
                xs[MROW0 - start:MROW0] = x[b, 0:start]
        xs[MROW0:] = x[b, start:start + own]
        am = np.zeros((128, 4), np.float32)
        am[:, 0:p] = 1.0
        m = dict(shared)
        m["xs"] = xs
        m["am"] = am
        in_maps.append(m)
    nc = build_nc()
    res = run_bass_kernel_spmd(nc, in_maps, core_ids=list(range(8)))
    out = np.zeros((B, S, D), np.float32)
    for c in range(8):
        b, p = c // 4, c % 4
        out[b, p * own:(p + 1) * own] = res.results[c]["out"]
    return out
```

```python
import numpy as np
from contextlib import ExitStack
import concourse.bass as bass
import concourse.mybir as mybir
from concourse.bass_utils import run_bass_kernel_spmd
from concourse.alu_op_type import AluOpType as ALU

F32 = mybir.dt.float32
BF16 = mybir.dt.bfloat16
AF = mybir.ActivationFunctionType

D = 2048
NKC = 16
TT = 512
NTC = 4
FF = 5632
NFC = 44
USE_CC = True
NPRE = 4 if USE_CC else 12
XROW0 = 128
MROW0 = XROW0 if USE_CC else XROW0 + NPRE * TT
NMAIN = 4
EPS = 1e-6
C_Q, C_K, C_V, C_GO, C_GKL, C_GB, C_GC, C_XC, C_MA, C_MB = 0, 1024, 2048, 4096, 6144, 6160, 8208, 10256, 12304, 14352
ENGS = ("pe", "act", "dve", "pool", "sp")
ARENAS = ("XT", "FA")


class Op:
    __slots__ = ("id", "eng", "fn", "dma", "deps", "sem", "tick", "signal", "inc")

    def __init__(self, id, eng, fn, dma):
        self.id = id
        self.eng = eng
        self.fn = fn
        self.dma = dma
        self.deps = set()
        self.sem = None
        self.tick = None
        self.signal = False
        self.inc = 16 if dma else 1


class Prog:
    def __init__(self, nc):
        self.nc = nc
        self.ops = []
        self.by_eng = {e: [] for e in ENGS}
        self.last_writer = {}
        self.readers = {}
        self.arena_keys = {a: [] for a in ARENAS}
        self.dma_keys = {}

    def _expand(self, k):
        if isinstance(k, tuple) and k and k[0] in ARENAS:
            lst = self.arena_keys[k[0]]
            if k not in self.last_writer and k not in self.readers:
                lst.append(k)
                self.readers[k] = []
            return [o for o in lst if o[1] < k[2] and k[1] < o[2]]
        return [k]

    def op(self, eng, fn, reads=(), writes=(), dma_key=None, extra_deps=(), inc=None):
        o = Op(len(self.ops), eng, fn, dma_key is not None)
        deps = o.deps
        for r in reads:
            for k in self._expand(r):
                w = self.last_writer.get(k)
                if w is not None:
                    deps.add(w)
        for w_ in writes:
            for k in self._expand(w_):
                w = self.last_writer.get(k)
                if w is not None:
                    deps.add(w)
                rl = self.readers.get(k)
                if rl:
                    deps.update(rl)
        deps.update(extra_deps)
        deps.discard(o.id)
        for r in reads:
            self.readers.setdefault(r, []).append(o.id)
        for w_ in writes:
            for k in self._expand(w_):
                self.last_writer[k] = o.id
                self.readers[k] = []
        if dma_key is not None:
            c = self.dma_keys.get(dma_key, 0) + 1
            self.dma_keys[dma_key] = c
            if inc is not None:
                o.inc = inc
            o.sem = ("dma", dma_key)
            o.tick = o.inc * c
            o.signal = True
        self.ops.append(o)
        self.by_eng[eng].append(o)
        return o.id

    def emit(self, stack):
        nc = self.nc
        ops = self.ops
        for o in ops:
            nd = set()
            for d in o.deps:
                p = ops[d]
                if (not p.dma) and p.eng == o.eng and o.eng == "pe":
                    continue
                nd.add(d)
            o.deps = nd
            for d in nd:
                ops[d].signal = True
        cnt = {e: 0 for e in ENGS}
        for e in ENGS:
            for o in self.by_eng[e]:
                if o.dma:
                    continue
                if o.signal:
                    cnt[e] += 1
                    o.sem = ("eng", e)
                    o.tick = cnt[e]
        sems = {}
        for e in ENGS:
            if cnt[e] > 0:
                sems[("eng", e)] = stack.enter_context(nc.semaphore("prog_" + e))
        for i, k in enumerate(self.dma_keys):
            sems[("dma", k)] = stack.enter_context(nc.semaphore("dma_%d" % i))
        self.n_sems = len(sems)
        self.max_tick = max([cnt[e] for e in ENGS] + [16 * c for c in self.dma_keys.values()] + [0])
        block = stack.enter_context(nc.Block())

        def run(eng_name, eng):
            waited = {}
            for o in self.by_eng[eng_name]:
                need = {}
                for d in o.deps:
                    p = ops[d]
                    if need.get(p.sem, 0) < p.tick:
                        need[p.sem] = p.tick
                for s, v in need.items():
                    if waited.get(s, 0) >= v:
                        continue
                    eng.wait_ge(sems[s], v)
                    waited[s] = v
                inst = o.fn(eng)
                if o.signal:
                    inst.then_inc(sems[o.sem], o.inc)

        @block.tensor
        def _(e):
            run("pe", e)

        @block.scalar
        def _(e):
            run("act", e)

        @block.vector
        def _(e):
            run("dve", e)

        @block.gpsimd
        def _(e):
            run("pool", e)

        @block.sync
        def _(e):
            run("sp", e)


class V:
    __slots__ = ("ap", "keys")

    def __init__(self, ap, keys):
        self.ap = ap
        self.keys = tuple(keys)

    def __getitem__(self, idx):
        return V(self.ap[idx], self.keys)


def _k(*vs):
    out = []
    for v in vs:
        if isinstance(v, V):
            out.extend(v.keys)
    return out


def _a(v):
    return v.ap if isinstance(v, V) else v


def build_nc():
    nc = bass.Bass("TRN2", target_bir_lowering=False)
    xs = nc.dram_tensor("xs", [MROW0 + NMAIN * TT, D], F32, kind="ExternalInput").ap()
    am_d = nc.dram_tensor("am", [128, 4], F32, kind="ExternalInput").ap()
    lsrc = [nc.dram_tensor("lsrc%d" % i, [128, 2048], F32) for i in range(2)] + [nc.dram_tensor("lsrc2", [128, 32], F32)]
    lall = [nc.dram_tensor("lall%d" % i, [512, 2048], F32) for i in range(2)] + [nc.dram_tensor("lall2", [512, 32], F32)]
    w_in = nc.dram_tensor("w_in", [D, 16400], F32, kind="ExternalInput").ap()
    w_out = nc.dram_tensor("w_out", [D, D], F32, kind="ExternalInput").ap()
    w_gu = nc.dram_tensor("w_gu", [D, 2 * FF], F32, kind="ExternalInput").ap()
    w_dn = nc.dram_tensor("w_dn", [FF, D], F32, kind="ExternalInput").ap()
    wgk_d = nc.dram_tensor("wgk", [16, 1024], F32, kind="ExternalInput").ap()
    vecs_d = nc.dram_tensor("vecs", [128, 92], F32, kind="ExternalInput").ap()
    fnwb_d = nc.dram_tensor("fnwb", [128, D], F32, kind="ExternalInput").ap()
    cst_d = nc.dram_tensor("cst", [128, 384], F32, kind="ExternalInput").ap()
    out_d = nc.dram_tensor("out", [NMAIN * TT, D], F32, kind="ExternalOutput").ap()

    with ExitStack() as st:
        def sb(name, shape, dt):
            return st.enter_context(nc.sbuf_tensor("s_" + name, shape, dt))

        xnT_t = sb("xnT", [128, NKC, TT], BF16)
        mgT_t = sb("mgT", [128, NKC, TT], BF16)
        wr_t = [sb("wr%d" % i, [128, NKC, 512], BF16) for i in range(3)]
        S_t = sb("S", [128, 8, 512], F32)
        Sbf_t = sb("Sbf", [128, 8, 512], BF16)
        cst_t = sb("cst", [128, 384], F32)
        identb_t = sb("identb", [128, 128], BF16)
        ones_t = sb("ones512", [128, 512], F32)
        vecs_t = sb("vecs", [128, 92], F32)
        negb_t = sb("negb", [128, 8], F32)
        fnwb_t = sb("fnwb", [128, D], F32)
        wgk_t = sb("wgkb", [16, 1024], BF16)
        gkl_t = sb("gkl", [16, TT], BF16)
        wgl_t = sb("wgl", [128, NKC, 16], BF16)
        uh_t = sb("uh", [128, 16, 2], F32)
        xnh_t = sb("xnh", [128, NKC, 2], BF16)
        gch_t = sb("gch", [128, 4, 2], F32)
        ss_t = sb("ss", [128, 4], F32)
        rs_t = sb("rs", [128, 4], F32)
        nbl_t = sb("nbl", [128, 4], F32)
        dl_t = sb("dl", [128, 8, 4], F32)
        am_t = sb("am", [128, 4], F32)
        LD_t = sb("LD", [128, 32], F32)
        LDall_t = sb("LDall", [128, 4, 32], F32)
        Deff_t = sb("Deff", [128, 8], F32)
        one8_t = sb("one8", [128, 8], F32)
        xnp_t = [sb("xnp%d" % i, [128, D], BF16) for i in range(2)]
        ssp_t = sb("ssp", [128, 2], F32)
        rsp_t = sb("rsp", [128, 2], F32)
        XT_t = sb("XT", [128, 8192], F32)
        FA_t = sb("FA", [128, 11264], F32)
        banks = [st.enter_context(nc.psum_tensor("ps%d" % i, [128, 512], F32)) for i in range(8)]

        def arena(name, t, lo, nbytes, dt, pat=None, **kw):
            assert lo % 4 == 0 and nbytes % 4 == 0
            ap = t[:, lo // 4:(lo + nbytes) // 4]
            if dt == BF16:
                ap = ap.bitcast(BF16)
            if pat is not None:
                ap = ap.rearrange(pat, **kw)
            return V(ap, [(name, lo, lo + nbytes)])

        xt = arena("XT", XT_t, 0, 32768, F32, "p (a b) -> p a b", a=4)
        xt_tc = [arena("XT", XT_t, tc * 8192, 8192, F32) for tc in range(4)]
        spb = arena("XT", XT_t, 0, 2048, F32)
        E1 = arena("XT", XT_t, 2048, 2048, F32)
        E2 = arena("XT", XT_t, 4096, 2048, F32)
        E3 = arena("XT", XT_t, 6144, 2048, F32)
        qd = [arena("XT", XT_t, 8192 + i * 2048, 2048, BF16, "p (a b) -> p a b", a=2) for i in range(2)]
        kd = [arena("XT", XT_t, 12288 + i * 2048, 2048, BF16, "p (a b) -> p a b", a=2) for i in range(2)]
        kteT = arena("XT", XT_t, 16384, 2048, BF16, "p (a b) -> p a b", a=2)
        kte = [arena("XT", XT_t, 18432 + i * 2048, 2048, BF16, "p (a b) -> p a b", a=4) for i in range(2)]
        vb = [arena("XT", XT_t, 22528 + i * 4096, 4096, BF16, "p (a b) -> p a b", a=4) for i in range(2)]
        actT = [arena("FA", FA_t, fc * 1024, 1024, BF16) for fc in range(NFC)]
        xn_tm = [arena("FA", FA_t, tc * 4096, 4096, BF16) for tc in range(4)]
        junk = arena("FA", FA_t, 16384, 4096, BF16)
        scT = [arena("FA", FA_t, i * 256, 256, BF16) for i in range(2)]
        o_sb = arena("FA", FA_t, 512, 8192, F32, "p (a b) -> p a b", a=4)
        o_cc = [arena("FA", FA_t, 512 + cc * 2048, 2048, F32) for cc in range(4)]
        sq = [arena("FA", FA_t, 8704 + i * 2048, 2048, F32) for i in range(2)]
        lnv = arena("FA", FA_t, 12800, 2048, F32)
        gc = [arena("FA", FA_t, 14848 + cc * 2048, 2048, F32) for cc in range(4)]
        cv = [arena("FA", FA_t, 23040 + cc * 2048, 2048, F32) for cc in range(4)]
        sg = arena("FA", FA_t, 31232, 2048, F32)
        sb2 = arena("FA", FA_t, 33280, 2048, F32)
        tmp = arena("FA", FA_t, 35328, 2048, F32)
        ubuf = [arena("FA", FA_t, 37376 + i * 2064, 2064, F32) for i in range(2)]

        ft = [V(mgT_t[:, 2 * i:2 * i + 2, :].rearrange("p a b -> p (a b)").bitcast(F32),
                [("mgT", 2 * i), ("mgT", 2 * i + 1)]) for i in range(8)]
        xpc = [V(mgT_t[:, 8 * i:8 * i + 8, :].rearrange("p a b -> p (a b)").bitcast(F32),
                 [("mgT", g) for g in range(8 * i, 8 * i + 8)]) for i in range(2)]
        xnp = [V(xnp_t[i][:], [("xnp", i)]) for i in range(2)]
        ssp = [V(ssp_t[:, i:i + 1], [("ssp", i)]) for i in range(2)]
        rsp = [V(rsp_t[:, i:i + 1], [("rsp", i)]) for i in range(2)]
        xnTalt = [arena("FA", FA_t, kc * 1024, 1024, BF16) for kc in range(NKC)]
        xnTalt_halo = V(FA_t[:, 0:4096].bitcast(BF16).rearrange("p (a b) -> p a b", a=NKC)[:, :, TT - 2:TT],
                        [("FA", 0, 16384)])
        am = V(am_t[:], ["am"])
        LD = V(LD_t[:], ["LD"])
        LDall = V(LDall_t[:], ["LDall"])
        Deff = V(Deff_t[:], ["Deff"])
        one8 = V(one8_t[:], ["one8"])
        Lbuf = [V(mgT_t[:, 8 * i:8 * i + 8, :].rearrange("p a b -> p (a b)").bitcast(F32).rearrange("p (a b) -> p a b", a=4),
                  [("mgT", g) for g in range(8 * i, 8 * i + 8)]) for i in range(2)]
        xnTalt_halo0 = V(FA_t[:, 0:4096].bitcast(BF16).rearrange("p (a b) -> p a b", a=NKC)[:, :, 126:128],
                         [("FA", 0, 16384)])
        xnT = [V(xnT_t[:, kc, :], [("xnT", kc)]) for kc in range(NKC)]
        mgT = [V(mgT_t[:, g, :], [("mgT", g)]) for g in range(NKC)]
        Sv = [V(S_t[:, i, :], [("S", i)]) for i in range(8)]
        Sbf = [V(Sbf_t[:, i, :], [("Sbf", i)]) for i in range(8)]
        ident_f = V(cst_t[:, 0:128], ["cst"])
        maskT = V(cst_t[:, 128:256], ["cst"])
        ones128 = V(cst_t[:, 256:384], ["cst"])
        identb = V(identb_t[:], ["identb"])
        ones512 = V(ones_t[:], ["ones512"])
        vecs = V(vecs_t[:], ["vecs"])
        negb = V(negb_t[:], ["negb"])
        fnwb = V(fnwb_t[:], ["fnwb"])
        wgk = V(wgk_t[:], ["wgk"])
        gkl = V(gkl_t[:], ["gkl"])
        uh = [V(uh_t[:, g, :], [("uh", g)]) for g in range(16)]
        xnh = V(xnh_t[:], ["xnh"])
        gch = [V(gch_t[:, cc, :], [("gch", cc)]) for cc in range(4)]
        ss = V(ss_t[:], ["ss"])
        rs = V(rs_t[:], ["rs"])
        nbl = V(nbl_t[:], ["nbl"])
        dl = [V(dl_t[:, i, :], [("dl", i)]) for i in range(8)]
        bankv = [V(banks[i][:], [("ps", i)]) for i in range(8)]
        bankb = [V(banks[i][:].bitcast(BF16), [("ps", i)]) for i in range(8)]

        def vcol(c):
            return vecs[:, c:c + 1]

        def emit_all(P, plan):
            state = {"bank": 0, "wi": 0, "issued": 0}
            rec = []

            def ACT(out, in_, func, bias=None, scale=None):
                kw = {}
                if bias is not None:
                    kw["bias"] = _a(bias)
                if scale is not None:
                    kw["scale"] = scale
                P.op("act", lambda e: e.activation(out=out.ap, in_=in_.ap, func=func, **kw),
                     reads=_k(in_, bias), writes=_k(out))

            def ACOPY(out, in_):
                P.op("act", lambda e: e.copy(out=out.ap, in_=in_.ap), reads=_k(in_), writes=_k(out))

            def TT_(out, in0, in1, op):
                P.op("dve", lambda e: e.tensor_tensor(out=out.ap, in0=in0.ap, in1=in1.ap, op=op),
                     reads=_k(in0, in1), writes=_k(out))

            def TS(out, in0, s1, op0):
                P.op("dve", lambda e: e.tensor_scalar(out=out.ap, in0=in0.ap, scalar1=_a(s1), scalar2=None, op0=op0),
                     reads=_k(in0, s1), writes=_k(out))

            def STT(out, in0, s, in1, op0, op1):
                P.op("dve", lambda e: e.scalar_tensor_tensor(out=out.ap, in0=in0.ap, scalar=_a(s), in1=in1.ap,
                                                             op0=op0, op1=op1),
                     reads=_k(in0, s, in1), writes=_k(out))

            def DCOPY(out, in_):
                P.op("dve", lambda e: e.tensor_copy(out=out.ap, in_=in_.ap), reads=_k(in_), writes=_k(out))

            def MM(out, lhsT, rhs, start, stop):
                P.op("pe", lambda e: e.matmul(out.ap, lhsT=lhsT.ap, rhs=rhs.ap, start=start, stop=stop),
                     reads=_k(lhsT, rhs), writes=_k(out))

            def TR(out, in_):
                P.op("pe", lambda e: e.transpose(out=out.ap, in_=in_.ap, identity=identb.ap),
                     reads=_k(in_, identb), writes=_k(out))

            def nbank():
                b = state["bank"]
                state["bank"] = (b + 1) % 8
                return b

            def issue(desc, idx):
                src, k0, nk, segs = desc
                slot = idx % 3
                dram = {"in": w_in, "out": w_out, "gu": w_gu, "dn": w_dn}[src]
                off = 0
                prev = list(P.readers.get(("w", slot), ()))
                lw = P.last_writer.get(("w", slot))
                if lw is not None:
                    prev.append(lw)
                for si, (c0, n) in enumerate(segs):
                    view = dram[k0 * 128:(k0 + nk) * 128, c0:c0 + n].rearrange("(kc p) c -> p kc c", p=128)
                    dst = wr_t[slot][:, 0:nk, off:off + n]
                    key = ("w", slot) if si == 0 else ("w2", slot)
                    P.op("pool", lambda e, dst=dst, view=view: e.dma_start(out=dst, in_=view),
                         writes=[key], dma_key=key, extra_deps=prev if si > 0 else ())
                    off += n

            def wblock(src, k0, nk, c0, n, c1=None, n1=0):
                segs = ((c0, n),) if c1 is None else ((c0, n), (c1, n1))
                n = n + n1
                desc = (src, k0, nk, segs)
                i = state["wi"]
                state["wi"] = i + 1
                if plan is None:
                    rec.append(desc)
                else:
                    assert plan[i] == desc, (i, plan[i], desc)
                    while state["issued"] < min(len(plan), i + 3):
                        issue(plan[state["issued"]], state["issued"])
                        state["issued"] += 1
                slot = i % 3
                return V(wr_t[slot][:, 0:nk, 0:n], [("w", slot)] + ([("w2", slot)] if len(segs) > 1 else []))

            def group_fm(W, cc, M=128, N=TT, extra_rhs=None, X=None):
                X = xnT if X is None else X
                b = nbank()
                out = bankv[b][0:M, 0:N]
                for kc in range(NKC):
                    MM(out, W[:, kc, cc * 128:cc * 128 + M], X[kc][:, 0:N], kc == 0, kc == NKC - 1)
                if extra_rhs is not None:
                    b2 = nbank()
                    out2 = bankv[b2][0:M, 0:2]
                    for kc in range(NKC):
                        MM(out2, W[:, kc, cc * 128:cc * 128 + M], extra_rhs[:, kc, :], kc == 0, kc == NKC - 1)
                    return out, out2
                return out

            def group_tm(W, src, tc, ncol=512):
                b = nbank()
                out = bankv[b][:, 0:ncol]
                n = len(src)
                for kc in range(n):
                    MM(out, src[kc][:, tc * 128:(tc + 1) * 128], W[:, kc, 0:ncol], kc == 0, kc == n - 1)
                return out

            if True:
                P.op("sp", lambda e: e.dma_start(out=cst_t[:], in_=cst_d), writes=["cst"], dma_key="c_cst")
                P.op("sp", lambda e: e.dma_start(out=vecs_t[:], in_=vecs_d), writes=["vecs"], dma_key="c_vecs")
                P.op("sp", lambda e: e.dma_start(out=fnwb_t[:], in_=fnwb_d), writes=["fnwb"], dma_key="c_fnwb")
                P.op("pool", lambda e: e.dma_start(out=wgk_t[:], in_=wgk_d), writes=["wgk"], dma_key="c_wgk")
                P.op("pool", lambda e: e.dma_start(out=wgl_t[:],
                                                   in_=w_in[:, C_GKL:C_GKL + 16].rearrange("(kc p) c -> p kc c", p=128)),
                     writes=["wgl"], dma_key="c_wgl")
                DCOPY(identb, ident_f)
                P.op("dve", lambda e: e.memset(ones_t[:], 1.0), writes=["ones512"])
                for i in range(8):
                    P.op("dve", lambda e, i=i: e.memset(S_t[:, i, :], 0.0), writes=[("S", i)])
                TS(negb, vecs[:, 84:92], -1.0, ALU.mult)
                P.op("sp", lambda e: e.dma_start(out=am_t[:], in_=am_d), writes=["am"], dma_key="c_am")
                P.op("dve", lambda e: e.memset(LD_t[:], 0.0), writes=["LD"])
                P.op("dve", lambda e: e.memset(one8_t[:], 1.0), writes=["one8"])

            def load_x(row0):
                P.op("sp", lambda e: e.dma_start(out=xt.ap, in_=xs[row0:row0 + TT, :].rearrange("(tc p) d -> p tc d", p=128)),
                     writes=_k(xt), dma_key="x")

            def norm_to_T(wcol0, dst):
                for tc in range(4):
                    P.op("dve", lambda e, tc=tc: e.scalar_tensor_tensor(
                        out=junk.ap, in0=xt_tc[tc].ap, scalar=1.0, in1=xt_tc[tc].ap,
                        op0=ALU.mult, op1=ALU.mult, accum_out=ss_t[:, tc:tc + 1]),
                        reads=_k(xt_tc[tc]), writes=_k(junk, ss))
                ACT(rs, ss, AF.Ln, bias=EPS, scale=1.0 / D)
                ACT(rs, rs, AF.Exp, scale=-0.5)
                for tc in range(4):
                    TS(xn_tm[tc], xt_tc[tc], rs[:, tc:tc + 1], ALU.mult)
                for kc in range(NKC):
                    b = nbank()
                    for tc in range(4):
                        TR(bankb[b][:, tc * 128:(tc + 1) * 128], xn_tm[tc][:, kc * 128:(kc + 1) * 128])
                    TS(dst[kc], bankb[b][:, 0:TT], vcol(wcol0 + kc), ALU.mult)

            def s0_norm(src, j):
                P.op("dve", lambda e: e.scalar_tensor_tensor(
                    out=xnp[j].ap, in0=src.ap, scalar=1.0, in1=src.ap,
                    op0=ALU.mult, op1=ALU.mult, accum_out=ssp[j].ap),
                    reads=_k(src), writes=_k(xnp[j], ssp[j]))
                ACT(rsp[j], ssp[j], AF.Ln, bias=EPS, scale=1.0 / D)
                ACT(rsp[j], rsp[j], AF.Exp, scale=-0.5)
                TS(xnp[j], src, rsp[j], ALU.mult)

            def s0_load(row0, tc, j):
                r0 = row0 + tc * 128
                P.op("sp", lambda e: e.dma_start(out=xpc[j].ap, in_=xs[r0:r0 + 128, :]),
                     writes=_k(xpc[j]), dma_key=("xp", j))
                s0_norm(xpc[j], j)

            def s0_tr(tc, j, dst, wcol0):
                for kq in range(4):
                    b = nbank()
                    for i in range(4):
                        kc = kq * 4 + i
                        TR(bankb[b][:, i * 128:(i + 1) * 128], xnp[j][:, kc * 128:(kc + 1) * 128])
                    for i in range(4):
                        kc = kq * 4 + i
                        TS(dst[kc][:, tc * 128:(tc + 1) * 128], bankb[b][:, i * 128:(i + 1) * 128],
                           vcol(wcol0 + kc), ALU.mult)

            def final_stats():
                for tc in range(4):
                    P.op("dve", lambda e, tc=tc: e.scalar_tensor_tensor(
                        out=junk.ap, in0=xt_tc[tc].ap, scalar=1.0, in1=xt_tc[tc].ap,
                        op0=ALU.mult, op1=ALU.mult, accum_out=ss_t[:, tc:tc + 1]),
                        reads=_k(xt_tc[tc]), writes=_k(junk, ss))
                ACT(rs, ss, AF.Ln, bias=EPS, scale=1.0 / D)
                ACT(rs, rs, AF.Exp, scale=-0.5)

            def gk_low(X=None):
                W = V(wgl_t[:], ["wgl"])
                ps = group_fm(W, 0, M=16, X=X)
                ACOPY(gkl, ps)

            def decay_common(hd):
                b = nbank()
                ps = bankv[b]
                MM(ps, wgk[:, hd * 128:(hd + 1) * 128], gkl, True, True)
                ACT(spb, ps, AF.Exp, bias=negb[:, hd:hd + 1], scale=-1.0)
                ACT(spb, spb, AF.Ln, bias=1.0, scale=1.0)

            def scan(lo, hi):
                P.op("dve", lambda e: e.tensor_tensor_scan(out=E2.ap[:, lo:hi], data0=ones512.ap[:, lo:hi],
                                                           data1=spb.ap[:, lo:hi], initial=0.0,
                                                           op0=ALU.mult, op1=ALU.add),
                     reads=_k(spb, ones512), writes=_k(E2))

            def kt_and_v(h, pp, Wv, X=None):
                X = xnT if X is None else X
                for tc in range(4):
                    ps = group_tm(Wv, X, tc)
                    ACOPY(vb[pp][:, tc, :], ps)
                b = nbank()
                for tc in range(4):
                    for dkc in range(2):
                        TR(bankb[b][:, (tc * 2 + dkc) * 128:(tc * 2 + dkc + 1) * 128],
                           kteT[:, dkc, tc * 128:(tc + 1) * 128])
                P.op("act", lambda e: e.copy(out=kte[pp].ap, in_=bankb[b].ap.rearrange("p (a b) -> p a b", a=4)),
                     reads=_k(bankb[b]), writes=_k(kte[pp]))

            def prefix_tile(pt, last):
                X = xnT if pt % 2 == 0 else xnTalt
                Xn = xnT if (pt + 1) % 2 == 0 or last else xnTalt
                nrow0 = MROW0 if last else XROW0 + (pt + 1) * TT
                gk_low(X)
                for h in range(4):
                    s0_load(nrow0, h, h % 2)
                    if h >= 1:
                        s0_tr(h - 1, (h - 1) % 2, Xn, 0)
                    Wk = wblock("in", 0, NKC, C_K + h * 256, 256)
                    for dkc in range(2):
                        hd = h * 2 + dkc
                        decay_common(hd)
                        scan(0, TT)
                        TS(nbl[:, 0:1], E2[:, TT - 1:TT], -1.0 / 16, ALU.mult)
                        TT_(LD[:, hd:hd + 1], LD[:, hd:hd + 1], nbl[:, 0:1], ALU.add)
                        ACT(E3, E2, AF.Exp, bias=nbl[:, 0:1], scale=1.0 / 16)
                        ACT(dl[hd][:, 0:1], nbl[:, 0:1], AF.Exp)
                        ps = group_fm(Wk, dkc, X=X)
                        TT_(kteT[:, dkc, :], ps, E3, ALU.mult)
                    Wv = wblock("in", 0, NKC, C_V + h * 512, 512)
                    kt_and_v(h, 0, Wv, X)
                    for dkc in range(2):
                        hd = h * 2 + dkc
                        b = nbank()
                        for tc in range(4):
                            MM(bankv[b], kte[0][:, tc, dkc * 128:(dkc + 1) * 128], vb[0][:, tc, :], tc == 0, tc == 3)
                        STT(Sv[hd], Sv[hd], dl[hd][:, 0:1], bankv[b], ALU.mult, ALU.add)
                s0_tr(3, 1, Xn, 0)

            def qkv(h, pp):
                Wqk = wblock("in", 0, NKC, C_Q + h * 256, 256, C_K + h * 256, 256)
                for dkc in range(2):
                    hd = h * 2 + dkc
                    decay_common(hd)
                    for tc in range(4):
                        scan(tc * 128, (tc + 1) * 128)
                    ACT(E1, E2, AF.Exp, scale=-1.0 / 16)
                    TS(nbl, V(E2.ap.rearrange("p (a b) -> p a b", a=4)[:, :, 127], E2.keys), -1.0 / 16, ALU.mult)
                    for tc in range(4):
                        ACT(E3[:, tc * 128:(tc + 1) * 128], E2[:, tc * 128:(tc + 1) * 128], AF.Exp,
                            bias=nbl[:, tc:tc + 1], scale=1.0 / 16)
                    ACT(E2, E2, AF.Exp, scale=1.0 / 16)
                    ACT(dl[hd], nbl, AF.Exp)
                    ps = group_fm(Wqk, dkc)
                    STT(qd[pp][:, dkc, :], ps, 1.0 / 16, E1, ALU.mult, ALU.mult)
                    ps = group_fm(Wqk, 2 + dkc)
                    TT_(kd[pp][:, dkc, :], ps, E2, ALU.mult)
                    TT_(kteT[:, dkc, :], ps, E3, ALU.mult)
                Wv = wblock("in", 0, NKC, C_V + h * 512, 512)
                kt_and_v(h, pp, Wv)

            def gla_A(h, pp, tc):
                tsl = slice(tc * 128, (tc + 1) * 128)
                b = nbank()
                sc = bankv[b][:, 0:128]
                for dkc in range(2):
                    MM(sc, kd[pp][:, dkc, tsl], qd[pp][:, dkc, tsl], dkc == 0, dkc == 1)
                TT_(scT[tc % 2], sc, maskT, ALU.mult)

            def gla_B(h, pp, tc):
                tsl = slice(tc * 128, (tc + 1) * 128)
                s_ = scT[tc % 2]
                b = nbank()
                for dvc in range(4):
                    dsl = slice(dvc * 128, (dvc + 1) * 128)
                    o = bankv[b][:, dsl]
                    MM(o, vb[pp][:, tc, dsl], s_, True, False)
                    MM(o, Sbf[h * 2][:, dsl], qd[pp][:, 0, tsl], False, False)
                    MM(o, Sbf[h * 2 + 1][:, dsl], qd[pp][:, 1, tsl], False, True)
                P.op("act", lambda e, b=b, tsl=tsl: e.copy(out=o_sb.ap[:, :, tsl],
                                                           in_=bankv[b].ap.rearrange("p (a b) -> p a b", a=4)),
                     reads=_k(bankv[b]), writes=_k(o_sb))
                for dkc in range(2):
                    hd = h * 2 + dkc
                    b = nbank()
                    MM(bankv[b], kte[pp][:, tc, dkc * 128:(dkc + 1) * 128], vb[pp][:, tc, :], True, True)
                    STT(Sv[hd], Sv[hd], dl[hd][:, tc:tc + 1], bankv[b], ALU.mult, ALU.add)
                    ACOPY(Sbf[hd], Sv[hd])

            def gla_norm(h):
                b = nbank()
                for dvc in range(4):
                    ACT(sq[dvc % 2], o_cc[dvc], AF.Square)
                    MM(bankv[b], ones128, sq[dvc % 2], dvc == 0, dvc == 3)
                ACT(lnv, bankv[b], AF.Ln, bias=EPS, scale=1.0 / 512)
                ACT(lnv, lnv, AF.Exp, scale=-0.5)
                for dvc in range(4):
                    STT(o_cc[dvc], o_cc[dvc], vcol(80 + dvc), lnv, ALU.mult, ALU.mult)

            def gla(h, pp):
                for tc in range(4):
                    gla_A(h, pp, tc)
                    gla_B(h, pp, tc)
                gla_norm(h)

            def bmix(h, first_tile, hook=None):
                cnt = [0]

                def tick():
                    cnt[0] += 1
                    if hook is not None:
                        hook(cnt[0])
                Wgc = wblock("in", 0, NKC, C_GC + h * 512, 512)
                for cc in range(4):
                    if first_tile:
                        ps, psh = group_fm(Wgc, cc, extra_rhs=xnh)
                        ACOPY(gch[cc], psh)
                    else:
                        ps = group_fm(Wgc, cc)
                    ACOPY(gc[cc], ps)
                    tick()
                Wxc = wblock("in", 0, NKC, C_XC + h * 512, 512)
                for cc in range(4):
                    g = h * 4 + cc
                    u = ubuf[cc % 2]
                    if first_tile:
                        ps, psh = group_fm(Wxc, cc, extra_rhs=xnh)
                        TT_(uh[g], psh, gch[cc], ALU.mult)
                    else:
                        ps = group_fm(Wxc, cc)
                    DCOPY(u[:, 0:2], uh[g])
                    TT_(u[:, 2:514], ps, gc[cc], ALU.mult)
                    TS(cv[cc], u[:, 2:514], vcol(32 + g * 3 + 2), ALU.mult)
                    STT(cv[cc], u[:, 1:513], vcol(32 + g * 3 + 1), cv[cc], ALU.mult, ALU.add)
                    STT(cv[cc], u[:, 0:512], vcol(32 + g * 3 + 0), cv[cc], ALU.mult, ALU.add)
                    DCOPY(uh[g], u[:, 512:514])
                    tick()
                Wgb = wblock("in", 0, NKC, C_GB + h * 512, 512)
                for cc in range(4):
                    ps = group_fm(Wgb, cc)
                    TT_(cv[cc], ps, cv[cc], ALU.mult)
                    tick()

            def bmix_gate(h):
                Wgo = wblock("in", 0, NKC, C_GO + h * 512, 512)
                for cc in range(4):
                    ps = group_fm(Wgo, cc)
                    ACT(sg, ps, AF.Sigmoid)
                    TT_(tmp, ps, sg, ALU.mult)
                    TT_(o_cc[cc], tmp, o_cc[cc], ALU.mult)
                Wma = wblock("in", 0, NKC, C_MA + h * 512, 512)
                for cc in range(4):
                    ps = group_fm(Wma, cc)
                    ACT(sg, ps, AF.Sigmoid)
                    TT_(o_cc[cc], o_cc[cc], sg, ALU.mult)
                Wmb = wblock("in", 0, NKC, C_MB + h * 512, 512)
                for cc in range(4):
                    g = h * 4 + cc
                    ps = group_fm(Wmb, cc)
                    ACT(sb2, ps, AF.Sigmoid)
                    TT_(cv[cc], cv[cc], sb2, ALU.mult)
                    TT_(mgT[g], cv[cc], o_cc[cc], ALU.add)

            def exchange_start():
                P.op("sp", lambda e: e.dma_start(out=lsrc[0].ap(), in_=S_t[:, 0:4, :].rearrange("p a b -> p (a b)")),
                     reads=[("S", i) for i in range(4)], writes=["lsrc0"], dma_key="ls0")
                P.op("sp", lambda e: e.dma_start(out=lsrc[1].ap(), in_=S_t[:, 4:8, :].rearrange("p a b -> p (a b)")),
                     reads=[("S", i) for i in range(4, 8)], writes=["lsrc1"], dma_key="ls1")
                P.op("sp", lambda e: e.dma_start(out=lsrc[2].ap(), in_=LD_t[:]), reads=["LD"], writes=["lsrc2"],
                     dma_key="ls2")
                prior = [o.id for o in P.by_eng["pool"] if o.dma][-3:]
                ccs = []
                for i in range(3):
                    ccs.append(P.op("pool", lambda e, i=i: e.collective_compute(
                        "AllGather", ALU.bypass, replica_groups=[[0, 1, 2, 3], [4, 5, 6, 7]],
                        ins=[lsrc[i].ap()], outs=[lall[i].ap()]),
                        reads=["lsrc%d" % i], writes=["lall%d" % i], dma_key="cc%d" % i, inc=1,
                        extra_deps=prior))

            def combine():
                for i in range(8):
                    P.op("dve", lambda e, i=i: e.memset(S_t[:, i, :], 0.0), writes=[("S", i)])
                P.op("sp", lambda e: e.dma_start(out=LDall_t[:], in_=lall[2].ap().rearrange("(r p) c -> p r c", p=128)),
                     reads=["lall2"], writes=["LDall"], dma_key="ld_all")
                ACT(LDall, LDall, AF.Exp)
                for j in range(3):
                    for half in range(2):
                        Lb = Lbuf[half]
                        P.op("sp", lambda e, j=j, Lb=Lb, half=half: e.dma_start(
                            out=Lb.ap, in_=lall[half].ap()[j * 128:(j + 1) * 128, :].rearrange("p (a b) -> p a b", a=4)),
                            reads=["lall%d" % half], writes=_k(Lb), dma_key=("lb", half))
                    TS(Deff, LDall[:, j, 0:8], -1.0, ALU.add)
                    STT(Deff, Deff, am[:, j:j + 1], one8, ALU.mult, ALU.add)
                    for hd in range(8):
                        TS(Sv[hd], Sv[hd], Deff[:, hd:hd + 1], ALU.mult)
                        STT(Sv[hd], Lbuf[hd // 4][:, hd % 4, :], am[:, j:j + 1], Sv[hd], ALU.mult, ALU.add)
                for i in range(8):
                    ACOPY(Sbf[i], Sv[i])

            def main_tile(mt):
                row0 = MROW0 + mt * TT
                gk_low()
                qkv(0, 0)
                for h in range(4):
                    if h + 1 < 4:
                        qkv(h + 1, (h + 1) % 2)
                    pp = h % 2
                    if USE_CC and mt == 0 and h == 0:
                        bmix(h, True)
                        combine()
                        gla(h, pp)
                    else:
                        gla_A(h, pp, 0)

                        def hook(n, h=h, pp=pp):
                            if n % 3 == 0:
                                tc = n // 3 - 1
                                gla_B(h, pp, tc)
                                if tc + 1 < 4:
                                    gla_A(h, pp, tc + 1)
                        bmix(h, mt == 0, hook)
                        gla_norm(h)
                    if h == 3:
                        load_x(row0)
                    bmix_gate(h)
                for db in range(4):
                    Wo = wblock("out", 0, NKC, db * 512, 512)
                    for tc in range(4):
                        ps = group_tm(Wo, mgT, tc)
                        dsl = slice(db * 512, (db + 1) * 512)
                        TT_(xt_tc[tc][:, dsl], ps, xt_tc[tc][:, dsl], ALU.add)
                        if db == 3:
                            s0_norm(xt_tc[tc], tc % 2)
                            if tc >= 1:
                                s0_tr(tc - 1, (tc - 1) % 2, xnT, 16)
                s0_tr(3, 1, xnT, 16)
                for fb in range(11):
                    Wg = wblock("gu", 0, NKC, fb * 512, 512)
                    for cc in range(4):
                        ps = group_fm(Wg, cc)
                        ACT(ft[4 + cc % 2], ps, AF.Sigmoid)
                        TT_(ft[cc], ps, ft[4 + cc % 2], ALU.mult)
                    Wu = wblock("gu", 0, NKC, FF + fb * 512, 512)
                    for cc in range(4):
                        ps = group_fm(Wu, cc)
                        TT_(actT[fb * 4 + cc], ps, ft[cc], ALU.mult)
                for db in range(4):
                    bs = [nbank() for _ in range(4)]
                    subs = [(0, 16), (16, 16), (32, 12)]
                    for (k0, nk) in subs:
                        Wd = wblock("dn", k0, nk, db * 512, 512)
                        for tc in range(4):
                            for j in range(nk):
                                fc = k0 + j
                                MM(bankv[bs[tc]], actT[fc][:, tc * 128:(tc + 1) * 128], Wd[:, j, :],
                                   fc == 0, fc == NFC - 1)
                    for tc in range(4):
                        dsl = slice(db * 512, (db + 1) * 512)
                        TT_(xt_tc[tc][:, dsl], bankv[bs[tc]], xt_tc[tc][:, dsl], ALU.add)
                        if db == 3 and mt == NMAIN - 1:
                            j = tc % 2
                            P.op("dve", lambda e, tc=tc, j=j: e.scalar_tensor_tensor(
                                out=xnp[j].ap, in0=xt_tc[tc].ap, scalar=1.0, in1=xt_tc[tc].ap,
                                op0=ALU.mult, op1=ALU.mult, accum_out=ssp[j].ap),
                                reads=_k(xt_tc[tc]), writes=_k(xnp[j], ssp[j]))
                            ACT(rsp[j], ssp[j], AF.Ln, bias=EPS, scale=1.0 / D)
                            ACT(rsp[j], rsp[j], AF.Exp, scale=-0.5)
                            STT(xt_tc[tc], xt_tc[tc], rsp[j], fnwb, ALU.mult, ALU.mult)
                            r0 = mt * TT + tc * 128
                            P.op("sp", lambda e, tc=tc, r0=r0: e.dma_start(out=out_d[r0:r0 + 128, :], in_=xt_tc[tc].ap),
                                 reads=_k(xt_tc[tc]), dma_key=("o", mt))
                    if mt + 1 < NMAIN:
                        s0_load(row0 + TT, db, db % 2)
                        if db >= 1:
                            s0_tr(db - 1, (db - 1) % 2, xnT, 0)
                if mt + 1 < NMAIN:
                    s0_tr(3, 1, xnT, 0)
                if mt == NMAIN - 1:
                    return
                final_stats()
                for tc in range(4):
                    STT(xt_tc[tc], xt_tc[tc], rs[:, tc:tc + 1], fnwb, ALU.mult, ALU.mult)
                P.op("sp", lambda e: e.dma_start(out=out_d[mt * TT:(mt + 1) * TT, :].rearrange("(tc p) d -> p tc d", p=128),
                                                 in_=xt.ap),
                     reads=_k(xt), dma_key=("o", mt))

            s0_load(0, 0, 0)
            s0_tr(0, 0, xnTalt, 0)
            DCOPY(xnh, xnTalt_halo0)
            for tc in range(4):
                s0_load(XROW0, tc, (tc + 1) % 2)
                s0_tr(tc, (tc + 1) % 2, xnT, 0)
            for pt in range(NPRE):
                prefix_tile(pt, pt == NPRE - 1)
            if USE_CC:
                exchange_start()
            else:
                for i in range(8):
                    ACOPY(Sbf[i], Sv[i])
            for mt in range(NMAIN):
                main_tile(mt)
            P.op("sp", lambda e: None, extra_deps=[o.id for o in P.ops if o.dma and o.eng == "sp"])
            return rec

        plan = emit_all(Prog(nc), None)
        P = Prog(nc)
        emit_all(P, plan)
        P.emit(st)
        print("ops", len(P.ops), "sems", P.n_sems, "max_tick", P.max_tick, "wblocks", len(plan))
    return nc


def _consts():
    c = np.zeros((128, 384), np.float32)
    c[:, 0:128] = np.eye(128, dtype=np.float32)
    j = np.arange(128)[:, None]
    i = np.arange(128)[None, :]
    c[:, 128:256] = (j <= i).astype(np.float32)
    c[:, 256:384] = 1.0
    return c


def kernel(x, mix_norm_w, w_in, w_gk_up, b_gk_up, gla_norm_w, conv_w, w_out,
           ffn_norm_w, w_gate_up, w_down, final_norm_w):
    x = np.asarray(x, np.float32)
    B, S, _ = x.shape
    vecs = np.zeros((128, 92), np.float32)
    vecs[:, 0:16] = np.asarray(mix_norm_w, np.float32)[0].reshape(16, 128).T
    vecs[:, 16:32] = np.asarray(ffn_norm_w, np.float32)[0].reshape(16, 128).T
    cw = np.asarray(conv_w, np.float32)[0]
    vecs[:, 32:80] = cw.reshape(3, 16, 128).transpose(2, 1, 0).reshape(128, 48)
    vecs[:, 80:84] = np.asarray(gla_norm_w, np.float32)[0].reshape(4, 128).T
    vecs[:, 84:92] = np.asarray(b_gk_up, np.float32)[0].reshape(8, 128).T
    fnwb = np.ascontiguousarray(np.broadcast_to(np.asarray(final_norm_w, np.float32)[None, :], (128, D)))
    shared = {
        "w_in": np.ascontiguousarray(np.asarray(w_in, np.float32)[0]),
        "w_out": np.ascontiguousarray(np.asarray(w_out, np.float32)[0]),
        "w_gu": np.ascontiguousarray(np.asarray(w_gate_up, np.float32)[0]),
        "w_dn": np.ascontiguousarray(np.asarray(w_down, np.float32)[0]),
        "wgk": np.ascontiguousarray(np.asarray(w_gk_up, np.float32)[0]),
        "vecs": vecs, "fnwb": fnwb, "cst": _consts(),
    }
    in_maps = []
    own = NMAIN * TT
    for c in range(8):
        b, p = c // 4, c % 4
        xs = np.zeros((MROW0 + own, D), np.float32)
        start = p * own
        if start > 0:
            xs[0:XROW0] = x[b, start - XROW0:start]
            if not USE_CC:
                xs[MROW0 - start:MROW0] = x[b, 0:start]
        xs[MROW0:] = x[b, start:start + own]
        am = np.zeros((128, 4), np.float32)
        am[:, 0:p] = 1.0
        m = dict(shared)
        m["xs"] = xs
        m["am"] = am
        in_maps.append(m)
    nc = build_nc()
    res = run_bass_kernel_spmd(nc, in_maps, core_ids=list(range(8)))
    out = np.zeros((B, S, D), np.float32)
    for c in range(8):
        b, p = c // 4, c % 4
        out[b, p * own:(p + 1) * own] = res.results[c]["out"]
    return out
```

```python
import numpy as np
from contextlib import ExitStack
import concourse.bass as bass
import concourse.mybir as mybir
from concourse.bass_utils import run_bass_kernel_spmd
from concourse.alu_op_type import AluOpType as ALU

F32 = mybir.dt.float32
BF16 = mybir.dt.bfloat16
AF = mybir.ActivationFunctionType

D = 2048
NKC = 16
TT = 512
NTC = 4
FF = 5632
NFC = 44
USE_CC = True
NPRE = 4 if USE_CC else 12
XROW0 = 128
MROW0 = XROW0 if USE_CC else XROW0 + NPRE * TT
NMAIN = 4
EPS = 1e-6
C_Q, C_K, C_V, C_GO, C_GKL, C_GB, C_GC, C_XC, C_MA, C_MB = 0, 1024, 2048, 4096, 6144, 6160, 8208, 10256, 12304, 14352
ENGS = ("pe", "act", "dve", "pool", "sp")
ARENAS = ("XT", "FA")


class Op:
    __slots__ = ("id", "eng", "fn", "dma", "deps", "sem", "tick", "signal", "inc")

    def __init__(self, id, eng, fn, dma):
        self.id = id
        self.eng = eng
        self.fn = fn
        self.dma = dma
        self.deps = set()
        self.sem = None
        self.tick = None
        self.signal = False
        self.inc = 16 if dma else 1


class Prog:
    def __init__(self, nc):
        self.nc = nc
        self.ops = []
        self.by_eng = {e: [] for e in ENGS}
        self.last_writer = {}
        self.readers = {}
        self.arena_keys = {a: [] for a in ARENAS}
        self.dma_keys = {}

    def _expand(self, k):
        if isinstance(k, tuple) and k and k[0] in ARENAS:
            lst = self.arena_keys[k[0]]
            if k not in self.last_writer and k not in self.readers:
                lst.append(k)
                self.readers[k] = []
            return [o for o in lst if o[1] < k[2] and k[1] < o[2]]
        return [k]

    def op(self, eng, fn, reads=(), writes=(), dma_key=None, extra_deps=(), inc=None):
        o = Op(len(self.ops), eng, fn, dma_key is not None)
        deps = o.deps
        for r in reads:
            for k in self._expand(r):
                w = self.last_writer.get(k)
                if w is not None:
                    deps.add(w)
        for w_ in writes:
            for k in self._expand(w_):
                w = self.last_writer.get(k)
                if w is not None:
                    deps.add(w)
                rl = self.readers.get(k)
                if rl:
                    deps.update(rl)
        deps.update(extra_deps)
        deps.discard(o.id)
        for r in reads:
            self.readers.setdefault(r, []).append(o.id)
        for w_ in writes:
            for k in self._expand(w_):
                self.last_writer[k] = o.id
                self.readers[k] = []
        if dma_key is not None:
            c = self.dma_keys.get(dma_key, 0) + 1
            self.dma_keys[dma_key] = c
            if inc is not None:
                o.inc = inc
            o.sem = ("dma", dma_key)
            o.tick = o.inc * c
            o.signal = True
        self.ops.append(o)
        self.by_eng[eng].append(o)
        return o.id

    def emit(self, stack):
        nc = self.nc
        ops = self.ops
        for o in ops:
            nd = set()
            for d in o.deps:
                p = ops[d]
                if (not p.dma) and p.eng == o.eng and o.eng == "pe":
                    continue
                nd.add(d)
            o.deps = nd
            for d in nd:
                ops[d].signal = True
        cnt = {e: 0 for e in ENGS}
        for e in ENGS:
            for o in self.by_eng[e]:
                if o.dma:
                    continue
                if o.signal:
                    cnt[e] += 1
                    o.sem = ("eng", e)
                    o.tick = cnt[e]
        sems = {}
        for e in ENGS:
            if cnt[e] > 0:
                sems[("eng", e)] = stack.enter_context(nc.semaphore("prog_" + e))
        for i, k in enumerate(self.dma_keys):
            sems[("dma", k)] = stack.enter_context(nc.semaphore("dma_%d" % i))
        self.n_sems = len(sems)
        self.max_tick = max([cnt[e] for e in ENGS] + [16 * c for c in self.dma_keys.values()] + [0])
        block = stack.enter_context(nc.Block())

        def run(eng_name, eng):
            waited = {}
            for o in self.by_eng[eng_name]:
                need = {}
                for d in o.deps:
                    p = ops[d]
                    if need.get(p.sem, 0) < p.tick:
                        need[p.sem] = p.tick
                for s, v in need.items():
                    if waited.get(s, 0) >= v:
                        continue
                    eng.wait_ge(sems[s], v)
                    waited[s] = v
                inst = o.fn(eng)
                if o.signal:
                    inst.then_inc(sems[o.sem], o.inc)

        @block.tensor
        def _(e):
            run("pe", e)

        @block.scalar
        def _(e):
            run("act", e)

        @block.vector
        def _(e):
            run("dve", e)

        @block.gpsimd
        def _(e):
            run("pool", e)

        @block.sync
        def _(e):
            run("sp", e)


class V:
    __slots__ = ("ap", "keys")

    def __init__(self, ap, keys):
        self.ap = ap
        self.keys = tuple(keys)

    def __getitem__(self, idx):
        return V(self.ap[idx], self.keys)


def _k(*vs):
    out = []
    for v in vs:
        if isinstance(v, V):
            out.extend(v.keys)
    return out


def _a(v):
    return v.ap if isinstance(v, V) else v


def build_nc():
    nc = bass.Bass("TRN2", target_bir_lowering=False)
    xs = nc.dram_tensor("xs", [MROW0 + NMAIN * TT, D], F32, kind="ExternalInput").ap()
    am_d = nc.dram_tensor("am", [128, 4], F32, kind="ExternalInput").ap()
    lsrc = [nc.dram_tensor("lsrc%d" % i, [128, 2048], F32) for i in range(2)] + [nc.dram_tensor("lsrc2", [128, 32], F32)]
    lall = [nc.dram_tensor("lall%d" % i, [512, 2048], F32) for i in range(2)] + [nc.dram_tensor("lall2", [512, 32], F32)]
    w_in = nc.dram_tensor("w_in", [D, 16400], F32, kind="ExternalInput").ap()
    w_out = nc.dram_tensor("w_out", [D, D], F32, kind="ExternalInput").ap()
    w_gu = nc.dram_tensor("w_gu", [D, 2 * FF], F32, kind="ExternalInput").ap()
    w_dn = nc.dram_tensor("w_dn", [FF, D], F32, kind="ExternalInput").ap()
    wgk_d = nc.dram_tensor("wgk", [16, 1024], F32, kind="ExternalInput").ap()
    vecs_d = nc.dram_tensor("vecs", [128, 92], F32, kind="ExternalInput").ap()
    fnwb_d = nc.dram_tensor("fnwb", [128, D], F32, kind="ExternalInput").ap()
    cst_d = nc.dram_tensor("cst", [128, 384], F32, kind="ExternalInput").ap()
    out_d = nc.dram_tensor("out", [NMAIN * TT, D], F32, kind="ExternalOutput").ap()

    with ExitStack() as st:
        def sb(name, shape, dt):
            return st.enter_context(nc.sbuf_tensor("s_" + name, shape, dt))

        xnT_t = sb("xnT", [128, NKC, TT], BF16)
        mgT_t = sb("mgT", [128, NKC, TT], BF16)
        wr_t = [sb("wr%d" % i, [128, NKC, 512], BF16) for i in range(3)]
        S_t = sb("S", [128, 8, 512], F32)
        Sbf_t = sb("Sbf", [128, 8, 512], BF16)
        cst_t = sb("cst", [128, 384], F32)
        identb_t = sb("identb", [128, 128], BF16)
        ones_t = sb("ones512", [128, 512], F32)
        vecs_t = sb("vecs", [128, 92], F32)
        negb_t = sb("negb", [128, 8], F32)
        fnwb_t = sb("fnwb", [128, D], F32)
        wgk_t = sb("wgkb", [16, 1024], BF16)
        gkl_t = sb("gkl", [16, TT], BF16)
        wgl_t = sb("wgl", [128, NKC, 16], BF16)
        uh_t = sb("uh", [128, 16, 2], F32)
        xnh_t = sb("xnh", [128, NKC, 2], BF16)
        gch_t = sb("gch", [128, 4, 2], F32)
        ss_t = sb("ss", [128, 4], F32)
        rs_t = sb("rs", [128, 4], F32)
        nbl_t = sb("nbl", [128, 4], F32)
        dl_t = sb("dl", [128, 8, 4], F32)
        am_t = sb("am", [128, 4], F32)
        LD_t = sb("LD", [128, 32], F32)
        LDall_t = sb("LDall", [128, 4, 32], F32)
        Deff_t = sb("Deff", [128, 8], F32)
        one8_t = sb("one8", [128, 8], F32)
        xnp_t = [sb("xnp%d" % i, [128, D], BF16) for i in range(2)]
        ssp_t = sb("ssp", [128, 2], F32)
        rsp_t = sb("rsp", [128, 2], F32)
        XT_t = sb("XT", [128, 8192], F32)
        FA_t = sb("FA", [128, 11264], F32)
        banks = [st.enter_context(nc.psum_tensor("ps%d" % i, [128, 512], F32)) for i in range(8)]

        def arena(name, t, lo, nbytes, dt, pat=None, **kw):
            assert lo % 4 == 0 and nbytes % 4 == 0
            ap = t[:, lo // 4:(lo + nbytes) // 4]
            if dt == BF16:
                ap = ap.bitcast(BF16)
            if pat is not None:
                ap = ap.rearrange(pat, **kw)
            return V(ap, [(name, lo, lo + nbytes)])

        xt = arena("XT", XT_t, 0, 32768, F32, "p (a b) -> p a b", a=4)
        xt_tc = [arena("XT", XT_t, tc * 8192, 8192, F32) for tc in range(4)]
        spb = arena("XT", XT_t, 0, 2048, F32)
        E1 = arena("XT", XT_t, 2048, 2048, F32)
        E2 = arena("XT", XT_t, 4096, 2048, F32)
        E3 = arena("XT", XT_t, 6144, 2048, F32)
        qd = [arena("XT", XT_t, 8192 + i * 2048, 2048, BF16, "p (a b) -> p a b", a=2) for i in range(2)]
        kd = [arena("XT", XT_t, 12288 + i * 2048, 2048, BF16, "p (a b) -> p a b", a=2) for i in range(2)]
        kteT = arena("XT", XT_t, 16384, 2048, BF16, "p (a b) -> p a b", a=2)
        kte = [arena("XT", XT_t, 18432 + i * 2048, 2048, BF16, "p (a b) -> p a b", a=4) for i in range(2)]
        vb = [arena("XT", XT_t, 22528 + i * 4096, 4096, BF16, "p (a b) -> p a b", a=4) for i in range(2)]
        actT = [arena("FA", FA_t, fc * 1024, 1024, BF16) for fc in range(NFC)]
        xn_tm = [arena("FA", FA_t, tc * 4096, 4096, BF16) for tc in range(4)]
        junk = arena("FA", FA_t, 16384, 4096, BF16)
        scT = [arena("FA", FA_t, i * 256, 256, BF16) for i in range(2)]
        o_sb = arena("FA", FA_t, 512, 8192, F32, "p (a b) -> p a b", a=4)
        o_cc = [arena("FA", FA_t, 512 + cc * 2048, 2048, F32) for cc in range(4)]
        sq = [arena("FA", FA_t, 8704 + i * 2048, 2048, F32) for i in range(2)]
        lnv = arena("FA", FA_t, 12800, 2048, F32)
        gc = [arena("FA", FA_t, 14848 + cc * 2048, 2048, F32) for cc in range(4)]
        cv = [arena("FA", FA_t, 23040 + cc * 2048, 2048, F32) for cc in range(4)]
        sg = arena("FA", FA_t, 31232, 2048, F32)
        sb2 = arena("FA", FA_t, 33280, 2048, F32)
        tmp = arena("FA", FA_t, 35328, 2048, F32)
        ubuf = [arena("FA", FA_t, 37376 + i * 2064, 2064, F32) for i in range(2)]

        ft = [V(mgT_t[:, 2 * i:2 * i + 2, :].rearrange("p a b -> p (a b)").bitcast(F32),
                [("mgT", 2 * i), ("mgT", 2 * i + 1)]) for i in range(8)]
        xpc = [V(mgT_t[:, 8 * i:8 * i + 8, :].rearrange("p a b -> p (a b)").bitcast(F32),
                 [("mgT", g) for g in range(8 * i, 8 * i + 8)]) for i in range(2)]
        xnp = [V(xnp_t[i][:], [("xnp", i)]) for i in range(2)]
        ssp = [V(ssp_t[:, i:i + 1], [("ssp", i)]) for i in range(2)]
        rsp = [V(rsp_t[:, i:i + 1], [("rsp", i)]) for i in range(2)]
        xnTalt = [arena("FA", FA_t, kc * 1024, 1024, BF16) for kc in range(NKC)]
        xnTalt_halo = V(FA_t[:, 0:4096].bitcast(BF16).rearrange("p (a b) -> p a b", a=NKC)[:, :, TT - 2:TT],
                        [("FA", 0, 16384)])
        am = V(am_t[:], ["am"])
        LD = V(LD_t[:], ["LD"])
        LDall = V(LDall_t[:], ["LDall"])
        Deff = V(Deff_t[:], ["Deff"])
        one8 = V(one8_t[:], ["one8"])
        Lbuf = [V(mgT_t[:, 8 * i:8 * i + 8, :].rearrange("p a b -> p (a b)").bitcast(F32).rearrange("p (a b) -> p a b", a=4),
                  [("mgT", g) for g in range(8 * i, 8 * i + 8)]) for i in range(2)]
        xnTalt_halo0 = V(FA_t[:, 0:4096].bitcast(BF16).rearrange("p (a b) -> p a b", a=NKC)[:, :, 126:128],
                         [("FA", 0, 16384)])
        xnT = [V(xnT_t[:, kc, :], [("xnT", kc)]) for kc in range(NKC)]
        mgT = [V(mgT_t[:, g, :], [("mgT", g)]) for g in range(NKC)]
        Sv = [V(S_t[:, i, :], [("S", i)]) for i in range(8)]
        Sbf = [V(Sbf_t[:, i, :], [("Sbf", i)]) for i in range(8)]
        ident_f = V(cst_t[:, 0:128], ["cst"])
        maskT = V(cst_t[:, 128:256], ["cst"])
        ones128 = V(cst_t[:, 256:384], ["cst"])
        identb = V(identb_t[:], ["identb"])
        ones512 = V(ones_t[:], ["ones512"])
        vecs = V(vecs_t[:], ["vecs"])
        negb = V(negb_t[:], ["negb"])
        fnwb = V(fnwb_t[:], ["fnwb"])
        wgk = V(wgk_t[:], ["wgk"])
        gkl = V(gkl_t[:], ["gkl"])
        uh = [V(uh_t[:, g, :], [("uh", g)]) for g in range(16)]
        xnh = V(xnh_t[:], ["xnh"])
        gch = [V(gch_t[:, cc, :], [("gch", cc)]) for cc in range(4)]
        ss = V(ss_t[:], ["ss"])
        rs = V(rs_t[:], ["rs"])
        nbl = V(nbl_t[:], ["nbl"])
        dl = [V(dl_t[:, i, :], [("dl", i)]) for i in range(8)]
        bankv = [V(banks[i][:], [("ps", i)]) for i in range(8)]
        bankb = [V(banks[i][:].bitcast(BF16), [("ps", i)]) for i in range(8)]

        def vcol(c):
            return vecs[:, c:c + 1]

        def emit_all(P, plan):
            state = {"bank": 0, "wi": 0, "issued": 0}
            rec = []

            def ACT(out, in_, func, bias=None, scale=None):
                kw = {}
                if bias is not None:
                    kw["bias"] = _a(bias)
                if scale is not None:
                    kw["scale"] = scale
                P.op("act", lambda e: e.activation(out=out.ap, in_=in_.ap, func=func, **kw),
                     reads=_k(in_, bias), writes=_k(out))

            def ACOPY(out, in_):
                P.op("act", lambda e: e.copy(out=out.ap, in_=in_.ap), reads=_k(in_), writes=_k(out))

            def TT_(out, in0, in1, op):
                P.op("dve", lambda e: e.tensor_tensor(out=out.ap, in0=in0.ap, in1=in1.ap, op=op),
                     reads=_k(in0, in1), writes=_k(out))

            def TS(out, in0, s1, op0):
                P.op("dve", lambda e: e.tensor_scalar(out=out.ap, in0=in0.ap, scalar1=_a(s1), scalar2=None, op0=op0),
                     reads=_k(in0, s1), writes=_k(out))

            def STT(out, in0, s, in1, op0, op1):
                P.op("dve", lambda e: e.scalar_tensor_tensor(out=out.ap, in0=in0.ap, scalar=_a(s), in1=in1.ap,
                                                             op0=op0, op1=op1),
                     reads=_k(in0, s, in1), writes=_k(out))

            def DCOPY(out, in_):
                P.op("dve", lambda e: e.tensor_copy(out=out.ap, in_=in_.ap), reads=_k(in_), writes=_k(out))

            def MM(out, lhsT, rhs, start, stop):
                P.op("pe", lambda e: e.matmul(out.ap, lhsT=lhsT.ap, rhs=rhs.ap, start=start, stop=stop),
                     reads=_k(lhsT, rhs), writes=_k(out))

            def TR(out, in_):
                P.op("pe", lambda e: e.transpose(out=out.ap, in_=in_.ap, identity=identb.ap),
                     reads=_k(in_, identb), writes=_k(out))

            def nbank():
                b = state["bank"]
                state["bank"] = (b + 1) % 8
                return b

            def issue(desc, idx):
                src, k0, nk, segs = desc
                slot = idx % 3
                dram = {"in": w_in, "out": w_out, "gu": w_gu, "dn": w_dn}[src]
                off = 0
                prev = list(P.readers.get(("w", slot), ()))
                lw = P.last_writer.get(("w", slot))
                if lw is not None:
                    prev.append(lw)
                for si, (c0, n) in enumerate(segs):
                    view = dram[k0 * 128:(k0 + nk) * 128, c0:c0 + n].rearrange("(kc p) c -> p kc c", p=128)
                    dst = wr_t[slot][:, 0:nk, off:off + n]
                    key = ("w", slot) if si == 0 else ("w2", slot)
                    P.op("pool", lambda e, dst=dst, view=view: e.dma_start(out=dst, in_=view),
                         writes=[key], dma_key=key, extra_deps=prev if si > 0 else ())
                    off += n

            def wblock(src, k0, nk, c0, n, c1=None, n1=0):
                segs = ((c0, n),) if c1 is None else ((c0, n), (c1, n1))
                n = n + n1
                desc = (src, k0, nk, segs)
                i = state["wi"]
                state["wi"] = i + 1
                if plan is None:
                    rec.append(desc)
                else:
                    assert plan[i] == desc, (i, plan[i], desc)
                    while state["issued"] < min(len(plan), i + 3):
                        issue(plan[state["issued"]], state["issued"])
                        state["issued"] += 1
                slot = i % 3
                return V(wr_t[slot][:, 0:nk, 0:n], [("w", slot)] + ([("w2", slot)] if len(segs) > 1 else []))

            def group_fm(W, cc, M=128, N=TT, extra_rhs=None, X=None):
                X = xnT if X is None else X
                b = nbank()
                out = bankv[b][0:M, 0:N]
                for kc in range(NKC):
                    MM(out, W[:, kc, cc * 128:cc * 128 + M], X[kc][:, 0:N], kc == 0, kc == NKC - 1)
                if extra_rhs is not None:
                    b2 = nbank()
                    out2 = bankv[b2][0:M, 0:2]
                    for kc in range(NKC):
                        MM(out2, W[:, kc, cc * 128:cc * 128 + M], extra_rhs[:, kc, :], kc == 0, kc == NKC - 1)
                    return out, out2
                return out

            def group_tm(W, src, tc, ncol=512):
                b = nbank()
                out = bankv[b][:, 0:ncol]
                n = len(src)
                for kc in range(n):
                    MM(out, src[kc][:, tc * 128:(tc + 1) * 128], W[:, kc, 0:ncol], kc == 0, kc == n - 1)
                return out

            if True:
                P.op("sp", lambda e: e.dma_start(out=cst_t[:], in_=cst_d), writes=["cst"], dma_key="c_cst")
                P.op("sp", lambda e: e.dma_start(out=vecs_t[:], in_=vecs_d), writes=["vecs"], dma_key="c_vecs")
                P.op("sp", lambda e: e.dma_start(out=fnwb_t[:], in_=fnwb_d), writes=["fnwb"], dma_key="c_fnwb")
                P.op("pool", lambda e: e.dma_start(out=wgk_t[:], in_=wgk_d), writes=["wgk"], dma_key="c_wgk")
                P.op("pool", lambda e: e.dma_start(out=wgl_t[:],
                                                   in_=w_in[:, C_GKL:C_GKL + 16].rearrange("(kc p) c -> p kc c", p=128)),
                     writes=["wgl"], dma_key="c_wgl")
                DCOPY(identb, ident_f)
                P.op("dve", lambda e: e.memset(ones_t[:], 1.0), writes=["ones512"])
                for i in range(8):
                    P.op("dve", lambda e, i=i: e.memset(S_t[:, i, :], 0.0), writes=[("S", i)])
                TS(negb, vecs[:, 84:92], -1.0, ALU.mult)
                P.op("sp", lambda e: e.dma_start(out=am_t[:], in_=am_d), writes=["am"], dma_key="c_am")
                P.op("dve", lambda e: e.memset(LD_t[:], 0.0), writes=["LD"])
                P.op("dve", lambda e: e.memset(one8_t[:], 1.0), writes=["one8"])

            def load_x(row0):
                P.op("sp", lambda e: e.dma_start(out=xt.ap, in_=xs[row0:row0 + TT, :].rearrange("(tc p) d -> p tc d", p=128)),
                     writes=_k(xt), dma_key="x")

            def norm_to_T(wcol0, dst):
                for tc in range(4):
                    P.op("dve", lambda e, tc=tc: e.scalar_tensor_tensor(
                        out=junk.ap, in0=xt_tc[tc].ap, scalar=1.0, in1=xt_tc[tc].ap,
                        op0=ALU.mult, op1=ALU.mult, accum_out=ss_t[:, tc:tc + 1]),
                        reads=_k(xt_tc[tc]), writes=_k(junk, ss))
                ACT(rs, ss, AF.Ln, bias=EPS, scale=1.0 / D)
                ACT(rs, rs, AF.Exp, scale=-0.5)
                for tc in range(4):
                    TS(xn_tm[tc], xt_tc[tc], rs[:, tc:tc + 1], ALU.mult)
                for kc in range(NKC):
                    b = nbank()
                    for tc in range(4):
                        TR(bankb[b][:, tc * 128:(tc + 1) * 128], xn_tm[tc][:, kc * 128:(kc + 1) * 128])
                    TS(dst[kc], bankb[b][:, 0:TT], vcol(wcol0 + kc), ALU.mult)

            def s0_norm(src, j):
                P.op("dve", lambda e: e.scalar_tensor_tensor(
                    out=xnp[j].ap, in0=src.ap, scalar=1.0, in1=src.ap,
                    op0=ALU.mult, op1=ALU.mult, accum_out=ssp[j].ap),
                    reads=_k(src), writes=_k(xnp[j], ssp[j]))
                ACT(rsp[j], ssp[j], AF.Ln, bias=EPS, scale=1.0 / D)
                ACT(rsp[j], rsp[j], AF.Exp, scale=-0.5)
                TS(xnp[j], src, rsp[j], ALU.mult)

            def s0_load(row0, tc, j):
                r0 = row0 + tc * 128
                P.op("sp", lambda e: e.dma_start(out=xpc[j].ap, in_=xs[r0:r0 + 128, :]),
                     writes=_k(xpc[j]), dma_key=("xp", j))
                s0_norm(xpc[j], j)

            def s0_tr(tc, j, dst, wcol0):
                for kq in range(4):
                    b = nbank()
                    for i in range(4):
                        kc = kq * 4 + i
                        TR(bankb[b][:, i * 128:(i + 1) * 128], xnp[j][:, kc * 128:(kc + 1) * 128])
                    for i in range(4):
                        kc = kq * 4 + i
                        TS(dst[kc][:, tc * 128:(tc + 1) * 128], bankb[b][:, i * 128:(i + 1) * 128],
                           vcol(wcol0 + kc), ALU.mult)

            def final_stats():
                for tc in range(4):
                    P.op("dve", lambda e, tc=tc: e.scalar_tensor_tensor(
                        out=junk.ap, in0=xt_tc[tc].ap, scalar=1.0, in1=xt_tc[tc].ap,
                        op0=ALU.mult, op1=ALU.mult, accum_out=ss_t[:, tc:tc + 1]),
                        reads=_k(xt_tc[tc]), writes=_k(junk, ss))
                ACT(rs, ss, AF.Ln, bias=EPS, scale=1.0 / D)
                ACT(rs, rs, AF.Exp, scale=-0.5)

            def gk_low(X=None):
                W = V(wgl_t[:], ["wgl"])
                ps = group_fm(W, 0, M=16, X=X)
                ACOPY(gkl, ps)

            def decay_common(hd):
                b = nbank()
                ps = bankv[b]
                MM(ps, wgk[:, hd * 128:(hd + 1) * 128], gkl, True, True)
                ACT(spb, ps, AF.Exp, bias=negb[:, hd:hd + 1], scale=-1.0)
                ACT(spb, spb, AF.Ln, bias=1.0, scale=1.0)

            def scan(lo, hi):
                P.op("dve", lambda e: e.tensor_tensor_scan(out=E2.ap[:, lo:hi], data0=ones512.ap[:, lo:hi],
                                                           data1=spb.ap[:, lo:hi], initial=0.0,
                                                           op0=ALU.mult, op1=ALU.add),
                     reads=_k(spb, ones512), writes=_k(E2))

            def kt_and_v(h, pp, Wv, X=None):
                X = xnT if X is None else X
                for tc in range(4):
                    ps = group_tm(Wv, X, tc)
                    ACOPY(vb[pp][:, tc, :], ps)
                b = nbank()
                for tc in range(4):
                    for dkc in range(2):
                        TR(bankb[b][:, (tc * 2 + dkc) * 128:(tc * 2 + dkc + 1) * 128],
                           kteT[:, dkc, tc * 128:(tc + 1) * 128])
                P.op("act", lambda e: e.copy(out=kte[pp].ap, in_=bankb[b].ap.rearrange("p (a b) -> p a b", a=4)),
                     reads=_k(bankb[b]), writes=_k(kte[pp]))

            def prefix_tile(pt, last):
                X = xnT if pt % 2 == 0 else xnTalt
                Xn = xnT if (pt + 1) % 2 == 0 or last else xnTalt
                nrow0 = MROW0 if last else XROW0 + (pt + 1) * TT
                gk_low(X)
                for h in range(4):
                    s0_load(nrow0, h, h % 2)
                    if h >= 1:
                        s0_tr(h - 1, (h - 1) % 2, Xn, 0)
                    Wk = wblock("in", 0, NKC, C_K + h * 256, 256)
                    for dkc in range(2):
                        hd = h * 2 + dkc
                        decay_common(hd)
                        scan(0, TT)
                        TS(nbl[:, 0:1], E2[:, TT - 1:TT], -1.0 / 16, ALU.mult)
                        TT_(LD[:, hd:hd + 1], LD[:, hd:hd + 1], nbl[:, 0:1], ALU.add)
                        ACT(E3, E2, AF.Exp, bias=nbl[:, 0:1], scale=1.0 / 16)
                        ACT(dl[hd][:, 0:1], nbl[:, 0:1], AF.Exp)
                        ps = group_fm(Wk, dkc, X=X)
                        TT_(kteT[:, dkc, :], ps, E3, ALU.mult)
                    Wv = wblock("in", 0, NKC, C_V + h * 512, 512)
                    kt_and_v(h, 0, Wv, X)
                    for dkc in range(2):
                        hd = h * 2 + dkc
                        b = nbank()
                        for tc in range(4):
                            MM(bankv[b], kte[0][:, tc, dkc * 128:(dkc + 1) * 128], vb[0][:, tc, :], tc == 0, tc == 3)
                        STT(Sv[hd], Sv[hd], dl[hd][:, 0:1], bankv[b], ALU.mult, ALU.add)
                s0_tr(3, 1, Xn, 0)

            def qkv(h, pp):
                Wqk = wblock("in", 0, NKC, C_Q + h * 256, 256, C_K + h * 256, 256)
                for dkc in range(2):
                    hd = h * 2 + dkc
                    ps_q = group_fm(Wqk, dkc)
                    decay_common(hd)
                    for tc in range(4):
                        scan(tc * 128, (tc + 1) * 128)
                    ACT(E1, E2, AF.Exp, scale=-1.0 / 16)
                    TS(nbl, V(E2.ap.rearrange("p (a b) -> p a b", a=4)[:, :, 127], E2.keys), -1.0 / 16, ALU.mult)
                    for tc in range(4):
                        ACT(E3[:, tc * 128:(tc + 1) * 128], E2[:, tc * 128:(tc + 1) * 128], AF.Exp,
                            bias=nbl[:, tc:tc + 1], scale=1.0 / 16)
                    ACT(E2, E2, AF.Exp, scale=1.0 / 16)
                    ACT(dl[hd], nbl, AF.Exp)
                    STT(qd[pp][:, dkc, :], ps_q, 1.0 / 16, E1, ALU.mult, ALU.mult)
                    ps = group_fm(Wqk, 2 + dkc)
                    TT_(kd[pp][:, dkc, :], ps, E2, ALU.mult)
                    TT_(kteT[:, dkc, :], ps, E3, ALU.mult)
                Wv = wblock("in", 0, NKC, C_V + h * 512, 512)
                kt_and_v(h, pp, Wv)

            def gla_A(h, pp, tc):
                tsl = slice(tc * 128, (tc + 1) * 128)
                b = nbank()
                sc = bankv[b][:, 0:128]
                for dkc in range(2):
                    MM(sc, kd[pp][:, dkc, tsl], qd[pp][:, dkc, tsl], dkc == 0, dkc == 1)
                TT_(scT[tc % 2], sc, maskT, ALU.mult)

            def gla_B(h, pp, tc):
                tsl = slice(tc * 128, (tc + 1) * 128)
                s_ = scT[tc % 2]
                b = nbank()
                for dvc in range(4):
                    dsl = slice(dvc * 128, (dvc + 1) * 128)
                    o = bankv[b][:, dsl]
                    MM(o, vb[pp][:, tc, dsl], s_, True, False)
                    MM(o, Sbf[h * 2][:, dsl], qd[pp][:, 0, tsl], False, False)
                    MM(o, Sbf[h * 2 + 1][:, dsl], qd[pp][:, 1, tsl], False, True)
                P.op("act", lambda e, b=b, tsl=tsl: e.copy(out=o_sb.ap[:, :, tsl],
                                                           in_=bankv[b].ap.rearrange("p (a b) -> p a b", a=4)),
                     reads=_k(bankv[b]), writes=_k(o_sb))
                for dkc in range(2):
                    hd = h * 2 + dkc
                    b = nbank()
                    MM(bankv[b], kte[pp][:, tc, dkc * 128:(dkc + 1) * 128], vb[pp][:, tc, :], True, True)
                    STT(Sv[hd], Sv[hd], dl[hd][:, tc:tc + 1], bankv[b], ALU.mult, ALU.add)
                    ACOPY(Sbf[hd], Sv[hd])

            def gla_norm(h):
                b = nbank()
                for dvc in range(4):
                    ACT(sq[dvc % 2], o_cc[dvc], AF.Square)
                    MM(bankv[b], ones128, sq[dvc % 2], dvc == 0, dvc == 3)
                ACT(lnv, bankv[b], AF.Ln, bias=EPS, scale=1.0 / 512)
                ACT(lnv, lnv, AF.Exp, scale=-0.5)
                for dvc in range(4):
                    STT(o_cc[dvc], o_cc[dvc], vcol(80 + dvc), lnv, ALU.mult, ALU.mult)

            def gla(h, pp):
                for tc in range(4):
                    gla_A(h, pp, tc)
                    gla_B(h, pp, tc)

            def bmix(h, first_tile, hook=None):
                cnt = [0]

                def tick():
                    cnt[0] += 1
                    if hook is not None:
                        hook(cnt[0])
                Wgc = wblock("in", 0, NKC, C_GC + h * 512, 512)
                for cc in range(4):
                    if first_tile:
                        ps, psh = group_fm(Wgc, cc, extra_rhs=xnh)
                        ACOPY(gch[cc], psh)
                    else:
                        ps = group_fm(Wgc, cc)
                    ACOPY(gc[cc], ps)
                    tick()
                Wxc = wblock("in", 0, NKC, C_XC + h * 512, 512)
                for cc in range(4):
                    g = h * 4 + cc
                    u = ubuf[cc % 2]
                    if first_tile:
                        ps, psh = group_fm(Wxc, cc, extra_rhs=xnh)
                        TT_(uh[g], psh, gch[cc], ALU.mult)
                    else:
                        ps = group_fm(Wxc, cc)
                    DCOPY(u[:, 0:2], uh[g])
                    TT_(u[:, 2:514], ps, gc[cc], ALU.mult)
                    TS(cv[cc], u[:, 2:514], vcol(32 + g * 3 + 2), ALU.mult)
                    STT(cv[cc], u[:, 1:513], vcol(32 + g * 3 + 1), cv[cc], ALU.mult, ALU.add)
                    STT(cv[cc], u[:, 0:512], vcol(32 + g * 3 + 0), cv[cc], ALU.mult, ALU.add)
                    DCOPY(uh[g], u[:, 512:514])
                    tick()
                Wgb = wblock("in", 0, NKC, C_GB + h * 512, 512)
                for cc in range(4):
                    ps = group_fm(Wgb, cc)
                    TT_(cv[cc], ps, cv[cc], ALU.mult)
                    tick()

            def bmix_gate(h):
                Wgo = wblock("in", 0, NKC, C_GO + h * 512, 512)
                for cc in range(4):
                    ps = group_fm(Wgo, cc)
                    if cc == 0:
                        gla_norm(h)
                    ACT(sg, ps, AF.Sigmoid)
                    TT_(tmp, ps, sg, ALU.mult)
                    TT_(o_cc[cc], tmp, o_cc[cc], ALU.mult)
                Wma = wblock("in", 0, NKC, C_MA + h * 512, 512)
                for cc in range(4):
                    ps = group_fm(Wma, cc)
                    ACT(sg, ps, AF.Sigmoid)
                    TT_(o_cc[cc], o_cc[cc], sg, ALU.mult)
                Wmb = wblock("in", 0, NKC, C_MB + h * 512, 512)
                for cc in range(4):
                    g = h * 4 + cc
                    ps = group_fm(Wmb, cc)
                    ACT(sb2, ps, AF.Sigmoid)
                    TT_(cv[cc], cv[cc], sb2, ALU.mult)
                    TT_(mgT[g], cv[cc], o_cc[cc], ALU.add)

            def exchange_start():
                P.op("sp", lambda e: e.dma_start(out=lsrc[0].ap(), in_=S_t[:, 0:4, :].rearrange("p a b -> p (a b)")),
                     reads=[("S", i) for i in range(4)], writes=["lsrc0"], dma_key="ls0")
                P.op("sp", lambda e: e.dma_start(out=lsrc[1].ap(), in_=S_t[:, 4:8, :].rearrange("p a b -> p (a b)")),
                     reads=[("S", i) for i in range(4, 8)], writes=["lsrc1"], dma_key="ls1")
                P.op("sp", lambda e: e.dma_start(out=lsrc[2].ap(), in_=LD_t[:]), reads=["LD"], writes=["lsrc2"],
                     dma_key="ls2")
                prior = [o.id for o in P.by_eng["pool"] if o.dma][-3:]
                ccs = []
                for i in range(3):
                    ccs.append(P.op("pool", lambda e, i=i: e.collective_compute(
                        "AllGather", ALU.bypass, replica_groups=[[0, 1, 2, 3], [4, 5, 6, 7]],
                        ins=[lsrc[i].ap()], outs=[lall[i].ap()]),
                        reads=["lsrc%d" % i], writes=["lall%d" % i], dma_key="cc%d" % i, inc=1,
                        extra_deps=prior))

            def combine():
                for i in range(8):
                    P.op("dve", lambda e, i=i: e.memset(S_t[:, i, :], 0.0), writes=[("S", i)])
                P.op("sp", lambda e: e.dma_start(out=LDall_t[:], in_=lall[2].ap().rearrange("(r p) c -> p r c", p=128)),
                     reads=["lall2"], writes=["LDall"], dma_key="ld_all")
                ACT(LDall, LDall, AF.Exp)
                for j in range(3):
                    for half in range(2):
                        Lb = Lbuf[half]
                        P.op("sp", lambda e, j=j, Lb=Lb, half=half: e.dma_start(
                            out=Lb.ap, in_=lall[half].ap()[j * 128:(j + 1) * 128, :].rearrange("p (a b) -> p a b", a=4)),
                            reads=["lall%d" % half], writes=_k(Lb), dma_key=("lb", half))
                    TS(Deff, LDall[:, j, 0:8], -1.0, ALU.add)
                    STT(Deff, Deff, am[:, j:j + 1], one8, ALU.mult, ALU.add)
                    for hd in range(8):
                        TS(Sv[hd], Sv[hd], Deff[:, hd:hd + 1], ALU.mult)
                        STT(Sv[hd], Lbuf[hd // 4][:, hd % 4, :], am[:, j:j + 1], Sv[hd], ALU.mult, ALU.add)
                for i in range(8):
                    ACOPY(Sbf[i], Sv[i])

            def main_tile(mt):
                row0 = MROW0 + mt * TT
                gk_low()
                qkv(0, 0)
                for h in range(4):
                    if h + 1 < 4:
                        qkv(h + 1, (h + 1) % 2)
                    pp = h % 2
                    if USE_CC and mt == 0 and h == 0:
                        bmix(h, True)
                        combine()
                        gla(h, pp)
                    else:
                        gla_A(h, pp, 0)

                        def hook(n, h=h, pp=pp):
                            if n % 3 == 0:
                                tc = n // 3 - 1
                                gla_B(h, pp, tc)
                                if tc + 1 < 4:
                                    gla_A(h, pp, tc + 1)
                        bmix(h, mt == 0, hook)
                    if h == 3:
                        load_x(row0)
                    bmix_gate(h)
                for db in range(4):
                    Wo = wblock("out", 0, NKC, db * 512, 512)
                    for tc in range(4):
                        ps = group_tm(Wo, mgT, tc)
                        dsl = slice(db * 512, (db + 1) * 512)
                        TT_(xt_tc[tc][:, dsl], ps, xt_tc[tc][:, dsl], ALU.add)
                        if db == 3:
                            s0_norm(xt_tc[tc], tc % 2)
                            if tc >= 1:
                                s0_tr(tc - 1, (tc - 1) % 2, xnT, 16)
                s0_tr(3, 1, xnT, 16)
                for fb in range(11):
                    Wg = wblock("gu", 0, NKC, fb * 512, 512)
                    for cc in range(4):
                        ps = group_fm(Wg, cc)
                        ACT(ft[4 + cc % 2], ps, AF.Sigmoid)
                        TT_(ft[cc], ps, ft[4 + cc % 2], ALU.mult)
                    Wu = wblock("gu", 0, NKC, FF + fb * 512, 512)
                    for cc in range(4):
                        ps = group_fm(Wu, cc)
                        TT_(actT[fb * 4 + cc], ps, ft[cc], ALU.mult)
                for db in range(4):
                    bs = [nbank() for _ in range(4)]
                    subs = [(0, 16), (16, 16), (32, 12)]
                    for (k0, nk) in subs:
                        Wd = wblock("dn", k0, nk, db * 512, 512)
                        for tc in range(4):
                            for j in range(nk):
                                fc = k0 + j
                                MM(bankv[bs[tc]], actT[fc][:, tc * 128:(tc + 1) * 128], Wd[:, j, :],
                                   fc == 0, fc == NFC - 1)
                    for tc in range(4):
                        dsl = slice(db * 512, (db + 1) * 512)
                        TT_(xt_tc[tc][:, dsl], bankv[bs[tc]], xt_tc[tc][:, dsl], ALU.add)
                        if db == 3 and mt == NMAIN - 1:
                            j = tc % 2
                            P.op("dve", lambda e, tc=tc, j=j: e.scalar_tensor_tensor(
                                out=xnp[j].ap, in0=xt_tc[tc].ap, scalar=1.0, in1=xt_tc[tc].ap,
                                op0=ALU.mult, op1=ALU.mult, accum_out=ssp[j].ap),
                                reads=_k(xt_tc[tc]), writes=_k(xnp[j], ssp[j]))
                            ACT(rsp[j], ssp[j], AF.Ln, bias=EPS, scale=1.0 / D)
                            ACT(rsp[j], rsp[j], AF.Exp, scale=-0.5)
                            STT(xt_tc[tc], xt_tc[tc], rsp[j], fnwb, ALU.mult, ALU.mult)
                            r0 = mt * TT + tc * 128
                            P.op("sp", lambda e, tc=tc, r0=r0: e.dma_start(out=out_d[r0:r0 + 128, :], in_=xt_tc[tc].ap),
                                 reads=_k(xt_tc[tc]), dma_key=("o", mt))
                    if mt + 1 < NMAIN:
                        s0_load(row0 + TT, db, db % 2)
                        if db >= 1:
                            s0_tr(db - 1, (db - 1) % 2, xnT, 0)
                if mt + 1 < NMAIN:
                    s0_tr(3, 1, xnT, 0)
                if mt == NMAIN - 1:
                    return
                final_stats()
                for tc in range(4):
                    STT(xt_tc[tc], xt_tc[tc], rs[:, tc:tc + 1], fnwb, ALU.mult, ALU.mult)
                P.op("sp", lambda e: e.dma_start(out=out_d[mt * TT:(mt + 1) * TT, :].rearrange("(tc p) d -> p tc d", p=128),
                                                 in_=xt.ap),
                     reads=_k(xt), dma_key=("o", mt))

            s0_load(0, 0, 0)
            s0_tr(0, 0, xnTalt, 0)
            DCOPY(xnh, xnTalt_halo0)
            for tc in range(4):
                s0_load(XROW0, tc, (tc + 1) % 2)
                s0_tr(tc, (tc + 1) % 2, xnT, 0)
            for pt in range(NPRE):
                prefix_tile(pt, pt == NPRE - 1)
            if USE_CC:
                exchange_start()
            else:
                for i in range(8):
                    ACOPY(Sbf[i], Sv[i])
            for mt in range(NMAIN):
                main_tile(mt)
            P.op("sp", lambda e: None, extra_deps=[o.id for o in P.ops if o.dma and o.eng == "sp"])
            return rec

        plan = emit_all(Prog(nc), None)
        P = Prog(nc)
        emit_all(P, plan)
        P.emit(st)
        print("ops", len(P.ops), "sems", P.n_sems, "max_tick", P.max_tick, "wblocks", len(plan))
    return nc


def _consts():
    c = np.zeros((128, 384), np.float32)
    c[:, 0:128] = np.eye(128, dtype=np.float32)
    j = np.arange(128)[:, None]
    i = np.arange(128)[None, :]
    c[:, 128:256] = (j <= i).astype(np.float32)
    c[:, 256:384] = 1.0
    return c


def kernel(x, mix_norm_w, w_in, w_gk_up, b_gk_up, gla_norm_w, conv_w, w_out,
           ffn_norm_w, w_gate_up, w_down, final_norm_w):
    x = np.asarray(x, np.float32)
    B, S, _ = x.shape
    vecs = np.zeros((128, 92), np.float32)
    vecs[:, 0:16] = np.asarray(mix_norm_w, np.float32)[0].reshape(16, 128).T
    vecs[:, 16:32] = np.asarray(ffn_norm_w, np.float32)[0].reshape(16, 128).T
    cw = np.asarray(conv_w, np.float32)[0]
    vecs[:, 32:80] = cw.reshape(3, 16, 128).transpose(2, 1, 0).reshape(128, 48)
    vecs[:, 80:84] = np.asarray(gla_norm_w, np.float32)[0].reshape(4, 128).T
    vecs[:, 84:92] = np.asarray(b_gk_up, np.float32)[0].reshape(8, 128).T
    fnwb = np.ascontiguousarray(np.broadcast_to(np.asarray(final_norm_w, np.float32)[None, :], (128, D)))
    shared = {
        "w_in": np.ascontiguousarray(np.asarray(w_in, np.float32)[0]),
        "w_out": np.ascontiguousarray(np.asarray(w_out, np.float32)[0]),
        "w_gu": np.ascontiguousarray(np.asarray(w_gate_up, np.float32)[0]),
        "w_dn": np.ascontiguousarray(np.asarray(w_down, np.float32)[0]),
        "wgk": np.ascontiguousarray(np.asarray(w_gk_up, np.float32)[0]),
        "vecs": vecs, "fnwb": fnwb, "cst": _consts(),
    }
    in_maps = []
    own = NMAIN * TT
    for c in range(8):
        b, p = c // 4, c % 4
        xs = np.zeros((MROW0 + own, D), np.float32)
        start = p * own
        if start > 0:
            xs[0:XROW0] = x[b, start - XROW0:start]
            if not USE_CC:
                xs[MROW0 - start:MROW0] = x[b, 0:start]
        xs[MROW0:] = x[b, start:start + own]
        am = np.zeros((128, 4), np.float32)
        am[:, 0:p] = 1.0
        m = dict(shared)
        m["xs"] = xs
        m["am"] = am
        in_maps.append(m)
    nc = build_nc()
    res = run_bass_kernel_spmd(nc, in_maps, core_ids=list(range(8)))
    out = np.zeros((B, S, D), np.float32)
    for c in range(8):
        b, p = c // 4, c % 4
        out[b, p * own:(p + 1) * own] = res.results[c]["out"]
    return out
```

```python
import numpy as np
from contextlib import ExitStack
import concourse.bass as bass
import concourse.mybir as mybir
from concourse.bass_utils import run_bass_kernel_spmd
from concourse.alu_op_type import AluOpType as ALU

F32 = mybir.dt.float32
BF16 = mybir.dt.bfloat16
AF = mybir.ActivationFunctionType

D = 2048
NKC = 16
TT = 512
NTC = 4
FF = 5632
NFC = 44
USE_CC = True
NPRE = 4 if USE_CC else 12
XROW0 = 128
MROW0 = XROW0 if USE_CC else XROW0 + NPRE * TT
NMAIN = 4
EPS = 1e-6
C_Q, C_K, C_V, C_GO, C_GKL, C_GB, C_GC, C_XC, C_MA, C_MB = 0, 1024, 2048, 4096, 6144, 6160, 8208, 10256, 12304, 14352
ENGS = ("pe", "act", "dve", "pool", "sp")
ARENAS = ("XT", "FA")


class Op:
    __slots__ = ("id", "eng", "fn", "dma", "deps", "sem", "tick", "signal", "inc")

    def __init__(self, id, eng, fn, dma):
        self.id = id
        self.eng = eng
        self.fn = fn
        self.dma = dma
        self.deps = set()
        self.sem = None
        self.tick = None
        self.signal = False
        self.inc = 16 if dma else 1


class Prog:
    def __init__(self, nc):
        self.nc = nc
        self.ops = []
        self.by_eng = {e: [] for e in ENGS}
        self.last_writer = {}
        self.readers = {}
        self.arena_keys = {a: [] for a in ARENAS}
        self.dma_keys = {}

    def _expand(self, k):
        if isinstance(k, tuple) and k and k[0] in ARENAS:
            lst = self.arena_keys[k[0]]
            if k not in self.last_writer and k not in self.readers:
                lst.append(k)
                self.readers[k] = []
            return [o for o in lst if o[1] < k[2] and k[1] < o[2]]
        return [k]

    def op(self, eng, fn, reads=(), writes=(), dma_key=None, extra_deps=(), inc=None):
        o = Op(len(self.ops), eng, fn, dma_key is not None)
        deps = o.deps
        for r in reads:
            for k in self._expand(r):
                w = self.last_writer.get(k)
                if w is not None:
                    deps.add(w)
        for w_ in writes:
            for k in self._expand(w_):
                w = self.last_writer.get(k)
                if w is not None:
                    deps.add(w)
                rl = self.readers.get(k)
                if rl:
                    deps.update(rl)
        deps.update(extra_deps)
        deps.discard(o.id)
        for r in reads:
            self.readers.setdefault(r, []).append(o.id)
        for w_ in writes:
            for k in self._expand(w_):
                self.last_writer[k] = o.id
                self.readers[k] = []
        if dma_key is not None:
            c = self.dma_keys.get(dma_key, 0) + 1
            self.dma_keys[dma_key] = c
            if inc is not None:
                o.inc = inc
            o.sem = ("dma", dma_key)
            o.tick = o.inc * c
            o.signal = True
        self.ops.append(o)
        self.by_eng[eng].append(o)
        return o.id

    def emit(self, stack):
        nc = self.nc
        ops = self.ops
        for o in ops:
            nd = set()
            for d in o.deps:
                p = ops[d]
                if (not p.dma) and p.eng == o.eng and o.eng == "pe":
                    continue
                nd.add(d)
            o.deps = nd
            for d in nd:
                ops[d].signal = True
        cnt = {e: 0 for e in ENGS}
        for e in ENGS:
            for o in self.by_eng[e]:
                if o.dma:
                    continue
                if o.signal:
                    cnt[e] += 1
                    o.sem = ("eng", e)
                    o.tick = cnt[e]
        sems = {}
        for e in ENGS:
            if cnt[e] > 0:
                sems[("eng", e)] = stack.enter_context(nc.semaphore("prog_" + e))
        for i, k in enumerate(self.dma_keys):
            sems[("dma", k)] = stack.enter_context(nc.semaphore("dma_%d" % i))
        self.n_sems = len(sems)
        self.max_tick = max([cnt[e] for e in ENGS] + [16 * c for c in self.dma_keys.values()] + [0])
        block = stack.enter_context(nc.Block())

        def run(eng_name, eng):
            waited = {}
            for o in self.by_eng[eng_name]:
                need = {}
                for d in o.deps:
                    p = ops[d]
                    if need.get(p.sem, 0) < p.tick:
                        need[p.sem] = p.tick
                for s, v in need.items():
                    if waited.get(s, 0) >= v:
                        continue
                    eng.wait_ge(sems[s], v)
                    waited[s] = v
                inst = o.fn(eng)
                if o.signal:
                    inst.then_inc(sems[o.sem], o.inc)

        @block.tensor
        def _(e):
            run("pe", e)

        @block.scalar
        def _(e):
            run("act", e)

        @block.vector
        def _(e):
            run("dve", e)

        @block.gpsimd
        def _(e):
            run("pool", e)

        @block.sync
        def _(e):
            run("sp", e)


class V:
    __slots__ = ("ap", "keys")

    def __init__(self, ap, keys):
        self.ap = ap
        self.keys = tuple(keys)

    def __getitem__(self, idx):
        return V(self.ap[idx], self.keys)


def _k(*vs):
    out = []
    for v in vs:
        if isinstance(v, V):
            out.extend(v.keys)
    return out


def _a(v):
    return v.ap if isinstance(v, V) else v


def build_nc():
    nc = bass.Bass("TRN2", target_bir_lowering=False)
    xs = nc.dram_tensor("xs", [MROW0 + NMAIN * TT, D], F32, kind="ExternalInput").ap()
    am_d = nc.dram_tensor("am", [128, 4], F32, kind="ExternalInput").ap()
    lsrc = [nc.dram_tensor("lsrc%d" % i, [128, 2048], F32) for i in range(2)] + [nc.dram_tensor("lsrc2", [128, 32], F32)]
    lall = [nc.dram_tensor("lall%d" % i, [512, 2048], F32) for i in range(2)] + [nc.dram_tensor("lall2", [512, 32], F32)]
    w_in = nc.dram_tensor("w_in", [D, 16400], F32, kind="ExternalInput").ap()
    w_out = nc.dram_tensor("w_out", [D, D], F32, kind="ExternalInput").ap()
    w_gu = nc.dram_tensor("w_gu", [D, 2 * FF], F32, kind="ExternalInput").ap()
    w_dn = nc.dram_tensor("w_dn", [FF, D], F32, kind="ExternalInput").ap()
    wgk_d = nc.dram_tensor("wgk", [16, 1024], F32, kind="ExternalInput").ap()
    vecs_d = nc.dram_tensor("vecs", [128, 92], F32, kind="ExternalInput").ap()
    fnwb_d = nc.dram_tensor("fnwb", [128, D], F32, kind="ExternalInput").ap()
    cst_d = nc.dram_tensor("cst", [128, 384], F32, kind="ExternalInput").ap()
    out_d = nc.dram_tensor("out", [NMAIN * TT, D], F32, kind="ExternalOutput").ap()

    with ExitStack() as st:
        def sb(name, shape, dt):
            return st.enter_context(nc.sbuf_tensor("s_" + name, shape, dt))

        xnT_t = sb("xnT", [128, NKC, TT], BF16)
        mgT_t = sb("mgT", [128, NKC, TT], BF16)
        wr_t = [sb("wr%d" % i, [128, NKC, 512], BF16) for i in range(3)]
        S_t = sb("S", [128, 8, 512], F32)
        Sbf_t = sb("Sbf", [128, 8, 512], BF16)
        cst_t = sb("cst", [128, 384], F32)
        identb_t = sb("identb", [128, 128], BF16)
        ones_t = sb("ones512", [128, 512], F32)
        vecs_t = sb("vecs", [128, 92], F32)
        negb_t = sb("negb", [128, 8], F32)
        fnwb_t = sb("fnwb", [128, D], F32)
        wgk_t = sb("wgkb", [16, 1024], BF16)
        gkl_t = sb("gkl", [16, TT], BF16)
        wgl_t = sb("wgl", [128, NKC, 16], BF16)
        uh_t = sb("uh", [128, 16, 2], F32)
        xnh_t = sb("xnh", [128, NKC, 2], BF16)
        gch_t = sb("gch", [128, 4, 2], F32)
        ss_t = sb("ss", [128, 4], F32)
        rs_t = sb("rs", [128, 4], F32)
        nbl_t = sb("nbl", [128, 4], F32)
        dl_t = sb("dl", [128, 8, 4], F32)
        am_t = sb("am", [128, 4], F32)
        LD_t = sb("LD", [128, 32], F32)
        LDall_t = sb("LDall", [128, 4, 32], F32)
        Deff_t = sb("Deff", [128, 8], F32)
        one8_t = sb("one8", [128, 8], F32)
        xnp_t = [sb("xnp%d" % i, [128, D], BF16) for i in range(2)]
        ssp_t = sb("ssp", [128, 2], F32)
        rsp_t = sb("rsp", [128, 2], F32)
        XT_t = sb("XT", [128, 8192], F32)
        FA_t = sb("FA", [128, 11264], F32)
        banks = [st.enter_context(nc.psum_tensor("ps%d" % i, [128, 512], F32)) for i in range(8)]

        def arena(name, t, lo, nbytes, dt, pat=None, **kw):
            assert lo % 4 == 0 and nbytes % 4 == 0
            ap = t[:, lo // 4:(lo + nbytes) // 4]
            if dt == BF16:
                ap = ap.bitcast(BF16)
            if pat is not None:
                ap = ap.rearrange(pat, **kw)
            return V(ap, [(name, lo, lo + nbytes)])

        xt = arena("XT", XT_t, 0, 32768, F32, "p (a b) -> p a b", a=4)
        xt_tc = [arena("XT", XT_t, tc * 8192, 8192, F32) for tc in range(4)]
        spb = arena("XT", XT_t, 0, 2048, F32)
        E1 = arena("XT", XT_t, 2048, 2048, F32)
        E2 = arena("XT", XT_t, 4096, 2048, F32)
        E3 = arena("XT", XT_t, 6144, 2048, F32)
        qd = [arena("XT", XT_t, 8192 + i * 2048, 2048, BF16, "p (a b) -> p a b", a=2) for i in range(2)]
        kd = [arena("XT", XT_t, 12288 + i * 2048, 2048, BF16, "p (a b) -> p a b", a=2) for i in range(2)]
        kteT = arena("XT", XT_t, 16384, 2048, BF16, "p (a b) -> p a b", a=2)
        kte = [arena("XT", XT_t, 18432 + i * 2048, 2048, BF16, "p (a b) -> p a b", a=4) for i in range(2)]
        vb = [arena("XT", XT_t, 22528 + i * 4096, 4096, BF16, "p (a b) -> p a b", a=4) for i in range(2)]
        actT = [arena("FA", FA_t, fc * 1024, 1024, BF16) for fc in range(NFC)]
        xn_tm = [arena("FA", FA_t, tc * 4096, 4096, BF16) for tc in range(4)]
        junk = arena("FA", FA_t, 16384, 4096, BF16)
        scT = [arena("FA", FA_t, i * 256, 256, BF16) for i in range(2)]
        o_sb = arena("FA", FA_t, 512, 8192, F32, "p (a b) -> p a b", a=4)
        o_cc = [arena("FA", FA_t, 512 + cc * 2048, 2048, F32) for cc in range(4)]
        sq = [arena("FA", FA_t, 8704 + i * 2048, 2048, F32) for i in range(2)]
        lnv = arena("FA", FA_t, 12800, 2048, F32)
        gc = [arena("FA", FA_t, 14848 + cc * 2048, 2048, F32) for cc in range(4)]
        cv = [arena("FA", FA_t, 23040 + cc * 2048, 2048, F32) for cc in range(4)]
        sg = arena("FA", FA_t, 31232, 2048, F32)
        sb2 = arena("FA", FA_t, 33280, 2048, F32)
        tmp = arena("FA", FA_t, 35328, 2048, F32)
        ubuf = [arena("FA", FA_t, 37376 + i * 2064, 2064, F32) for i in range(2)]

        ft = [V(mgT_t[:, 2 * i:2 * i + 2, :].rearrange("p a b -> p (a b)").bitcast(F32),
                [("mgT", 2 * i), ("mgT", 2 * i + 1)]) for i in range(8)]
        xpc = [V(mgT_t[:, 8 * i:8 * i + 8, :].rearrange("p a b -> p (a b)").bitcast(F32),
                 [("mgT", g) for g in range(8 * i, 8 * i + 8)]) for i in range(2)]
        xnp = [V(xnp_t[i][:], [("xnp", i)]) for i in range(2)]
        ssp = [V(ssp_t[:, i:i + 1], [("ssp", i)]) for i in range(2)]
        rsp = [V(rsp_t[:, i:i + 1], [("rsp", i)]) for i in range(2)]
        xnTalt = [arena("FA", FA_t, kc * 1024, 1024, BF16) for kc in range(NKC)]
        xnTalt_halo = V(FA_t[:, 0:4096].bitcast(BF16).rearrange("p (a b) -> p a b", a=NKC)[:, :, TT - 2:TT],
                        [("FA", 0, 16384)])
        am = V(am_t[:], ["am"])
        LD = V(LD_t[:], ["LD"])
        LDall = V(LDall_t[:], ["LDall"])
        Deff = V(Deff_t[:], ["Deff"])
        one8 = V(one8_t[:], ["one8"])
        Lbuf = [V(mgT_t[:, 8 * i:8 * i + 8, :].rearrange("p a b -> p (a b)").bitcast(F32).rearrange("p (a b) -> p a b", a=4),
                  [("mgT", g) for g in range(8 * i, 8 * i + 8)]) for i in range(2)]
        xnTalt_halo0 = V(FA_t[:, 0:4096].bitcast(BF16).rearrange("p (a b) -> p a b", a=NKC)[:, :, 126:128],
                         [("FA", 0, 16384)])
        xnT = [V(xnT_t[:, kc, :], [("xnT", kc)]) for kc in range(NKC)]
        mgT = [V(mgT_t[:, g, :], [("mgT", g)]) for g in range(NKC)]
        Sv = [V(S_t[:, i, :], [("S", i)]) for i in range(8)]
        Sbf = [V(Sbf_t[:, i, :], [("Sbf", i)]) for i in range(8)]
        ident_f = V(cst_t[:, 0:128], ["cst"])
        maskT = V(cst_t[:, 128:256], ["cst"])
        ones128 = V(cst_t[:, 256:384], ["cst"])
        identb = V(identb_t[:], ["identb"])
        ones512 = V(ones_t[:], ["ones512"])
        vecs = V(vecs_t[:], ["vecs"])
        negb = V(negb_t[:], ["negb"])
        fnwb = V(fnwb_t[:], ["fnwb"])
        wgk = V(wgk_t[:], ["wgk"])
        gkl = V(gkl_t[:], ["gkl"])
        uh = [V(uh_t[:, g, :], [("uh", g)]) for g in range(16)]
        xnh = V(xnh_t[:], ["xnh"])
        gch = [V(gch_t[:, cc, :], [("gch", cc)]) for cc in range(4)]
        ss = V(ss_t[:], ["ss"])
        rs = V(rs_t[:], ["rs"])
        nbl = V(nbl_t[:], ["nbl"])
        dl = [V(dl_t[:, i, :], [("dl", i)]) for i in range(8)]
        bankv = [V(banks[i][:], [("ps", i)]) for i in range(8)]
        bankb = [V(banks[i][:].bitcast(BF16), [("ps", i)]) for i in range(8)]

        def vcol(c):
            return vecs[:, c:c + 1]

        def emit_all(P, plan):
            state = {"bank": 0, "wi": 0, "issued": 0}
            rec = []

            def ACT(out, in_, func, bias=None, scale=None):
                kw = {}
                if bias is not None:
                    kw["bias"] = _a(bias)
                if scale is not None:
                    kw["scale"] = scale
                P.op("act", lambda e: e.activation(out=out.ap, in_=in_.ap, func=func, **kw),
                     reads=_k(in_, bias), writes=_k(out))

            def ACOPY(out, in_):
                P.op("act", lambda e: e.copy(out=out.ap, in_=in_.ap), reads=_k(in_), writes=_k(out))

            def TT_(out, in0, in1, op):
                P.op("dve", lambda e: e.tensor_tensor(out=out.ap, in0=in0.ap, in1=in1.ap, op=op),
                     reads=_k(in0, in1), writes=_k(out))

            def TS(out, in0, s1, op0):
                P.op("dve", lambda e: e.tensor_scalar(out=out.ap, in0=in0.ap, scalar1=_a(s1), scalar2=None, op0=op0),
                     reads=_k(in0, s1), writes=_k(out))

            def STT(out, in0, s, in1, op0, op1):
                P.op("dve", lambda e: e.scalar_tensor_tensor(out=out.ap, in0=in0.ap, scalar=_a(s), in1=in1.ap,
                                                             op0=op0, op1=op1),
                     reads=_k(in0, s, in1), writes=_k(out))

            def DCOPY(out, in_):
                P.op("dve", lambda e: e.tensor_copy(out=out.ap, in_=in_.ap), reads=_k(in_), writes=_k(out))

            def MM(out, lhsT, rhs, start, stop):
                P.op("pe", lambda e: e.matmul(out.ap, lhsT=lhsT.ap, rhs=rhs.ap, start=start, stop=stop),
                     reads=_k(lhsT, rhs), writes=_k(out))

            def TR(out, in_):
                P.op("pe", lambda e: e.transpose(out=out.ap, in_=in_.ap, identity=identb.ap),
                     reads=_k(in_, identb), writes=_k(out))

            def nbank():
                b = state["bank"]
                state["bank"] = (b + 1) % 8
                return b

            def issue(desc, idx):
                src, k0, nk, segs = desc
                slot = idx % 3
                dram = {"in": w_in, "out": w_out, "gu": w_gu, "dn": w_dn}[src]
                off = 0
                prev = list(P.readers.get(("w", slot), ()))
                lw = P.last_writer.get(("w", slot))
                if lw is not None:
                    prev.append(lw)
                for si, (c0, n) in enumerate(segs):
                    view = dram[k0 * 128:(k0 + nk) * 128, c0:c0 + n].rearrange("(kc p) c -> p kc c", p=128)
                    dst = wr_t[slot][:, 0:nk, off:off + n]
                    key = ("w", slot) if si == 0 else ("w2", slot)
                    P.op("pool", lambda e, dst=dst, view=view: e.dma_start(out=dst, in_=view),
                         writes=[key], dma_key=key, extra_deps=prev if si > 0 else ())
                    off += n

            def wblock(src, k0, nk, c0, n, c1=None, n1=0):
                segs = ((c0, n),) if c1 is None else ((c0, n), (c1, n1))
                n = n + n1
                desc = (src, k0, nk, segs)
                i = state["wi"]
                state["wi"] = i + 1
                if plan is None:
                    rec.append(desc)
                else:
                    assert plan[i] == desc, (i, plan[i], desc)
                    while state["issued"] < min(len(plan), i + 3):
                        issue(plan[state["issued"]], state["issued"])
                        state["issued"] += 1
                slot = i % 3
                return V(wr_t[slot][:, 0:nk, 0:n], [("w", slot)] + ([("w2", slot)] if len(segs) > 1 else []))

            def group_fm(W, cc, M=128, N=TT, extra_rhs=None, X=None):
                X = xnT if X is None else X
                b = nbank()
                out = bankv[b][0:M, 0:N]
                for kc in range(NKC):
                    MM(out, W[:, kc, cc * 128:cc * 128 + M], X[kc][:, 0:N], kc == 0, kc == NKC - 1)
                if extra_rhs is not None:
                    b2 = nbank()
                    out2 = bankv[b2][0:M, 0:2]
                    for kc in range(NKC):
                        MM(out2, W[:, kc, cc * 128:cc * 128 + M], extra_rhs[:, kc, :], kc == 0, kc == NKC - 1)
                    return out, out2
                return out

            def group_tm(W, src, tc, ncol=512):
                b = nbank()
                out = bankv[b][:, 0:ncol]
                n = len(src)
                for kc in range(n):
                    MM(out, src[kc][:, tc * 128:(tc + 1) * 128], W[:, kc, 0:ncol], kc == 0, kc == n - 1)
                return out

            if True:
                P.op("sp", lambda e: e.dma_start(out=cst_t[:], in_=cst_d), writes=["cst"], dma_key="c_cst")
                P.op("sp", lambda e: e.dma_start(out=vecs_t[:], in_=vecs_d), writes=["vecs"], dma_key="c_vecs")
                P.op("sp", lambda e: e.dma_start(out=fnwb_t[:], in_=fnwb_d), writes=["fnwb"], dma_key="c_fnwb")
                P.op("pool", lambda e: e.dma_start(out=wgk_t[:], in_=wgk_d), writes=["wgk"], dma_key="c_wgk")
                P.op("pool", lambda e: e.dma_start(out=wgl_t[:],
                                                   in_=w_in[:, C_GKL:C_GKL + 16].rearrange("(kc p) c -> p kc c", p=128)),
                     writes=["wgl"], dma_key="c_wgl")
                DCOPY(identb, ident_f)
                P.op("dve", lambda e: e.memset(ones_t[:], 1.0), writes=["ones512"])
                for i in range(8):
                    P.op("dve", lambda e, i=i: e.memset(S_t[:, i, :], 0.0), writes=[("S", i)])
                TS(negb, vecs[:, 84:92], -1.0, ALU.mult)
                P.op("sp", lambda e: e.dma_start(out=am_t[:], in_=am_d), writes=["am"], dma_key="c_am")
                P.op("dve", lambda e: e.memset(LD_t[:], 0.0), writes=["LD"])
                P.op("dve", lambda e: e.memset(one8_t[:], 1.0), writes=["one8"])

            def load_x(row0):
                P.op("sp", lambda e: e.dma_start(out=xt.ap, in_=xs[row0:row0 + TT, :].rearrange("(tc p) d -> p tc d", p=128)),
                     writes=_k(xt), dma_key="x")

            def norm_to_T(wcol0, dst):
                for tc in range(4):
                    P.op("dve", lambda e, tc=tc: e.scalar_tensor_tensor(
                        out=junk.ap, in0=xt_tc[tc].ap, scalar=1.0, in1=xt_tc[tc].ap,
                        op0=ALU.mult, op1=ALU.mult, accum_out=ss_t[:, tc:tc + 1]),
                        reads=_k(xt_tc[tc]), writes=_k(junk, ss))
                ACT(rs, ss, AF.Ln, bias=EPS, scale=1.0 / D)
                ACT(rs, rs, AF.Exp, scale=-0.5)
                for tc in range(4):
                    TS(xn_tm[tc], xt_tc[tc], rs[:, tc:tc + 1], ALU.mult)
                for kc in range(NKC):
                    b = nbank()
                    for tc in range(4):
                        TR(bankb[b][:, tc * 128:(tc + 1) * 128], xn_tm[tc][:, kc * 128:(kc + 1) * 128])
                    TS(dst[kc], bankb[b][:, 0:TT], vcol(wcol0 + kc), ALU.mult)

            def s0_norm(src, j):
                P.op("dve", lambda e: e.scalar_tensor_tensor(
                    out=xnp[j].ap, in0=src.ap, scalar=1.0, in1=src.ap,
                    op0=ALU.mult, op1=ALU.mult, accum_out=ssp[j].ap),
                    reads=_k(src), writes=_k(xnp[j], ssp[j]))
                ACT(rsp[j], ssp[j], AF.Ln, bias=EPS, scale=1.0 / D)
                ACT(rsp[j], rsp[j], AF.Exp, scale=-0.5)
                TS(xnp[j], src, rsp[j], ALU.mult)

            def s0_load(row0, tc, j):
                r0 = row0 + tc * 128
                P.op("sp", lambda e: e.dma_start(out=xpc[j].ap, in_=xs[r0:r0 + 128, :]),
                     writes=_k(xpc[j]), dma_key=("xp", j))
                s0_norm(xpc[j], j)

            def s0_tr(tc, j, dst, wcol0):
                for kq in range(4):
                    b = nbank()
                    for i in range(4):
                        kc = kq * 4 + i
                        TR(bankb[b][:, i * 128:(i + 1) * 128], xnp[j][:, kc * 128:(kc + 1) * 128])
                    for i in range(4):
                        kc = kq * 4 + i
                        TS(dst[kc][:, tc * 128:(tc + 1) * 128], bankb[b][:, i * 128:(i + 1) * 128],
                           vcol(wcol0 + kc), ALU.mult)

            def final_stats():
                for tc in range(4):
                    P.op("dve", lambda e, tc=tc: e.scalar_tensor_tensor(
                        out=junk.ap, in0=xt_tc[tc].ap, scalar=1.0, in1=xt_tc[tc].ap,
                        op0=ALU.mult, op1=ALU.mult, accum_out=ss_t[:, tc:tc + 1]),
                        reads=_k(xt_tc[tc]), writes=_k(junk, ss))
                ACT(rs, ss, AF.Ln, bias=EPS, scale=1.0 / D)
                ACT(rs, rs, AF.Exp, scale=-0.5)

            def gk_low(X=None):
                W = V(wgl_t[:], ["wgl"])
                ps = group_fm(W, 0, M=16, X=X)
                ACOPY(gkl, ps)

            def decay_common(hd):
                b = nbank()
                ps = bankv[b]
                MM(ps, wgk[:, hd * 128:(hd + 1) * 128], gkl, True, True)
                ACT(spb, ps, AF.Exp, bias=negb[:, hd:hd + 1], scale=-1.0)
                ACT(spb, spb, AF.Ln, bias=1.0, scale=1.0)

            def scan(lo, hi):
                P.op("dve", lambda e: e.tensor_tensor_scan(out=E2.ap[:, lo:hi], data0=ones512.ap[:, lo:hi],
                                                           data1=spb.ap[:, lo:hi], initial=0.0,
                                                           op0=ALU.mult, op1=ALU.add),
                     reads=_k(spb, ones512), writes=_k(E2))

            def kt_and_v(h, pp, Wv, X=None):
                X = xnT if X is None else X
                for tc in range(4):
                    ps = group_tm(Wv, X, tc)
                    ACOPY(vb[pp][:, tc, :], ps)
                b = nbank()
                for tc in range(4):
                    for dkc in range(2):
                        TR(bankb[b][:, (tc * 2 + dkc) * 128:(tc * 2 + dkc + 1) * 128],
                           kteT[:, dkc, tc * 128:(tc + 1) * 128])
                P.op("act", lambda e: e.copy(out=kte[pp].ap, in_=bankb[b].ap.rearrange("p (a b) -> p a b", a=4)),
                     reads=_k(bankb[b]), writes=_k(kte[pp]))

            def prefix_tile(pt, last):
                X = xnT if pt % 2 == 0 else xnTalt
                Xn = xnT if (pt + 1) % 2 == 0 or last else xnTalt
                nrow0 = MROW0 if last else XROW0 + (pt + 1) * TT
                gk_low(X)
                for h in range(4):
                    s0_load(nrow0, h, h % 2)
                    if h >= 1:
                        s0_tr(h - 1, (h - 1) % 2, Xn, 0)
                    Wk = wblock("in", 0, NKC, C_K + h * 256, 256)
                    for dkc in range(2):
                        hd = h * 2 + dkc
                        ps = group_fm(Wk, dkc, X=X)
                        decay_common(hd)
                        scan(0, TT)
                        TS(nbl[:, 0:1], E2[:, TT - 1:TT], -1.0 / 16, ALU.mult)
                        TT_(LD[:, hd:hd + 1], LD[:, hd:hd + 1], nbl[:, 0:1], ALU.add)
                        ACT(E3, E2, AF.Exp, bias=nbl[:, 0:1], scale=1.0 / 16)
                        ACT(dl[hd][:, 0:1], nbl[:, 0:1], AF.Exp)
                        TT_(kteT[:, dkc, :], ps, E3, ALU.mult)
                    Wv = wblock("in", 0, NKC, C_V + h * 512, 512)
                    kt_and_v(h, 0, Wv, X)
                    for dkc in range(2):
                        hd = h * 2 + dkc
                        b = nbank()
                        for tc in range(4):
                            MM(bankv[b], kte[0][:, tc, dkc * 128:(dkc + 1) * 128], vb[0][:, tc, :], tc == 0, tc == 3)
                        STT(Sv[hd], Sv[hd], dl[hd][:, 0:1], bankv[b], ALU.mult, ALU.add)
                s0_tr(3, 1, Xn, 0)

            def qkv(h, pp):
                Wqk = wblock("in", 0, NKC, C_Q + h * 256, 256, C_K + h * 256, 256)
                for dkc in range(2):
                    hd = h * 2 + dkc
                    ps_q = group_fm(Wqk, dkc)
                    decay_common(hd)
                    for tc in range(4):
                        scan(tc * 128, (tc + 1) * 128)
                    ACT(E1, E2, AF.Exp, scale=-1.0 / 16)
                    TS(nbl, V(E2.ap.rearrange("p (a b) -> p a b", a=4)[:, :, 127], E2.keys), -1.0 / 16, ALU.mult)
                    for tc in range(4):
                        ACT(E3[:, tc * 128:(tc + 1) * 128], E2[:, tc * 128:(tc + 1) * 128], AF.Exp,
                            bias=nbl[:, tc:tc + 1], scale=1.0 / 16)
                    ACT(E2, E2, AF.Exp, scale=1.0 / 16)
                    ACT(dl[hd], nbl, AF.Exp)
                    STT(qd[pp][:, dkc, :], ps_q, 1.0 / 16, E1, ALU.mult, ALU.mult)
                    ps = group_fm(Wqk, 2 + dkc)
                    TT_(kd[pp][:, dkc, :], ps, E2, ALU.mult)
                    TT_(kteT[:, dkc, :], ps, E3, ALU.mult)
                Wv = wblock("in", 0, NKC, C_V + h * 512, 512)
                kt_and_v(h, pp, Wv)

            def gla_A(h, pp, tc):
                tsl = slice(tc * 128, (tc + 1) * 128)
                b = nbank()
                sc = bankv[b][:, 0:128]
                for dkc in range(2):
                    MM(sc, kd[pp][:, dkc, tsl], qd[pp][:, dkc, tsl], dkc == 0, dkc == 1)
                TT_(scT[tc % 2], sc, maskT, ALU.mult)

            def gla_B(h, pp, tc):
                tsl = slice(tc * 128, (tc + 1) * 128)
                s_ = scT[tc % 2]
                b = nbank()
                for dvc in range(4):
                    dsl = slice(dvc * 128, (dvc + 1) * 128)
                    o = bankv[b][:, dsl]
                    MM(o, vb[pp][:, tc, dsl], s_, True, False)
                    MM(o, Sbf[h * 2][:, dsl], qd[pp][:, 0, tsl], False, False)
                    MM(o, Sbf[h * 2 + 1][:, dsl], qd[pp][:, 1, tsl], False, True)
                P.op("act", lambda e, b=b, tsl=tsl: e.copy(out=o_sb.ap[:, :, tsl],
                                                           in_=bankv[b].ap.rearrange("p (a b) -> p a b", a=4)),
                     reads=_k(bankv[b]), writes=_k(o_sb))
                for dkc in range(2):
                    hd = h * 2 + dkc
                    b = nbank()
                    MM(bankv[b], kte[pp][:, tc, dkc * 128:(dkc + 1) * 128], vb[pp][:, tc, :], True, True)
                    STT(Sv[hd], Sv[hd], dl[hd][:, tc:tc + 1], bankv[b], ALU.mult, ALU.add)
                    ACOPY(Sbf[hd], Sv[hd])

            def gla_norm(h):
                b = nbank()
                for dvc in range(4):
                    ACT(sq[dvc % 2], o_cc[dvc], AF.Square)
                    MM(bankv[b], ones128, sq[dvc % 2], dvc == 0, dvc == 3)
                ACT(lnv, bankv[b], AF.Ln, bias=EPS, scale=1.0 / 512)
                ACT(lnv, lnv, AF.Exp, scale=-0.5)
                for dvc in range(4):
                    STT(o_cc[dvc], o_cc[dvc], vcol(80 + dvc), lnv, ALU.mult, ALU.mult)

            def gla(h, pp):
                for tc in range(4):
                    gla_A(h, pp, tc)
                    gla_B(h, pp, tc)

            def bmix(h, first_tile, hook=None):
                cnt = [0]

                def tick():
                    cnt[0] += 1
                    if hook is not None:
                        hook(cnt[0])
                Wgc = wblock("in", 0, NKC, C_GC + h * 512, 512)
                for cc in range(4):
                    if first_tile:
                        ps, psh = group_fm(Wgc, cc, extra_rhs=xnh)
                        ACOPY(gch[cc], psh)
                    else:
                        ps = group_fm(Wgc, cc)
                    ACOPY(gc[cc], ps)
                    tick()
                Wxc = wblock("in", 0, NKC, C_XC + h * 512, 512)
                for cc in range(4):
                    g = h * 4 + cc
                    u = ubuf[cc % 2]
                    if first_tile:
                        ps, psh = group_fm(Wxc, cc, extra_rhs=xnh)
                        TT_(uh[g], psh, gch[cc], ALU.mult)
                    else:
                        ps = group_fm(Wxc, cc)
                    DCOPY(u[:, 0:2], uh[g])
                    TT_(u[:, 2:514], ps, gc[cc], ALU.mult)
                    TS(cv[cc], u[:, 2:514], vcol(32 + g * 3 + 2), ALU.mult)
                    STT(cv[cc], u[:, 1:513], vcol(32 + g * 3 + 1), cv[cc], ALU.mult, ALU.add)
                    STT(cv[cc], u[:, 0:512], vcol(32 + g * 3 + 0), cv[cc], ALU.mult, ALU.add)
                    DCOPY(uh[g], u[:, 512:514])
                    tick()
                Wgb = wblock("in", 0, NKC, C_GB + h * 512, 512)
                for cc in range(4):
                    ps = group_fm(Wgb, cc)
                    TT_(cv[cc], ps, cv[cc], ALU.mult)
                    tick()

            def bmix_gate(h):
                Wgo = wblock("in", 0, NKC, C_GO + h * 512, 512)
                for cc in range(4):
                    ps = group_fm(Wgo, cc)
                    if cc == 0:
                        gla_norm(h)
                    ACT(sg, ps, AF.Sigmoid)
                    TT_(tmp, ps, sg, ALU.mult)
                    TT_(o_cc[cc], tmp, o_cc[cc], ALU.mult)
                Wma = wblock("in", 0, NKC, C_MA + h * 512, 512)
                for cc in range(4):
                    ps = group_fm(Wma, cc)
                    ACT(sg, ps, AF.Sigmoid)
                    TT_(o_cc[cc], o_cc[cc], sg, ALU.mult)
                Wmb = wblock("in", 0, NKC, C_MB + h * 512, 512)
                for cc in range(4):
                    g = h * 4 + cc
                    ps = group_fm(Wmb, cc)
                    ACT(sb2, ps, AF.Sigmoid)
                    TT_(cv[cc], cv[cc], sb2, ALU.mult)
                    TT_(mgT[g], cv[cc], o_cc[cc], ALU.add)

            def exchange_start():
                P.op("sp", lambda e: e.dma_start(out=lsrc[0].ap(), in_=S_t[:, 0:4, :].rearrange("p a b -> p (a b)")),
                     reads=[("S", i) for i in range(4)], writes=["lsrc0"], dma_key="ls0")
                P.op("sp", lambda e: e.dma_start(out=lsrc[1].ap(), in_=S_t[:, 4:8, :].rearrange("p a b -> p (a b)")),
                     reads=[("S", i) for i in range(4, 8)], writes=["lsrc1"], dma_key="ls1")
                P.op("sp", lambda e: e.dma_start(out=lsrc[2].ap(), in_=LD_t[:]), reads=["LD"], writes=["lsrc2"],
                     dma_key="ls2")
                prior = [o.id for o in P.by_eng["pool"] if o.dma][-3:]
                ccs = []
                for i in range(3):
                    ccs.append(P.op("pool", lambda e, i=i: e.collective_compute(
                        "AllGather", ALU.bypass, replica_groups=[[0, 1, 2, 3], [4, 5, 6, 7]],
                        ins=[lsrc[i].ap()], outs=[lall[i].ap()]),
                        reads=["lsrc%d" % i], writes=["lall%d" % i], dma_key="cc%d" % i, inc=1,
                        extra_deps=prior))

            def combine():
                for i in range(8):
                    P.op("dve", lambda e, i=i: e.memset(S_t[:, i, :], 0.0), writes=[("S", i)])
                P.op("sp", lambda e: e.dma_start(out=LDall_t[:], in_=lall[2].ap().rearrange("(r p) c -> p r c", p=128)),
                     reads=["lall2"], writes=["LDall"], dma_key="ld_all")
                ACT(LDall, LDall, AF.Exp)
                for j in range(3):
                    for half in range(2):
                        Lb = Lbuf[half]
                        P.op("sp", lambda e, j=j, Lb=Lb, half=half: e.dma_start(
                            out=Lb.ap, in_=lall[half].ap()[j * 128:(j + 1) * 128, :].rearrange("p (a b) -> p a b", a=4)),
                            reads=["lall%d" % half], writes=_k(Lb), dma_key=("lb", half))
                    TS(Deff, LDall[:, j, 0:8], -1.0, ALU.add)
                    STT(Deff, Deff, am[:, j:j + 1], one8, ALU.mult, ALU.add)
                    for hd in range(8):
                        TS(Sv[hd], Sv[hd], Deff[:, hd:hd + 1], ALU.mult)
                        STT(Sv[hd], Lbuf[hd // 4][:, hd % 4, :], am[:, j:j + 1], Sv[hd], ALU.mult, ALU.add)
                for i in range(8):
                    ACOPY(Sbf[i], Sv[i])

            def main_tile(mt):
                row0 = MROW0 + mt * TT
                gk_low()
                qkv(0, 0)
                for h in range(4):
                    if h + 1 < 4:
                        qkv(h + 1, (h + 1) % 2)
                    pp = h % 2
                    if USE_CC and mt == 0 and h == 0:
                        bmix(h, True)
                        combine()
                        gla(h, pp)
                    else:
                        gla_A(h, pp, 0)

                        def hook(n, h=h, pp=pp):
                            if n % 3 == 0:
                                tc = n // 3 - 1
                                gla_B(h, pp, tc)
                                if tc + 1 < 4:
                                    gla_A(h, pp, tc + 1)
                        bmix(h, mt == 0, hook)
                    if h == 3:
                        load_x(row0)
                    bmix_gate(h)
                for db in range(4):
                    Wo = wblock("out", 0, NKC, db * 512, 512)
                    for tc in range(4):
                        ps = group_tm(Wo, mgT, tc)
                        dsl = slice(db * 512, (db + 1) * 512)
                        TT_(xt_tc[tc][:, dsl], ps, xt_tc[tc][:, dsl], ALU.add)
                        if db == 3:
                            s0_norm(xt_tc[tc], tc % 2)
                            if tc >= 1:
                                s0_tr(tc - 1, (tc - 1) % 2, xnT, 16)
                s0_tr(3, 1, xnT, 16)
                for fb in range(11):
                    Wg = wblock("gu", 0, NKC, fb * 512, 512)
                    for cc in range(4):
                        ps = group_fm(Wg, cc)
                        ACT(ft[4 + cc % 2], ps, AF.Sigmoid)
                        TT_(ft[cc], ps, ft[4 + cc % 2], ALU.mult)
                    Wu = wblock("gu", 0, NKC, FF + fb * 512, 512)
                    for cc in range(4):
                        ps = group_fm(Wu, cc)
                        TT_(actT[fb * 4 + cc], ps, ft[cc], ALU.mult)
                for db in range(4):
                    bs = [nbank() for _ in range(4)]
                    subs = [(0, 16), (16, 16), (32, 12)]
                    for (k0, nk) in subs:
                        Wd = wblock("dn", k0, nk, db * 512, 512)
                        for tc in range(4):
                            for j in range(nk):
                                fc = k0 + j
                                MM(bankv[bs[tc]], actT[fc][:, tc * 128:(tc + 1) * 128], Wd[:, j, :],
                                   fc == 0, fc == NFC - 1)
                    for tc in range(4):
                        dsl = slice(db * 512, (db + 1) * 512)
                        TT_(xt_tc[tc][:, dsl], bankv[bs[tc]], xt_tc[tc][:, dsl], ALU.add)
                        if db == 3 and mt == NMAIN - 1:
                            j = tc % 2
                            P.op("dve", lambda e, tc=tc, j=j: e.scalar_tensor_tensor(
                                out=xnp[j].ap, in0=xt_tc[tc].ap, scalar=1.0, in1=xt_tc[tc].ap,
                                op0=ALU.mult, op1=ALU.mult, accum_out=ssp[j].ap),
                                reads=_k(xt_tc[tc]), writes=_k(xnp[j], ssp[j]))
                            ACT(rsp[j], ssp[j], AF.Ln, bias=EPS, scale=1.0 / D)
                            ACT(rsp[j], rsp[j], AF.Exp, scale=-0.5)
                            STT(xt_tc[tc], xt_tc[tc], rsp[j], fnwb, ALU.mult, ALU.mult)
                            r0 = mt * TT + tc * 128
                            P.op("sp", lambda e, tc=tc, r0=r0: e.dma_start(out=out_d[r0:r0 + 128, :], in_=xt_tc[tc].ap),
                                 reads=_k(xt_tc[tc]), dma_key=("o", mt))
                    if mt + 1 < NMAIN:
                        s0_load(row0 + TT, db, db % 2)
                        if db >= 1:
                            s0_tr(db - 1, (db - 1) % 2, xnT, 0)
                if mt + 1 < NMAIN:
                    s0_tr(3, 1, xnT, 0)
                if mt == NMAIN - 1:
                    return
                final_stats()
                for tc in range(4):
                    STT(xt_tc[tc], xt_tc[tc], rs[:, tc:tc + 1], fnwb, ALU.mult, ALU.mult)
                P.op("sp", lambda e: e.dma_start(out=out_d[mt * TT:(mt + 1) * TT, :].rearrange("(tc p) d -> p tc d", p=128),
                                                 in_=xt.ap),
                     reads=_k(xt), dma_key=("o", mt))

            s0_load(0, 0, 0)
            s0_tr(0, 0, xnTalt, 0)
            DCOPY(xnh, xnTalt_halo0)
            for tc in range(4):
                s0_load(XROW0, tc, (tc + 1) % 2)
                s0_tr(tc, (tc + 1) % 2, xnT, 0)
            for pt in range(NPRE):
                prefix_tile(pt, pt == NPRE - 1)
            if USE_CC:
                exchange_start()
            else:
                for i in range(8):
                    ACOPY(Sbf[i], Sv[i])
            for mt in range(NMAIN):
                main_tile(mt)
            P.op("sp", lambda e: None, extra_deps=[o.id for o in P.ops if o.dma and o.eng == "sp"])
            return rec

        plan = emit_all(Prog(nc), None)
        P = Prog(nc)
        emit_all(P, plan)
        P.emit(st)
        print("ops", len(P.ops), "sems", P.n_sems, "max_tick", P.max_tick, "wblocks", len(plan))
    return nc


def _consts():
    c = np.zeros((128, 384), np.float32)
    c[:, 0:128] = np.eye(128, dtype=np.float32)
    j = np.arange(128)[:, None]
    i = np.arange(128)[None, :]
    c[:, 128:256] = (j <= i).astype(np.float32)
    c[:, 256:384] = 1.0
    return c


def kernel(x, mix_norm_w, w_in, w_gk_up, b_gk_up, gla_norm_w, conv_w, w_out,
           ffn_norm_w, w_gate_up, w_down, final_norm_w):
    x = np.asarray(x, np.float32)
    B, S, _ = x.shape
    vecs = np.zeros((128, 92), np.float32)
    vecs[:, 0:16] = np.asarray(mix_norm_w, np.float32)[0].reshape(16, 128).T
    vecs[:, 16:32] = np.asarray(ffn_norm_w, np.float32)[0].reshape(16, 128).T
    cw = np.asarray(conv_w, np.float32)[0]
    vecs[:, 32:80] = cw.reshape(3, 16, 128).transpose(2, 1, 0).reshape(128, 48)
    vecs[:, 80:84] = np.asarray(gla_norm_w, np.float32)[0].reshape(4, 128).T
    vecs[:, 84:92] = np.asarray(b_gk_up, np.float32)[0].reshape(8, 128).T
    fnwb = np.ascontiguousarray(np.broadcast_to(np.asarray(final_norm_w, np.float32)[None, :], (128, D)))
    shared = {
        "w_in": np.ascontiguousarray(np.asarray(w_in, np.float32)[0]),
        "w_out": np.ascontiguousarray(np.asarray(w_out, np.float32)[0]),
        "w_gu": np.ascontiguousarray(np.asarray(w_gate_up, np.float32)[0]),
        "w_dn": np.ascontiguousarray(np.asarray(w_down, np.float32)[0]),
        "wgk": np.ascontiguousarray(np.asarray(w_gk_up, np.float32)[0]),
        "vecs": vecs, "fnwb": fnwb, "cst": _consts(),
    }
    in_maps = []
    own = NMAIN * TT
    for c in range(8):
        b, p = c // 4, c % 4
        xs = np.zeros((MROW0 + own, D), np.float32)
        start = p * own
        if start > 0:
            xs[0:XROW0] = x[b, start - XROW0:start]
            if not USE_CC:
                xs[MROW0 - start:MROW0] = x[b, 0:start]
        xs[MROW0:] = x[b, start:start + own]
        am = np.zeros((128, 4), np.float32)
        am[:, 0:p] = 1.0
        m = dict(shared)
        m["xs"] = xs
        m["am"] = am
        in_maps.append(m)
    nc = build_nc()
    res = run_bass_kernel_spmd(nc, in_maps, core_ids=list(range(8)))
    out = np.zeros((B, S, D), np.float32)
    for c in range(8):
        b, p = c // 4, c % 4
        out[b, p * own:(p + 1) * own] = res.results[c]["out"]
    return out
```

```python
import numpy as np
from contextlib import ExitStack
import concourse.bass as bass
import concourse.mybir as mybir
from concourse.bass_utils import run_bass_kernel_spmd
from concourse.alu_op_type import AluOpType as ALU

F32 = mybir.dt.float32
BF16 = mybir.dt.bfloat16
AF = mybir.ActivationFunctionType

D = 2048
NKC = 16
TT = 512
NTC = 4
FF = 5632
NFC = 44
USE_CC = True
NPRE = 4 if USE_CC else 12
XROW0 = 128
MROW0 = XROW0 if USE_CC else XROW0 + NPRE * TT
NMAIN = 4
EPS = 1e-6
C_Q, C_K, C_V, C_GO, C_GKL, C_GB, C_GC, C_XC, C_MA, C_MB = 0, 1024, 2048, 4096, 6144, 6160, 8208, 10256, 12304, 14352
ENGS = ("pe", "act", "dve", "pool", "sp")
ARENAS = ("XT", "FA")


class Op:
    __slots__ = ("id", "eng", "fn", "dma", "deps", "sem", "tick", "signal", "inc")

    def __init__(self, id, eng, fn, dma):
        self.id = id
        self.eng = eng
        self.fn = fn
        self.dma = dma
        self.deps = set()
        self.sem = None
        self.tick = None
        self.signal = False
        self.inc = 16 if dma else 1


class Prog:
    def __init__(self, nc):
        self.nc = nc
        self.ops = []
        self.by_eng = {e: [] for e in ENGS}
        self.last_writer = {}
        self.readers = {}
        self.arena_keys = {a: [] for a in ARENAS}
        self.dma_keys = {}

    def _expand(self, k):
        if isinstance(k, tuple) and k and k[0] in ARENAS:
            lst = self.arena_keys[k[0]]
            if k not in self.last_writer and k not in self.readers:
                lst.append(k)
                self.readers[k] = []
            return [o for o in lst if o[1] < k[2] and k[1] < o[2]]
        return [k]

    def op(self, eng, fn, reads=(), writes=(), dma_key=None, extra_deps=(), inc=None):
        o = Op(len(self.ops), eng, fn, dma_key is not None)
        deps = o.deps
        for r in reads:
            for k in self._expand(r):
                w = self.last_writer.get(k)
                if w is not None:
                    deps.add(w)
        for w_ in writes:
            for k in self._expand(w_):
                w = self.last_writer.get(k)
                if w is not None:
                    deps.add(w)
                rl = self.readers.get(k)
                if rl:
                    deps.update(rl)
        deps.update(extra_deps)
        deps.discard(o.id)
        for r in reads:
            self.readers.setdefault(r, []).append(o.id)
        for w_ in writes:
            for k in self._expand(w_):
                self.last_writer[k] = o.id
                self.readers[k] = []
        if dma_key is not None:
            c = self.dma_keys.get(dma_key, 0) + 1
            self.dma_keys[dma_key] = c
            if inc is not None:
                o.inc = inc
            o.sem = ("dma", dma_key)
            o.tick = o.inc * c
            o.signal = True
        self.ops.append(o)
        self.by_eng[eng].append(o)
        return o.id

    def emit(self, stack):
        nc = self.nc
        ops = self.ops
        for o in ops:
            nd = set()
            for d in o.deps:
                p = ops[d]
                if (not p.dma) and p.eng == o.eng and o.eng == "pe":
                    continue
                nd.add(d)
            o.deps = nd
            for d in nd:
                ops[d].signal = True
        cnt = {e: 0 for e in ENGS}
        for e in ENGS:
            for o in self.by_eng[e]:
                if o.dma:
                    continue
                if o.signal:
                    cnt[e] += 1
                    o.sem = ("eng", e)
                    o.tick = cnt[e]
        sems = {}
        for e in ENGS:
            if cnt[e] > 0:
                sems[("eng", e)] = stack.enter_context(nc.semaphore("prog_" + e))
        for i, k in enumerate(self.dma_keys):
            sems[("dma", k)] = stack.enter_context(nc.semaphore("dma_%d" % i))
        self.n_sems = len(sems)
        self.max_tick = max([cnt[e] for e in ENGS] + [16 * c for c in self.dma_keys.values()] + [0])
        block = stack.enter_context(nc.Block())

        def run(eng_name, eng):
            waited = {}
            for o in self.by_eng[eng_name]:
                need = {}
                for d in o.deps:
                    p = ops[d]
                    if need.get(p.sem, 0) < p.tick:
                        need[p.sem] = p.tick
                for s, v in need.items():
                    if waited.get(s, 0) >= v:
                        continue
                    eng.wait_ge(sems[s], v)
                    waited[s] = v
                inst = o.fn(eng)
                if o.signal:
                    inst.then_inc(sems[o.sem], o.inc)

        @block.tensor
        def _(e):
            run("pe", e)

        @block.scalar
        def _(e):
            run("act", e)

        @block.vector
        def _(e):
            run("dve", e)

        @block.gpsimd
        def _(e):
            run("pool", e)

        @block.sync
        def _(e):
            run("sp", e)


class V:
    __slots__ = ("ap", "keys")

    def __init__(self, ap, keys):
        self.ap = ap
        self.keys = tuple(keys)

    def __getitem__(self, idx):
        return V(self.ap[idx], self.keys)


def _k(*vs):
    out = []
    for v in vs:
        if isinstance(v, V):
            out.extend(v.keys)
    return out


def _a(v):
    return v.ap if isinstance(v, V) else v


def build_nc():
    nc = bass.Bass("TRN2", target_bir_lowering=False)
    xs = nc.dram_tensor("xs", [MROW0 + NMAIN * TT, D], F32, kind="ExternalInput").ap()
    am_d = nc.dram_tensor("am", [128, 4], F32, kind="ExternalInput").ap()
    lsrc = [nc.dram_tensor("lsrc%d" % i, [128, 2048], F32) for i in range(2)] + [nc.dram_tensor("lsrc2", [128, 32], F32)]
    lall = [nc.dram_tensor("lall%d" % i, [512, 2048], F32) for i in range(2)] + [nc.dram_tensor("lall2", [512, 32], F32)]
    w_in = nc.dram_tensor("w_in", [D, 16400], F32, kind="ExternalInput").ap()
    w_out = nc.dram_tensor("w_out", [D, D], F32, kind="ExternalInput").ap()
    w_gu = nc.dram_tensor("w_gu", [D, 2 * FF], F32, kind="ExternalInput").ap()
    w_dn = nc.dram_tensor("w_dn", [FF, D], F32, kind="ExternalInput").ap()
    wgk_d = nc.dram_tensor("wgk", [16, 1024], F32, kind="ExternalInput").ap()
    vecs_d = nc.dram_tensor("vecs", [128, 92], F32, kind="ExternalInput").ap()
    fnwb_d = nc.dram_tensor("fnwb", [128, D], F32, kind="ExternalInput").ap()
    cst_d = nc.dram_tensor("cst", [128, 384], F32, kind="ExternalInput").ap()
    out_d = nc.dram_tensor("out", [NMAIN * TT, D], F32, kind="ExternalOutput").ap()

    with ExitStack() as st:
        def sb(name, shape, dt):
            return st.enter_context(nc.sbuf_tensor("s_" + name, shape, dt))

        xnT_t = sb("xnT", [128, NKC, TT], BF16)
        mgT_t = sb("mgT", [128, NKC, TT], BF16)
        wr_t = [sb("wr%d" % i, [128, NKC, 512], BF16) for i in range(3)]
        S_t = sb("S", [128, 8, 512], F32)
        Sbf_t = sb("Sbf", [128, 8, 512], BF16)
        cst_t = sb("cst", [128, 384], F32)
        identb_t = sb("identb", [128, 128], BF16)
        ones_t = sb("ones512", [128, 512], F32)
        vecs_t = sb("vecs", [128, 92], F32)
        negb_t = sb("negb", [128, 8], F32)
        fnwb_t = sb("fnwb", [128, D], F32)
        wgk_t = sb("wgkb", [16, 1024], BF16)
        gkl_t = sb("gkl", [16, TT], BF16)
        wgl_t = sb("wgl", [128, NKC, 16], BF16)
        uh_t = sb("uh", [128, 16, 2], F32)
        xnh_t = sb("xnh", [128, NKC, 2], BF16)
        gch_t = sb("gch", [128, 4, 2], F32)
        ss_t = sb("ss", [128, 4], F32)
        rs_t = sb("rs", [128, 4], F32)
        nbl_t = sb("nbl", [128, 4], F32)
        dl_t = sb("dl", [128, 8, 4], F32)
        am_t = sb("am", [128, 4], F32)
        LD_t = sb("LD", [128, 32], F32)
        LDall_t = sb("LDall", [128, 4, 32], F32)
        Deff_t = sb("Deff", [128, 8], F32)
        one8_t = sb("one8", [128, 8], F32)
        xnp_t = [sb("xnp%d" % i, [128, D], BF16) for i in range(2)]
        ssp_t = sb("ssp", [128, 2], F32)
        rsp_t = sb("rsp", [128, 2], F32)
        XT_t = sb("XT", [128, 8192], F32)
        FA_t = sb("FA", [128, 11264], F32)
        banks = [st.enter_context(nc.psum_tensor("ps%d" % i, [128, 512], F32)) for i in range(8)]

        def arena(name, t, lo, nbytes, dt, pat=None, **kw):
            assert lo % 4 == 0 and nbytes % 4 == 0
            ap = t[:, lo // 4:(lo + nbytes) // 4]
            if dt == BF16:
                ap = ap.bitcast(BF16)
            if pat is not None:
                ap = ap.rearrange(pat, **kw)
            return V(ap, [(name, lo, lo + nbytes)])

        xt = arena("XT", XT_t, 0, 32768, F32, "p (a b) -> p a b", a=4)
        xt_tc = [arena("XT", XT_t, tc * 8192, 8192, F32) for tc in range(4)]
        spb = arena("XT", XT_t, 0, 2048, F32)
        E1 = arena("XT", XT_t, 2048, 2048, F32)
        E2 = arena("XT", XT_t, 4096, 2048, F32)
        E3 = arena("XT", XT_t, 6144, 2048, F32)
        qd = [arena("XT", XT_t, 8192 + i * 2048, 2048, BF16, "p (a b) -> p a b", a=2) for i in range(2)]
        kd = [arena("XT", XT_t, 12288 + i * 2048, 2048, BF16, "p (a b) -> p a b", a=2) for i in range(2)]
        kteT = arena("XT", XT_t, 16384, 2048, BF16, "p (a b) -> p a b", a=2)
        kte = [arena("XT", XT_t, 18432 + i * 2048, 2048, BF16, "p (a b) -> p a b", a=4) for i in range(2)]
        vb = [arena("XT", XT_t, 22528 + i * 4096, 4096, BF16, "p (a b) -> p a b", a=4) for i in range(2)]
        actT = [arena("FA", FA_t, fc * 1024, 1024, BF16) for fc in range(NFC)]
        xn_tm = [arena("FA", FA_t, tc * 4096, 4096, BF16) for tc in range(4)]
        junk = arena("FA", FA_t, 16384, 4096, BF16)
        scT = [arena("FA", FA_t, i * 256, 256, BF16) for i in range(2)]
        o_sb = arena("FA", FA_t, 512, 8192, F32, "p (a b) -> p a b", a=4)
        o_cc = [arena("FA", FA_t, 512 + cc * 2048, 2048, F32) for cc in range(4)]
        sq = [arena("FA", FA_t, 8704 + i * 2048, 2048, F32) for i in range(2)]
        lnv = arena("FA", FA_t, 12800, 2048, F32)
        gc = [arena("FA", FA_t, 14848 + cc * 2048, 2048, F32) for cc in range(4)]
        cv = [arena("FA", FA_t, 23040 + cc * 2048, 2048, F32) for cc in range(4)]
        sg = arena("FA", FA_t, 31232, 2048, F32)
        sb2 = arena("FA", FA_t, 33280, 2048, F32)
        tmp = arena("FA", FA_t, 35328, 2048, F32)
        ubuf = [arena("FA", FA_t, 37376 + i * 2064, 2064, F32) for i in range(2)]

        ft = [V(mgT_t[:, 2 * i:2 * i + 2, :].rearrange("p a b -> p (a b)").bitcast(F32),
                [("mgT", 2 * i), ("mgT", 2 * i + 1)]) for i in range(8)]
        xpc = [V(mgT_t[:, 8 * i:8 * i + 8, :].rearrange("p a b -> p (a b)").bitcast(F32),
                 [("mgT", g) for g in range(8 * i, 8 * i + 8)]) for i in range(2)]
        xnp = [V(xnp_t[i][:], [("xnp", i)]) for i in range(2)]
        ssp = [V(ssp_t[:, i:i + 1], [("ssp", i)]) for i in range(2)]
        rsp = [V(rsp_t[:, i:i + 1], [("rsp", i)]) for i in range(2)]
        xnTalt = [arena("FA", FA_t, kc * 1024, 1024, BF16) for kc in range(NKC)]
        xnTalt_halo = V(FA_t[:, 0:4096].bitcast(BF16).rearrange("p (a b) -> p a b", a=NKC)[:, :, TT - 2:TT],
                        [("FA", 0, 16384)])
        am = V(am_t[:], ["am"])
        LD = V(LD_t[:], ["LD"])
        LDall = V(LDall_t[:], ["LDall"])
        Deff = V(Deff_t[:], ["Deff"])
        one8 = V(one8_t[:], ["one8"])
        Lbuf = [V(mgT_t[:, 8 * i:8 * i + 8, :].rearrange("p a b -> p (a b)").bitcast(F32).rearrange("p (a b) -> p a b", a=4),
                  [("mgT", g) for g in range(8 * i, 8 * i + 8)]) for i in range(2)]
        xnTalt_halo0 = V(FA_t[:, 0:4096].bitcast(BF16).rearrange("p (a b) -> p a b", a=NKC)[:, :, 126:128],
                         [("FA", 0, 16384)])
        xnT = [V(xnT_t[:, kc, :], [("xnT", kc)]) for kc in range(NKC)]
        mgT = [V(mgT_t[:, g, :], [("mgT", g)]) for g in range(NKC)]
        Sv = [V(S_t[:, i, :], [("S", i)]) for i in range(8)]
        Sbf = [V(Sbf_t[:, i, :], [("Sbf", i)]) for i in range(8)]
        ident_f = V(cst_t[:, 0:128], ["cst"])
        maskT = V(cst_t[:, 128:256], ["cst"])
        ones128 = V(cst_t[:, 256:384], ["cst"])
        identb = V(identb_t[:], ["identb"])
        ones512 = V(ones_t[:], ["ones512"])
        vecs = V(vecs_t[:], ["vecs"])
        negb = V(negb_t[:], ["negb"])
        fnwb = V(fnwb_t[:], ["fnwb"])
        wgk = V(wgk_t[:], ["wgk"])
        gkl = V(gkl_t[:], ["gkl"])
        uh = [V(uh_t[:, g, :], [("uh", g)]) for g in range(16)]
        xnh = V(xnh_t[:], ["xnh"])
        gch = [V(gch_t[:, cc, :], [("gch", cc)]) for cc in range(4)]
        ss = V(ss_t[:], ["ss"])
        rs = V(rs_t[:], ["rs"])
        nbl = V(nbl_t[:], ["nbl"])
        dl = [V(dl_t[:, i, :], [("dl", i)]) for i in range(8)]
        bankv = [V(banks[i][:], [("ps", i)]) for i in range(8)]
        bankb = [V(banks[i][:].bitcast(BF16), [("ps", i)]) for i in range(8)]

        def vcol(c):
            return vecs[:, c:c + 1]

        def emit_all(P, plan):
            state = {"bank": 0, "wi": 0, "issued": 0}
            rec = []

            def ACT(out, in_, func, bias=None, scale=None):
                kw = {}
                if bias is not None:
                    kw["bias"] = _a(bias)
                if scale is not None:
                    kw["scale"] = scale
                P.op("act", lambda e: e.activation(out=out.ap, in_=in_.ap, func=func, **kw),
                     reads=_k(in_, bias), writes=_k(out))

            def ACOPY(out, in_):
                P.op("act", lambda e: e.copy(out=out.ap, in_=in_.ap), reads=_k(in_), writes=_k(out))

            def TT_(out, in0, in1, op):
                P.op("dve", lambda e: e.tensor_tensor(out=out.ap, in0=in0.ap, in1=in1.ap, op=op),
                     reads=_k(in0, in1), writes=_k(out))

            def TS(out, in0, s1, op0):
                P.op("dve", lambda e: e.tensor_scalar(out=out.ap, in0=in0.ap, scalar1=_a(s1), scalar2=None, op0=op0),
                     reads=_k(in0, s1), writes=_k(out))

            def STT(out, in0, s, in1, op0, op1):
                P.op("dve", lambda e: e.scalar_tensor_tensor(out=out.ap, in0=in0.ap, scalar=_a(s), in1=in1.ap,
                                                             op0=op0, op1=op1),
                     reads=_k(in0, s, in1), writes=_k(out))

            def DCOPY(out, in_):
                P.op("dve", lambda e: e.tensor_copy(out=out.ap, in_=in_.ap), reads=_k(in_), writes=_k(out))

            def MM(out, lhsT, rhs, start, stop):
                P.op("pe", lambda e: e.matmul(out.ap, lhsT=lhsT.ap, rhs=rhs.ap, start=start, stop=stop),
                     reads=_k(lhsT, rhs), writes=_k(out))

            def TR(out, in_):
                P.op("pe", lambda e: e.transpose(out=out.ap, in_=in_.ap, identity=identb.ap),
                     reads=_k(in_, identb), writes=_k(out))

            def nbank():
                b = state["bank"]
                state["bank"] = (b + 1) % 8
                return b

            def issue(desc, idx):
                src, k0, nk, segs = desc
                slot = idx % 3
                dram = {"in": w_in, "out": w_out, "gu": w_gu, "dn": w_dn}[src]
                off = 0
                prev = list(P.readers.get(("w", slot), ()))
                lw = P.last_writer.get(("w", slot))
                if lw is not None:
                    prev.append(lw)
                for si, (c0, n) in enumerate(segs):
                    view = dram[k0 * 128:(k0 + nk) * 128, c0:c0 + n].rearrange("(kc p) c -> p kc c", p=128)
                    dst = wr_t[slot][:, 0:nk, off:off + n]
                    key = ("w", slot) if si == 0 else ("w2", slot)
                    P.op("pool", lambda e, dst=dst, view=view: e.dma_start(out=dst, in_=view),
                         writes=[key], dma_key=key, extra_deps=prev if si > 0 else ())
                    off += n

            def wblock(src, k0, nk, c0, n, c1=None, n1=0):
                segs = ((c0, n),) if c1 is None else ((c0, n), (c1, n1))
                n = n + n1
                desc = (src, k0, nk, segs)
                i = state["wi"]
                state["wi"] = i + 1
                if plan is None:
                    rec.append(desc)
                else:
                    assert plan[i] == desc, (i, plan[i], desc)
                    while state["issued"] < min(len(plan), i + 3):
                        issue(plan[state["issued"]], state["issued"])
                        state["issued"] += 1
                slot = i % 3
                return V(wr_t[slot][:, 0:nk, 0:n], [("w", slot)] + ([("w2", slot)] if len(segs) > 1 else []))

            def group_fm(W, cc, M=128, N=TT, extra_rhs=None, X=None):
                X = xnT if X is None else X
                b = nbank()
                out = bankv[b][0:M, 0:N]
                for kc in range(NKC):
                    MM(out, W[:, kc, cc * 128:cc * 128 + M], X[kc][:, 0:N], kc == 0, kc == NKC - 1)
                if extra_rhs is not None:
                    b2 = nbank()
                    out2 = bankv[b2][0:M, 0:2]
                    for kc in range(NKC):
                        MM(out2, W[:, kc, cc * 128:cc * 128 + M], extra_rhs[:, kc, :], kc == 0, kc == NKC - 1)
                    return out, out2
                return out

            def group_tm(W, src, tc, ncol=512):
                b = nbank()
                out = bankv[b][:, 0:ncol]
                n = len(src)
                for kc in range(n):
                    MM(out, src[kc][:, tc * 128:(tc + 1) * 128], W[:, kc, 0:ncol], kc == 0, kc == n - 1)
                return out

            if True:
                P.op("sp", lambda e: e.dma_start(out=cst_t[:], in_=cst_d), writes=["cst"], dma_key="c_cst")
                P.op("sp", lambda e: e.dma_start(out=vecs_t[:], in_=vecs_d), writes=["vecs"], dma_key="c_vecs")
                P.op("sp", lambda e: e.dma_start(out=fnwb_t[:], in_=fnwb_d), writes=["fnwb"], dma_key="c_fnwb")
                P.op("pool", lambda e: e.dma_start(out=wgk_t[:], in_=wgk_d), writes=["wgk"], dma_key="c_wgk")
                P.op("pool", lambda e: e.dma_start(out=wgl_t[:],
                                                   in_=w_in[:, C_GKL:C_GKL + 16].rearrange("(kc p) c -> p kc c", p=128)),
                     writes=["wgl"], dma_key="c_wgl")
                DCOPY(identb, ident_f)
                P.op("dve", lambda e: e.memset(ones_t[:], 1.0), writes=["ones512"])
                for i in range(8):
                    P.op("dve", lambda e, i=i: e.memset(S_t[:, i, :], 0.0), writes=[("S", i)])
                TS(negb, vecs[:, 84:92], -1.0, ALU.mult)
                P.op("sp", lambda e: e.dma_start(out=am_t[:], in_=am_d), writes=["am"], dma_key="c_am")
                P.op("dve", lambda e: e.memset(LD_t[:], 0.0), writes=["LD"])
                P.op("dve", lambda e: e.memset(one8_t[:], 1.0), writes=["one8"])

            def load_x(row0):
                P.op("sp", lambda e: e.dma_start(out=xt.ap, in_=xs[row0:row0 + TT, :].rearrange("(tc p) d -> p tc d", p=128)),
                     writes=_k(xt), dma_key="x")

            def norm_to_T(wcol0, dst):
                for tc in range(4):
                    P.op("dve", lambda e, tc=tc: e.scalar_tensor_tensor(
                        out=junk.ap, in0=xt_tc[tc].ap, scalar=1.0, in1=xt_tc[tc].ap,
                        op0=ALU.mult, op1=ALU.mult, accum_out=ss_t[:, tc:tc + 1]),
                        reads=_k(xt_tc[tc]), writes=_k(junk, ss))
                ACT(rs, ss, AF.Ln, bias=EPS, scale=1.0 / D)
                ACT(rs, rs, AF.Exp, scale=-0.5)
                for tc in range(4):
                    TS(xn_tm[tc], xt_tc[tc], rs[:, tc:tc + 1], ALU.mult)
                for kc in range(NKC):
                    b = nbank()
                    for tc in range(4):
                        TR(bankb[b][:, tc * 128:(tc + 1) * 128], xn_tm[tc][:, kc * 128:(kc + 1) * 128])
                    TS(dst[kc], bankb[b][:, 0:TT], vcol(wcol0 + kc), ALU.mult)

            def s0_norm(src, j):
                P.op("dve", lambda e: e.scalar_tensor_tensor(
                    out=xnp[j].ap, in0=src.ap, scalar=1.0, in1=src.ap,
                    op0=ALU.mult, op1=ALU.mult, accum_out=ssp[j].ap),
                    reads=_k(src), writes=_k(xnp[j], ssp[j]))
                ACT(rsp[j], ssp[j], AF.Ln, bias=EPS, scale=1.0 / D)
                ACT(rsp[j], rsp[j], AF.Exp, scale=-0.5)
                TS(xnp[j], src, rsp[j], ALU.mult)

            def s0_load(row0, tc, j):
                r0 = row0 + tc * 128
                P.op("sp", lambda e: e.dma_start(out=xpc[j].ap, in_=xs[r0:r0 + 128, :]),
                     writes=_k(xpc[j]), dma_key=("xp", j))
                s0_norm(xpc[j], j)

            def s0_tr(tc, j, dst, wcol0):
                for kq in range(4):
                    b = nbank()
                    for i in range(4):
                        kc = kq * 4 + i
                        TR(bankb[b][:, i * 128:(i + 1) * 128], xnp[j][:, kc * 128:(kc + 1) * 128])
                    for i in range(4):
                        kc = kq * 4 + i
                        TS(dst[kc][:, tc * 128:(tc + 1) * 128], bankb[b][:, i * 128:(i + 1) * 128],
                           vcol(wcol0 + kc), ALU.mult)

            def final_stats():
                for tc in range(4):
                    P.op("dve", lambda e, tc=tc: e.scalar_tensor_tensor(
                        out=junk.ap, in0=xt_tc[tc].ap, scalar=1.0, in1=xt_tc[tc].ap,
                        op0=ALU.mult, op1=ALU.mult, accum_out=ss_t[:, tc:tc + 1]),
                        reads=_k(xt_tc[tc]), writes=_k(junk, ss))
                ACT(rs, ss, AF.Ln, bias=EPS, scale=1.0 / D)
                ACT(rs, rs, AF.Exp, scale=-0.5)

            def gk_low(X=None):
                W = V(wgl_t[:], ["wgl"])
                ps = group_fm(W, 0, M=16, X=X)
                ACOPY(gkl, ps)

            def decay_common(hd):
                b = nbank()
                ps = bankv[b]
                MM(ps, wgk[:, hd * 128:(hd + 1) * 128], gkl, True, True)
                ACT(spb, ps, AF.Exp, bias=negb[:, hd:hd + 1], scale=-1.0)
                ACT(spb, spb, AF.Ln, bias=1.0, scale=1.0)

            def scan(lo, hi):
                P.op("dve", lambda e: e.tensor_tensor_scan(out=E2.ap[:, lo:hi], data0=ones512.ap[:, lo:hi],
                                                           data1=spb.ap[:, lo:hi], initial=0.0,
                                                           op0=ALU.mult, op1=ALU.add),
                     reads=_k(spb, ones512), writes=_k(E2))

            def kt_and_v(h, pp, Wv, X=None):
                X = xnT if X is None else X
                for tc in range(4):
                    ps = group_tm(Wv, X, tc)
                    ACOPY(vb[pp][:, tc, :], ps)
                b = nbank()
                for tc in range(4):
                    for dkc in range(2):
                        TR(bankb[b][:, (tc * 2 + dkc) * 128:(tc * 2 + dkc + 1) * 128],
                           kteT[:, dkc, tc * 128:(tc + 1) * 128])
                P.op("act", lambda e: e.copy(out=kte[pp].ap, in_=bankb[b].ap.rearrange("p (a b) -> p a b", a=4)),
                     reads=_k(bankb[b]), writes=_k(kte[pp]))

            def prefix_tile(pt, last):
                X = xnT if pt % 2 == 0 else xnTalt
                Xn = xnT if (pt + 1) % 2 == 0 or last else xnTalt
                nrow0 = MROW0 if last else XROW0 + (pt + 1) * TT
                gk_low(X)
                for h in range(4):
                    s0_load(nrow0, h, h % 2)
                    if h >= 1:
                        s0_tr(h - 1, (h - 1) % 2, Xn, 0)
                    Wk = wblock("in", 0, NKC, C_K + h * 256, 256)
                    for dkc in range(2):
                        hd = h * 2 + dkc
                        ps = group_fm(Wk, dkc, X=X)
                        decay_common(hd)
                        scan(0, TT)
                        TS(nbl[:, 0:1], E2[:, TT - 1:TT], -1.0 / 16, ALU.mult)
                        TT_(LD[:, hd:hd + 1], LD[:, hd:hd + 1], nbl[:, 0:1], ALU.add)
                        ACT(E3, E2, AF.Exp, bias=nbl[:, 0:1], scale=1.0 / 16)
                        ACT(dl[hd][:, 0:1], nbl[:, 0:1], AF.Exp)
                        TT_(kteT[:, dkc, :], ps, E3, ALU.mult)
                    Wv = wblock("in", 0, NKC, C_V + h * 512, 512)
                    kt_and_v(h, 0, Wv, X)
                    for dkc in range(2):
                        hd = h * 2 + dkc
                        b = nbank()
                        for tc in range(4):
                            MM(bankv[b], kte[0][:, tc, dkc * 128:(dkc + 1) * 128], vb[0][:, tc, :], tc == 0, tc == 3)
                        STT(Sv[hd], Sv[hd], dl[hd][:, 0:1], bankv[b], ALU.mult, ALU.add)
                s0_tr(3, 1, Xn, 0)

            def qkv(h, pp):
                Wqk = wblock("in", 0, NKC, C_Q + h * 256, 256, C_K + h * 256, 256)
                for dkc in range(2):
                    hd = h * 2 + dkc
                    ps_q = group_fm(Wqk, dkc)
                    decay_common(hd)
                    for tc in range(4):
                        scan(tc * 128, (tc + 1) * 128)
                    ACT(E1, E2, AF.Exp, scale=-1.0 / 16)
                    TS(nbl, V(E2.ap.rearrange("p (a b) -> p a b", a=4)[:, :, 127], E2.keys), -1.0 / 16, ALU.mult)
                    for tc in range(4):
                        ACT(E3[:, tc * 128:(tc + 1) * 128], E2[:, tc * 128:(tc + 1) * 128], AF.Exp,
                            bias=nbl[:, tc:tc + 1], scale=1.0 / 16)
                    ACT(E2, E2, AF.Exp, scale=1.0 / 16)
                    ACT(dl[hd], nbl, AF.Exp)
                    STT(qd[pp][:, dkc, :], ps_q, 1.0 / 16, E1, ALU.mult, ALU.mult)
                    ps = group_fm(Wqk, 2 + dkc)
                    TT_(kd[pp][:, dkc, :], ps, E2, ALU.mult)
                    TT_(kteT[:, dkc, :], ps, E3, ALU.mult)
                Wv = wblock("in", 0, NKC, C_V + h * 512, 512)
                kt_and_v(h, pp, Wv)

            def gla_A(h, pp, tc):
                tsl = slice(tc * 128, (tc + 1) * 128)
                b = nbank()
                sc = bankv[b][:, 0:128]
                for dkc in range(2):
                    MM(sc, kd[pp][:, dkc, tsl], qd[pp][:, dkc, tsl], dkc == 0, dkc == 1)
                TT_(scT[tc % 2], sc, maskT, ALU.mult)

            def gla_B(h, pp, tc):
                tsl = slice(tc * 128, (tc + 1) * 128)
                s_ = scT[tc % 2]
                b = nbank()
                for dvc in range(4):
                    dsl = slice(dvc * 128, (dvc + 1) * 128)
                    o = bankv[b][:, dsl]
                    MM(o, vb[pp][:, tc, dsl], s_, True, False)
                    MM(o, Sbf[h * 2][:, dsl], qd[pp][:, 0, tsl], False, False)
                    MM(o, Sbf[h * 2 + 1][:, dsl], qd[pp][:, 1, tsl], False, True)
                P.op("act", lambda e, b=b, tsl=tsl: e.copy(out=o_sb.ap[:, :, tsl],
                                                           in_=bankv[b].ap.rearrange("p (a b) -> p a b", a=4)),
                     reads=_k(bankv[b]), writes=_k(o_sb))
                for dkc in range(2):
                    hd = h * 2 + dkc
                    b = nbank()
                    MM(bankv[b], kte[pp][:, tc, dkc * 128:(dkc + 1) * 128], vb[pp][:, tc, :], True, True)
                    STT(Sv[hd], Sv[hd], dl[hd][:, tc:tc + 1], bankv[b], ALU.mult, ALU.add)
                    ACOPY(Sbf[hd], Sv[hd])

            def gla_norm(h):
                b = nbank()
                for dvc in range(4):
                    ACT(sq[dvc % 2], o_cc[dvc], AF.Square)
                    MM(bankv[b], ones128, sq[dvc % 2], dvc == 0, dvc == 3)
                ACT(lnv, bankv[b], AF.Ln, bias=EPS, scale=1.0 / 512)
                ACT(lnv, lnv, AF.Exp, scale=-0.5)
                for dvc in range(4):
                    STT(o_cc[dvc], o_cc[dvc], vcol(80 + dvc), lnv, ALU.mult, ALU.mult)

            def gla(h, pp):
                for tc in range(4):
                    gla_A(h, pp, tc)
                    gla_B(h, pp, tc)

            def bmix(h, first_tile, hook=None):
                cnt = [0]

                def tick():
                    cnt[0] += 1
                    if hook is not None:
                        hook(cnt[0])
                Wgc = wblock("in", 0, NKC, C_GC + h * 512, 512)
                for cc in range(4):
                    if first_tile:
                        ps, psh = group_fm(Wgc, cc, extra_rhs=xnh)
                        ACOPY(gch[cc], psh)
                    else:
                        ps = group_fm(Wgc, cc)
                    ACOPY(gc[cc], ps)
                    tick()
                Wxc = wblock("in", 0, NKC, C_XC + h * 512, 512)
                for cc in range(4):
                    g = h * 4 + cc
                    u = ubuf[cc % 2]
                    if first_tile:
                        ps, psh = group_fm(Wxc, cc, extra_rhs=xnh)
                        TT_(uh[g], psh, gch[cc], ALU.mult)
                    else:
                        ps = group_fm(Wxc, cc)
                    DCOPY(u[:, 0:2], uh[g])
                    TT_(u[:, 2:514], ps, gc[cc], ALU.mult)
                    TS(cv[cc], u[:, 2:514], vcol(32 + g * 3 + 2), ALU.mult)
                    STT(cv[cc], u[:, 1:513], vcol(32 + g * 3 + 1), cv[cc], ALU.mult, ALU.add)
                    STT(cv[cc], u[:, 0:512], vcol(32 + g * 3 + 0), cv[cc], ALU.mult, ALU.add)
                    DCOPY(uh[g], u[:, 512:514])
                    tick()
                Wgb = wblock("in", 0, NKC, C_GB + h * 512, 512)
                for cc in range(4):
                    ps = group_fm(Wgb, cc)
                    TT_(cv[cc], ps, cv[cc], ALU.mult)
                    tick()

            def bmix_gate(h):
                Wgo = wblock("in", 0, NKC, C_GO + h * 512, 512)
                for cc in range(4):
                    ps = group_fm(Wgo, cc)
                    if cc == 0:
                        gla_norm(h)
                    ACT(sg, ps, AF.Sigmoid)
                    TT_(tmp, ps, sg, ALU.mult)
                    TT_(o_cc[cc], tmp, o_cc[cc], ALU.mult)
                Wma = wblock("in", 0, NKC, C_MA + h * 512, 512)
                for cc in range(4):
                    ps = group_fm(Wma, cc)
                    ACT(sg, ps, AF.Sigmoid)
                    TT_(o_cc[cc], o_cc[cc], sg, ALU.mult)
                Wmb = wblock("in", 0, NKC, C_MB + h * 512, 512)
                for cc in range(4):
                    g = h * 4 + cc
                    ps = group_fm(Wmb, cc)
                    ACT(sb2, ps, AF.Sigmoid)
                    TT_(cv[cc], cv[cc], sb2, ALU.mult)
                    TT_(mgT[g], cv[cc], o_cc[cc], ALU.add)

            def exchange_start():
                P.op("sp", lambda e: e.dma_start(out=lsrc[0].ap(), in_=S_t[:, 0:4, :].rearrange("p a b -> p (a b)")),
                     reads=[("S", i) for i in range(4)], writes=["lsrc0"], dma_key="ls0")
                P.op("sp", lambda e: e.dma_start(out=lsrc[1].ap(), in_=S_t[:, 4:8, :].rearrange("p a b -> p (a b)")),
                     reads=[("S", i) for i in range(4, 8)], writes=["lsrc1"], dma_key="ls1")
                P.op("sp", lambda e: e.dma_start(out=lsrc[2].ap(), in_=LD_t[:]), reads=["LD"], writes=["lsrc2"],
                     dma_key="ls2")
                prior = [o.id for o in P.by_eng["pool"] if o.dma][-3:]
                ccs = []
                for i in range(3):
                    ccs.append(P.op("pool", lambda e, i=i: e.collective_compute(
                        "AllGather", ALU.bypass, replica_groups=[[0, 1, 2, 3], [4, 5, 6, 7]],
                        ins=[lsrc[i].ap()], outs=[lall[i].ap()]),
                        reads=["lsrc%d" % i], writes=["lall%d" % i], dma_key="cc%d" % i, inc=1,
                        extra_deps=prior))

            def combine():
                for i in range(8):
                    P.op("dve", lambda e, i=i: e.memset(S_t[:, i, :], 0.0), writes=[("S", i)])
                P.op("sp", lambda e: e.dma_start(out=LDall_t[:], in_=lall[2].ap().rearrange("(r p) c -> p r c", p=128)),
                     reads=["lall2"], writes=["LDall"], dma_key="ld_all")
                ACT(LDall, LDall, AF.Exp)
                for j in range(3):
                    for half in range(2):
                        Lb = Lbuf[half]
                        P.op("sp", lambda e, j=j, Lb=Lb, half=half: e.dma_start(
                            out=Lb.ap, in_=lall[half].ap()[j * 128:(j + 1) * 128, :].rearrange("p (a b) -> p a b", a=4)),
                            reads=["lall%d" % half], writes=_k(Lb), dma_key=("lb", half))
                    TS(Deff, LDall[:, j, 0:8], -1.0, ALU.add)
                    STT(Deff, Deff, am[:, j:j + 1], one8, ALU.mult, ALU.add)
                    for hd in range(8):
                        TS(Sv[hd], Sv[hd], Deff[:, hd:hd + 1], ALU.mult)
                        STT(Sv[hd], Lbuf[hd // 4][:, hd % 4, :], am[:, j:j + 1], Sv[hd], ALU.mult, ALU.add)
                for i in range(8):
                    ACOPY(Sbf[i], Sv[i])

            def main_tile(mt):
                row0 = MROW0 + mt * TT
                gk_low()
                qkv(0, 0)
                for h in range(4):
                    if h + 1 < 4:
                        qkv(h + 1, (h + 1) % 2)
                    pp = h % 2
                    if USE_CC and mt == 0 and h == 0:
                        bmix(h, True)
                        combine()
                        gla(h, pp)
                    else:
                        gla_A(h, pp, 0)

                        def hook(n, h=h, pp=pp):
                            if n % 3 == 0:
                                tc = n // 3 - 1
                                gla_B(h, pp, tc)
                                if tc + 1 < 4:
                                    gla_A(h, pp, tc + 1)
                        bmix(h, mt == 0, hook)
                    if h == 3:
                        load_x(row0)
                    bmix_gate(h)
                for db in range(4):
                    Wo = wblock("out", 0, NKC, db * 512, 512)
                    for tc in range(4):
                        ps = group_tm(Wo, mgT, tc)
                        dsl = slice(db * 512, (db + 1) * 512)
                        TT_(xt_tc[tc][:, dsl], ps, xt_tc[tc][:, dsl], ALU.add)
                        if db == 3:
                            s0_norm(xt_tc[tc], tc % 2)
                            if tc >= 1:
                                s0_tr(tc - 1, (tc - 1) % 2, xnT, 16)
                Wg0 = wblock("gu", 0, NKC, 0, 512)
                pre = []
                for cc in range(2):
                    b = nbank()
                    for kc in range(NKC):
                        MM(bankv[b][:, 0:384], Wg0[:, kc, cc * 128:(cc + 1) * 128], xnT[kc][:, 0:384], kc == 0, kc == NKC - 1)
                    pre.append(b)
                s0_tr(3, 1, xnT, 16)
                for cc in range(2):
                    for kc in range(NKC):
                        MM(bankv[pre[cc]][:, 384:512], Wg0[:, kc, cc * 128:(cc + 1) * 128], xnT[kc][:, 384:512],
                           kc == 0, kc == NKC - 1)
                for fb in range(11):
                    Wg = Wg0 if fb == 0 else wblock("gu", 0, NKC, fb * 512, 512)
                    for cc in range(4):
                        ps = bankv[pre[cc]] if (fb == 0 and cc < 2) else group_fm(Wg, cc)
                        ACT(ft[4 + cc % 2], ps, AF.Sigmoid)
                        TT_(ft[cc], ps, ft[4 + cc % 2], ALU.mult)
                    Wu = wblock("gu", 0, NKC, FF + fb * 512, 512)
                    for cc in range(4):
                        ps = group_fm(Wu, cc)
                        TT_(actT[fb * 4 + cc], ps, ft[cc], ALU.mult)
                for db in range(4):
                    bs = [nbank() for _ in range(4)]
                    subs = [(0, 16), (16, 16), (32, 12)]
                    for (k0, nk) in subs:
                        Wd = wblock("dn", k0, nk, db * 512, 512)
                        for tc in range(4):
                            for j in range(nk):
                                fc = k0 + j
                                MM(bankv[bs[tc]], actT[fc][:, tc * 128:(tc + 1) * 128], Wd[:, j, :],
                                   fc == 0, fc == NFC - 1)
                    for tc in range(4):
                        dsl = slice(db * 512, (db + 1) * 512)
                        TT_(xt_tc[tc][:, dsl], bankv[bs[tc]], xt_tc[tc][:, dsl], ALU.add)
                        if db == 3 and mt == NMAIN - 1:
                            j = tc % 2
                            P.op("dve", lambda e, tc=tc, j=j: e.scalar_tensor_tensor(
                                out=xnp[j].ap, in0=xt_tc[tc].ap, scalar=1.0, in1=xt_tc[tc].ap,
                                op0=ALU.mult, op1=ALU.mult, accum_out=ssp[j].ap),
                                reads=_k(xt_tc[tc]), writes=_k(xnp[j], ssp[j]))
                            ACT(rsp[j], ssp[j], AF.Ln, bias=EPS, scale=1.0 / D)
                            ACT(rsp[j], rsp[j], AF.Exp, scale=-0.5)
                            STT(xt_tc[tc], xt_tc[tc], rsp[j], fnwb, ALU.mult, ALU.mult)
                            r0 = mt * TT + tc * 128
                            P.op("sp", lambda e, tc=tc, r0=r0: e.dma_start(out=out_d[r0:r0 + 128, :], in_=xt_tc[tc].ap),
                                 reads=_k(xt_tc[tc]), dma_key=("o", mt))
                    if mt + 1 < NMAIN:
                        s0_load(row0 + TT, db, db % 2)
                        if db >= 1:
                            s0_tr(db - 1, (db - 1) % 2, xnT, 0)
                if mt + 1 < NMAIN:
                    s0_tr(3, 1, xnT, 0)
                if mt == NMAIN - 1:
                    return
                final_stats()
                for tc in range(4):
                    STT(xt_tc[tc], xt_tc[tc], rs[:, tc:tc + 1], fnwb, ALU.mult, ALU.mult)
                P.op("sp", lambda e: e.dma_start(out=out_d[mt * TT:(mt + 1) * TT, :].rearrange("(tc p) d -> p tc d", p=128),
                                                 in_=xt.ap),
                     reads=_k(xt), dma_key=("o", mt))

            for tc in range(4):
                s0_load(XROW0, tc, (tc + 1) % 2)
                s0_tr(tc, (tc + 1) % 2, xnT, 0)
            for pt in range(NPRE):
                prefix_tile(pt, pt == NPRE - 1)
            if USE_CC:
                exchange_start()
            else:
                for i in range(8):
                    ACOPY(Sbf[i], Sv[i])
            s0_load(0, 0, 0)
            s0_tr(0, 0, xnTalt, 0)
            DCOPY(xnh, xnTalt_halo0)
            for mt in range(NMAIN):
                main_tile(mt)
            P.op("sp", lambda e: None, extra_deps=[o.id for o in P.ops if o.dma and o.eng == "sp"])
            return rec

        plan = emit_all(Prog(nc), None)
        P = Prog(nc)
        emit_all(P, plan)
        P.emit(st)
        print("ops", len(P.ops), "sems", P.n_sems, "max_tick", P.max_tick, "wblocks", len(plan))
    return nc


def _consts():
    c = np.zeros((128, 384), np.float32)
    c[:, 0:128] = np.eye(128, dtype=np.float32)
    j = np.arange(128)[:, None]
    i = np.arange(128)[None, :]
    c[:, 128:256] = (j <= i).astype(np.float32)
    c[:, 256:384] = 1.0
    return c


def kernel(x, mix_norm_w, w_in, w_gk_up, b_gk_up, gla_norm_w, conv_w, w_out,
           ffn_norm_w, w_gate_up, w_down, final_norm_w):
    x = np.asarray(x, np.float32)
    B, S, _ = x.shape
    vecs = np.zeros((128, 92), np.float32)
    vecs[:, 0:16] = np.asarray(mix_norm_w, np.float32)[0].reshape(16, 128).T
    vecs[:, 16:32] = np.asarray(ffn_norm_w, np.float32)[0].reshape(16, 128).T
    cw = np.asarray(conv_w, np.float32)[0]
    vecs[:, 32:80] = cw.reshape(3, 16, 128).transpose(2, 1, 0).reshape(128, 48)
    vecs[:, 80:84] = np.asarray(gla_norm_w, np.float32)[0].reshape(4, 128).T
    vecs[:, 84:92] = np.asarray(b_gk_up, np.float32)[0].reshape(8, 128).T
    fnwb = np.ascontiguousarray(np.broadcast_to(np.asarray(final_norm_w, np.float32)[None, :], (128, D)))
    shared = {
        "w_in": np.ascontiguousarray(np.asarray(w_in, np.float32)[0]),
        "w_out": np.ascontiguousarray(np.asarray(w_out, np.float32)[0]),
        "w_gu": np.ascontiguousarray(np.asarray(w_gate_up, np.float32)[0]),
        "w_dn": np.ascontiguousarray(np.asarray(w_down, np.float32)[0]),
        "wgk": np.ascontiguousarray(np.asarray(w_gk_up, np.float32)[0]),
        "vecs": vecs, "fnwb": fnwb, "cst": _consts(),
    }
    in_maps = []
    own = NMAIN * TT
    for c in range(8):
        b, p = c // 4, c % 4
        xs = np.zeros((MROW0 + own, D), np.float32)
        start = p * own
        if start > 0:
            xs[0:XROW0] = x[b, start - XROW0:start]
            if not USE_CC:
                xs[MROW0 - start:MROW0] = x[b, 0:start]
        xs[MROW0:] = x[b, start:start + own]
        am = np.zeros((128, 4), np.float32)
        am[:, 0:p] = 1.0
        m = dict(shared)
        m["xs"] = xs
        m["am"] = am
        in_maps.append(m)
    nc = build_nc()
    res = run_bass_kernel_spmd(nc, in_maps, core_ids=list(range(8)))
    out = np.zeros((B, S, D), np.float32)
    for c in range(8):
        b, p = c // 4, c % 4
        out[b, p * own:(p + 1) * own] = res.results[c]["out"]
    return out
```
